# Optimizing a Trainium2 kernel written in Bass

```python
import math
import jax, jax.numpy as jnp
from jax import lax
import numpy as np

D_MODEL = 1024
BATCH = 2
SEQ = 8192
DEPTH = 1

GRID_W = 64
CTX_LEN = 256
ATTN_WIDTH = 512
HGRN_WIDTH = 512
D_MIX = ATTN_WIDTH + HGRN_WIDTH
ATTN_HEADS = 4
DIFF_HEAD_DIM = 64
HGRN_HEADS = 4
HGRN_KEY_DIM = HGRN_WIDTH // HGRN_HEADS
HGRN_VAL_DIM = HGRN_WIDTH // HGRN_HEADS
CHUNK = 64
Q_BLOCK = 128
ROPE_BASE = 10000.0
ROPE_FREQS = DIFF_HEAD_DIM // 4
EPS = 1e-5
IN_WIDTHS = (ATTN_WIDTH, ATTN_WIDTH, ATTN_WIDTH, ATTN_WIDTH,
             HGRN_WIDTH, HGRN_WIDTH, HGRN_WIDTH, HGRN_WIDTH, HGRN_WIDTH)
IN_COLS = sum(IN_WIDTHS)

kernel_name = "hymba_diffattn_hgrn2_prefix_ctx_block"


def _layer_norm(x, g, b):
    xf = x.astype(jnp.float32)
    mu = jnp.mean(xf, axis=-1, keepdims=True)
    var = jnp.mean(jnp.square(xf - mu), axis=-1, keepdims=True)
    return ((xf - mu) * lax.rsqrt(var + EPS) * g.astype(jnp.float32) + b.astype(jnp.float32)).astype(x.dtype)


def _rms_norm(x, g):
    xf = x.astype(jnp.float32)
    return xf * lax.rsqrt(jnp.mean(xf * xf, axis=-1, keepdims=True) + EPS) * g.astype(jnp.float32)


def _axial_rope_tables(n_tokens):
    rows = n_tokens // GRID_W
    r, cidx = jnp.meshgrid(jnp.arange(rows, dtype=jnp.float32),
                           jnp.arange(GRID_W, dtype=jnp.float32), indexing="ij")
    pos = jnp.stack([r.reshape(-1), cidx.reshape(-1)], axis=-1)
    inv_freq = ROPE_BASE ** (-jnp.arange(ROPE_FREQS, dtype=jnp.float32) / ROPE_FREQS)
    ang = pos[:, :, None] * inv_freq
    ang = jnp.stack([ang, ang], axis=2).reshape(n_tokens, DIFF_HEAD_DIM)
    return jnp.cos(ang), jnp.sin(ang)


def _apply_rope(x, cos, sin):
    xs = x.reshape(x.shape[:-1] + (2, 2, ROPE_FREQS))
    rot = jnp.concatenate([-xs[..., 1:, :], xs[..., :1, :]], axis=-2).reshape(x.shape)
    c = cos[None, :, None, None, :].astype(x.dtype)
    s = sin[None, :, None, None, :].astype(x.dtype)
    return x * c + rot * s


def _attn_heads(aq, ak, av, rope=None):
    B, T, _ = aq.shape
    q = aq.reshape(B, T, ATTN_HEADS, 2, DIFF_HEAD_DIM)
    k = ak.reshape(B, T, ATTN_HEADS, 2, DIFF_HEAD_DIM)
    if rope is not None:
        q = _apply_rope(q, *rope)
        k = _apply_rope(k, *rope)
    v = av.reshape(B, T, ATTN_HEADS, 2 * DIFF_HEAD_DIM)
    return q.transpose(0, 2, 3, 1, 4), k.transpose(0, 2, 3, 1, 4), v.transpose(0, 2, 1, 3)


def _diff_attn(q, k, v, lam):
    s = jnp.einsum("bhmqd,bhmkd->bhmqk", q, k).astype(jnp.float32) * (1.0 / math.sqrt(DIFF_HEAD_DIM))
    p = jax.nn.softmax(s, axis=-1)
    w = p[:, :, 0] - lam * p[:, :, 1]
    return jnp.einsum("bhqk,bhkv->bhqv", w.astype(v.dtype), v)


def _attn_readout(o, gain, lambda_init, dtype):
    o = _rms_norm(o, gain) * (1.0 - lambda_init)
    B, H, T, dv = o.shape
    return o.transpose(0, 2, 1, 3).reshape(B, T, H * dv).astype(dtype)


def _to_heads(a, dh):
    B, T, _ = a.shape
    return a.reshape(B, T, -1, dh).transpose(0, 2, 1, 3)


def _hgrn_gates(z, lb):
    zf = z.astype(jnp.float32)
    log_f = jnp.log(lb + (1.0 - lb) * jax.nn.sigmoid(zf))
    k = (1.0 - lb) * jax.nn.sigmoid(-zf)
    return _to_heads(k, HGRN_KEY_DIM), _to_heads(log_f, HGRN_KEY_DIM)


def _chunk_scan(q, k, v, log_f, s0):
    B, H, T, DK = q.shape
    n_chunks = T // CHUNK

    def to_chunks(a):
        return a.reshape(B, H, n_chunks, CHUNK, a.shape[-1]).transpose(2, 0, 1, 3, 4)

    incl = jnp.tril(jnp.ones((CHUNK, CHUNK), dtype=bool))

    def step(s, inp):
        qc, kc, vc, gc = inp
        b = jnp.cumsum(gc, axis=2)
        rel = b[:, :, :, None, :] - b[:, :, None, :, :]
        decay = jnp.exp(jnp.where(incl[:, :, None], rel, -jnp.inf))
        scores = jnp.einsum("bhtk,bhsk,bhtsk->bhts", qc, kc, decay)
        o = (jnp.einsum("bhts,bhsv->bhtv", scores, vc)
             + jnp.einsum("bhtk,bhkv->bhtv", qc * jnp.exp(b), s))
        b_last = b[:, :, -1:, :]
        s_new = (jnp.exp(b_last[:, :, 0, :])[..., None] * s
                 + jnp.einsum("bhsk,bhsv->bhkv", kc * jnp.exp(b_last - b), vc))
        return s_new, o

    s_final, o = lax.scan(step, s0, (to_chunks(q), to_chunks(k), to_chunks(v), to_chunks(log_f)))
    o = o.transpose(1, 2, 0, 3, 4).reshape(B, H, T, v.shape[-1])
    return o, s_final


def _hgrn2_bidir(hq, hi, hf_fwd, hf_bwd, lb_fwd, lb_bwd, s0_fwd, s0_bwd):
    q = _to_heads(jax.nn.silu(hq.astype(jnp.float32)), HGRN_KEY_DIM)
    v = _to_heads(hi.astype(jnp.float32), HGRN_VAL_DIM)
    k_f, g_f = _hgrn_gates(hf_fwd, lb_fwd)
    k_b, g_b = _hgrn_gates(hf_bwd, lb_bwd)
    o_f, s_f = _chunk_scan(q, k_f, v, g_f, s0_fwd)
    flip = lambda a: jnp.flip(a, axis=2)
    o_b, s_b = _chunk_scan(flip(q), flip(k_b), flip(v), flip(g_b), s0_bwd)
    return o_f + flip(o_b), s_f, s_b


def _hgrn_readout(o, gain, dtype):
    o = _rms_norm(o, gain)
    B, H, T, dv = o.shape
    return o.transpose(0, 2, 1, 3).reshape(B, T, H * dv).astype(dtype)


def _project(h, w):
    p = h @ w
    idx = list(np.cumsum(IN_WIDTHS)[:-1])
    return jnp.split(p, idx, axis=-1)


def _merge(attn_o, ag, hgrn_o, hg, w_out):
    y = jnp.concatenate([attn_o * jax.nn.silu(ag), hgrn_o * jax.nn.silu(hg)], axis=-1)
    return y @ w_out


def setup_inputs(seed: int = 0) -> dict:
    key = jax.random.key(seed)
    ks = jax.random.split(key, 14)
    beta = (8.0 * DEPTH) ** -0.25
    f32 = jnp.float32
    return {
        "x": jax.random.normal(ks[0], (BATCH, SEQ, D_MODEL), f32),
        "c": jax.random.normal(ks[1], (BATCH, D_MODEL), f32),
        "ctx": jax.random.normal(ks[2], (BATCH, CTX_LEN, D_MODEL), f32),
        "c_ctx": jax.random.normal(ks[3], (D_MODEL,), f32),
        "w_ada": jax.random.normal(ks[4], (DEPTH, D_MODEL, 3 * D_MODEL), f32) * (0.5 * D_MODEL ** -0.5),
        "b_ada": jax.random.normal(ks[5], (DEPTH, 3 * D_MODEL), f32) * 0.02,
        "w_in": jax.random.normal(ks[6], (DEPTH, D_MODEL, IN_COLS), f32) * D_MODEL ** -0.5,
        "w_out": jax.random.normal(ks[7], (DEPTH, D_MIX, D_MODEL), f32) * (D_MIX ** -0.5 * beta),
        "diff_lambda": jax.random.normal(ks[8], (DEPTH, 4, DIFF_HEAD_DIM), f32) * 0.1,
        "diff_subln_gain": 1.0 + 0.02 * jax.random.normal(ks[9], (DEPTH, 2 * DIFF_HEAD_DIM), f32),
        "hgrn_lower_bound": jax.random.normal(ks[10], (2, DEPTH + 1, HGRN_WIDTH), f32) * 0.1,
        "hgrn_norm_gain": 1.0 + 0.02 * jax.random.normal(ks[11], (DEPTH, HGRN_VAL_DIM), f32),
        "ln_gain": 1.0 + 0.02 * jax.random.normal(ks[12], (DEPTH, D_MODEL), f32),
        "ln_bias": 0.02 * jax.random.normal(ks[13], (DEPTH, D_MODEL), f32),
    }


def reference(x, c, ctx, c_ctx, w_ada, b_ada, w_in, w_out, diff_lambda, diff_subln_gain,
              hgrn_lower_bound, hgrn_norm_gain, ln_gain, ln_bias):
    B, N, D = x.shape
    alpha = (2.0 * DEPTH) ** 0.25
    rope = _axial_rope_tables(N)
    lower_bounds = jnp.cumsum(jax.nn.softmax(hgrn_lower_bound.astype(jnp.float32), axis=1), axis=1)
    n_blk = N // Q_BLOCK

    for layer in range(DEPTH):
        mod = jax.nn.silu(c) @ w_ada[layer] + b_ada[layer]
        shift, scale, gate = jnp.split(mod, 3, axis=-1)
        mod_c = jax.nn.silu(c_ctx) @ w_ada[layer] + b_ada[layer]
        shift_c, scale_c, gate_c = jnp.split(mod_c, 3, axis=-1)
        h_lat = x * (1.0 + scale[:, None, :]) + shift[:, None, :]
        h_ctx = ctx * (1.0 + scale_c) + shift_c

        aq_l, ak_l, av_l, ag_l, hq_l, hi_l, hff_l, hfb_l, hg_l = _project(h_lat, w_in[layer])
        aq_c, ak_c, av_c, ag_c, hq_c, hi_c, hff_c, hfb_c, hg_c = _project(h_ctx, w_in[layer])

        lambda_init = 0.8 - 0.6 * math.exp(-0.3 * layer)
        lp = diff_lambda[layer].astype(jnp.float32)
        lam = jnp.exp(jnp.sum(lp[0] * lp[1])) - jnp.exp(jnp.sum(lp[2] * lp[3])) + lambda_init
        q_c, k_c, v_c = _attn_heads(aq_c, ak_c, av_c)
        q_l, k_l, v_l = _attn_heads(aq_l, ak_l, av_l, rope)
        k_all = jnp.concatenate([k_c, k_l], axis=3)
        v_all = jnp.concatenate([v_c, v_l], axis=2)
        q_blocks = jnp.moveaxis(q_l.reshape(B, ATTN_HEADS, 2, n_blk, Q_BLOCK, DIFF_HEAD_DIM), 3, 0)
        o_blocks = lax.map(lambda qb: _diff_attn(qb, k_all, v_all, lam), q_blocks)
        o_att_l = jnp.moveaxis(o_blocks, 0, 2).reshape(B, ATTN_HEADS, N, 2 * DIFF_HEAD_DIM)
        attn_lat = _attn_readout(o_att_l, diff_subln_gain[layer], lambda_init, x.dtype)

        lb_f, lb_b = lower_bounds[0, layer], lower_bounds[1, layer]
        s_zero = jnp.zeros((B, HGRN_HEADS, HGRN_KEY_DIM, HGRN_VAL_DIM), jnp.float32)
        o_hg_c, s_ctx_f, s_ctx_b = _hgrn2_bidir(hq_c, hi_c, hff_c, hfb_c, lb_f, lb_b, s_zero, s_zero)
        o_hg_l, _, _ = _hgrn2_bidir(hq_l, hi_l, hff_l, hfb_l, lb_f, lb_b, s_ctx_f, s_ctx_b)
        hgrn_lat = _hgrn_readout(o_hg_l, hgrn_norm_gain[layer], x.dtype)

        y_lat = _merge(attn_lat, ag_l, hgrn_lat, hg_l, w_out[layer])

        if layer < DEPTH - 1:
            attn_ctx = _attn_readout(_diff_attn(q_c, k_c, v_c, lam), diff_subln_gain[layer],
                                     lambda_init, ctx.dtype)
            hgrn_ctx = _hgrn_readout(o_hg_c, hgrn_norm_gain[layer], ctx.dtype)
            y_ctx = _merge(attn_ctx, ag_c, hgrn_ctx, hg_c, w_out[layer])
            ctx = _layer_norm(alpha * ctx + gate_c * y_ctx, ln_gain[layer], ln_bias[layer])

        x = _layer_norm(alpha * x + gate[:, None, :] * y_lat, ln_gain[layer], ln_bias[layer])
    return x
```

```python
import math
import numpy as np
import concourse.bass as bass
import concourse.mybir as mybir
from concourse.bass_utils import run_bass_kernel_spmd

F32 = mybir.dt.float32
BF16 = mybir.dt.bfloat16
U8 = mybir.dt.uint8
AF = mybir.ActivationFunctionType
ALU = mybir.AluOpType
AX = mybir.AxisListType

D = 1024
CTX = 256
NCH = 8
EPS = 1e-5
LAMBDA_INIT = 0.8 - 0.6 * math.exp(0.0)
ALPHA = 2.0 ** 0.25
PREF_C = False
DEEP3 = True
PREF_D = False
WCOL = dict(aq=0, ak=512, av=1024, ag=1536, hq=2048, hi=2560, hff=3072, hfb=3584, hg=4096)


class _Op:
    __slots__ = ("eng", "fn", "idx", "inc", "semval", "dma", "dma_val", "waits")


class Prog:
    ENGS = ("pe", "act", "dve", "pool", "sp")

    def __init__(self):
        self.q = {e: [] for e in self.ENGS}
        self.lw = {}
        self.rd = {}
        self.waited = {e: {} for e in self.ENGS}
        self.dma_cnt = {}
        self.cap = None

    def capture(self, f):
        self.cap = []
        f()
        lst, self.cap = self.cap, None
        return lst

    def replay(self, item):
        if item is not None:
            self.op(item[0], item[1], r=item[2], w=item[3], dma=item[4])

    def mark(self):
        if self.cap is not None:
            self.cap.append(None)

    def op(self, eng, fn, r=(), w=(), dma=None, extra=()):
        if self.cap is not None:
            self.cap.append((eng, fn, tuple(r), tuple(w), dma))
            return None
        o = _Op()
        o.eng, o.fn, o.idx, o.inc, o.dma, o.semval, o.dma_val = eng, fn, len(self.q[eng]), False, dma, 0, 0
        deps = list(extra)
        for k in r:
            p = self.lw.get(k)
            if p is not None:
                deps.append(p)
        for k in w:
            p = self.lw.get(k)
            if p is not None:
                deps.append(p)
            deps.extend(self.rd.get(k, ()))
        best = {}
        wd = self.waited[eng]
        for p in deps:
            if p.dma is not None:
                key = ("dma", p.dma)
                if wd.get(key, 0) < p.dma_val and best.get(key, 0) < p.dma_val:
                    best[key] = p.dma_val
            else:
                if p.eng == eng and eng in ("pe", "sp"):
                    continue
                key = p.eng
                if wd.get(key, -1) < p.idx and (key not in best or best[key].idx < p.idx):
                    best[key] = p
        waits = []
        for key, v in best.items():
            if isinstance(key, tuple):
                wd[key] = v
                waits.append(("dma", key[1], v))
            else:
                wd[key] = v.idx
                v.inc = True
                waits.append(("eng", v))
        o.waits = waits
        if dma is not None:
            self.dma_cnt[dma] = self.dma_cnt.get(dma, 0) + 16
            o.dma_val = self.dma_cnt[dma]
        self.q[eng].append(o)
        for k in w:
            self.lw[k] = o
            self.rd[k] = []
        for k in r:
            self.rd.setdefault(k, []).append(o)
        return o

    def barrier(self):
        lasts = []
        for e in self.ENGS:
            if self.q[e]:
                lasts.append(self.q[e][-1])
        dmas = {}
        for e in self.ENGS:
            for o in self.q[e]:
                if o.dma is not None:
                    dmas[o.dma] = o
        for e in self.ENGS:
            self.op(e, lambda g: g.nop(), extra=lasts + list(dmas.values()))
        self.lw.clear()
        self.rd.clear()

    def emit(self, nc, block, engs, sems, dma_sems):
        for e in self.ENGS:
            c = 0
            for o in self.q[e]:
                if o.dma is None and o.inc:
                    c += 1
                    o.semval = c

        def run(e):
            def body(g):
                for o in self.q[e]:
                    for w in o.waits:
                        if w[0] == "dma":
                            g.wait_ge(dma_sems[w[1]], w[2])
                        else:
                            g.wait_ge(sems[w[1].eng], w[1].semval)
                    ins = o.fn(g)
                    if o.dma is not None:
                        ins.then_inc(dma_sems[o.dma], 16)
                    elif o.inc:
                        ins.then_inc(sems[e], 1)
            return body

        block.tensor(run("pe"))
        block.scalar(run("act"))
        block.vector(run("dve"))
        block.gpsimd(run("pool"))
        block.sync(run("sp"))


def build_program(SEQ):
    NQ = SEQ // 4
    NBO = NQ // 128
    NSTO = NQ // 512
    TALL = 2 * CTX + 4 * NQ
    NK = CTX + 4 * NQ
    NKT = NK // 128
    sts = []
    sts.append(dict(row=0, n=CTX, seg=0, kv=True, lat=False, key0=0))
    sts.append(dict(row=CTX, n=CTX, seg=1, kv=False, lat=False, key0=None))
    row, key, li = 2 * CTX, CTX, 0
    for seg in (2, 3, 4, 5):
        for _ in range(NSTO):
            sts.append(dict(row=row, n=512, seg=seg, kv=True, lat=True, key0=key, li=li))
            row += 512
            key += 512
            li += 1
    NST = len(sts)
    NLST = li

    nc = bass.Bass("TRN2", target_bir_lowering=False)

    def din(name, shape, dt=F32):
        return nc.dram_tensor(name, list(shape), dt, kind="ExternalInput").ap()

    xs = din("xs", [TALL, D])
    cvec = din("cvec", [2, D])
    w_ada = din("w_ada", [D, 3 * D])
    b_ada = din("b_ada", [3 * D])
    w_in = din("w_in", [D, 4608])
    wz = din("wz", [5, D, 512])
    lbseg = din("lbseg", [5, 2, 512])
    lbown = din("lbown", [2, 2, 512])
    rope = din("rope", [NLST, 128, 2, 512])
    sel = din("sel", [128, 16])
    cmat = din("cmat", [128, 6, 128])
    w_out = din("w_out", [D, D])
    dlam = din("dlam", [256])
    subg = din("subg", [128])
    hng = din("hng", [128])
    lng = din("lng", [D])
    lnb = din("lnb", [D])
    y = nc.dram_tensor("y", [NQ, D], F32, kind="ExternalOutput").ap()

    def dscr(name, shape, dt):
        return nc.dram_tensor(name, list(shape), dt).ap()

    hT_d = dscr("hT_d", [NST, 128, NCH * 512], BF16)
    kT_d = dscr("kT_d", [4, 128, NK], BF16)
    v_d = dscr("v_d", [4, 128, NKT, 130], BF16)
    q_d = dscr("q_d", [4, 2, 128, NQ], BF16)
    ga_d = dscr("ga_d", [NBO, 128, 512], BF16)
    gh_d = dscr("gh_d", [NBO, 128, 512], BF16)
    qbT_d = dscr("qbT_d", [NBO, 128, 512], BF16)
    incb_d = dscr("incb_d", [NBO, 128, 512], F32)
    op_d = dscr("op_d", [NBO, 128, 512], F32)

    pr = Prog()
    ARENA = 207 * 1024

    import contextlib
    stack = contextlib.ExitStack()
    with stack:
        arena = stack.enter_context(nc.sbuf_tensor("arena", [128, ARENA], U8))
        psum = stack.enter_context(nc.psum_tensor("psum", [128, 8, 512], F32))
        sems = {e: stack.enter_context(nc.semaphore("s_" + e)) for e in Prog.ENGS}

        class Carver:
            def __init__(self, base=0):
                self.off = base

            def take(self, shape, dt):
                nb = int(np.prod(shape)) * (4 if dt == F32 else 2)
                nb = (nb + 63) // 64 * 64
                a = arena[:, self.off:self.off + nb // 1].bitcast(dt)
                n = int(np.prod(shape))
                a = a[:, 0:n]
                self.off += nb
                assert self.off <= ARENA, f"arena overflow {self.off}"
                if len(shape) == 2:
                    a = a.rearrange("p (a b) -> p a b", a=shape[0])
                elif len(shape) == 3:
                    a = a.rearrange("p (a b c) -> p a b c", a=shape[0], b=shape[1])
                return a

        def bank(i):
            return psum[:, i, :]

        def bank_bf(i):
            return psum[:, i, :].bitcast(BF16)

        cv = Carver(0)
        cm = cv.take([6 * 128], F32)
        cm = cm
        identb = cv.take([128], BF16)
        rpermb = cv.take([128], BF16)
        maskf = cv.take([512], BF16)
        maskb = cv.take([512], BF16)
        ones_f = cv.take([4], F32)
        selt = cv.take([16], F32)
        sc = cv.take([2, 8], F32)
        sh_t = cv.take([2, 8], F32)
        sc1_t = cv.take([2, 8], F32)
        gate_t = cv.take([D], F32)
        gainA = cv.take([512], F32)
        gainH = cv.take([512], F32)
        lam_t = cv.take([4], F32)
        lbt_own = cv.take([2, 512], F32)
        oml_own = cv.take([2, 512], F32)
        S_cF = cv.take([512], F32)
        S_cB = cv.take([512], F32)
        S_cur = cv.take([512], F32)
        SF_fin = cv.take([512], F32)
        SB_fin = cv.take([512], F32)
        Db_all = cv.take([NBO, 4], F32)
        Y = cv.take([NBO, D], BF16)
        PBASE = cv.off

        identf = cm[:, 0:128]
        Lf = cm[:, 256:384]
        Lb = cm[:, 384:512]

        def dma(out, in_, key, r=(), w=(), slow=False):
            if slow:
                return pr.op("sp", lambda g: g.dma_start(out=out, in_=in_, allow_slow_non_contiguous=True),
                             r=r, w=w, dma=key)
            return pr.op("sp", lambda g: g.dma_start(out=out, in_=in_), r=r, w=w, dma=key)

        def act(out, in_, func, r, w, scale=1.0, bias=0.0):
            return pr.op("act", lambda g: g.activation(out=out, in_=in_, func=func, bias=bias, scale=scale), r=r, w=w)

        def mm(out, lhsT, rhs, start, stop, r, w):
            return pr.op("pe", lambda g: g.matmul(out, lhsT, rhs, start=start, stop=stop), r=r, w=w)

        def tr(out, in_, ident, r, w):
            return pr.op("pe", lambda g: g.transpose(out, in_, ident), r=r, w=w)

        def ts(eng, out, in0, s1, s2, op0, op1, r, w):
            if s2 is None:
                return pr.op(eng, lambda g: g.tensor_single_scalar(out, in0, s1, op0), r=r, w=w)
            return pr.op(eng, lambda g: g.tensor_scalar(out, in0, s1, s2, op0, op1), r=r, w=w)

        def tt(eng, out, in0, in1, op, r, w):
            return pr.op(eng, lambda g: g.tensor_tensor(out, in0, in1, op), r=r, w=w)

        def stt(out, in0, scalar, in1, op0, op1, r, w):
            return pr.op("dve", lambda g: g.scalar_tensor_tensor(out, in0, scalar, in1, op0, op1), r=r, w=w)

        def cp(eng, out, in_, r, w):
            if eng == "act":
                return pr.op("act", lambda g: g.copy(out, in_), r=r, w=w)
            return pr.op(eng, lambda g: g.tensor_copy(out, in_), r=r, w=w)

        def recip1p(buf, key):
            act(buf, buf, AF.Ln, r=[key], w=[key], bias=1.0)
            act(buf, buf, AF.Exp, r=[key], w=[key], scale=-1.0)

        def rsqrt_small(buf, key):
            act(buf, buf, AF.Ln, r=[key], w=[key])
            act(buf, buf, AF.Exp, r=[key], w=[key], scale=-0.5)

        def memset(eng, ap, val, w):
            return pr.op(eng, lambda g: g.memset(ap, val), w=w)

        c0 = Carver(PBASE)
        wst = c0.take([8, 512], F32)
        wst2 = c0.take([8, 512], F32)
        cvt = c0.take([2, 8], F32)
        bada_t = c0.take([24], F32)
        tmpa = c0.take([2, 8], F32)
        scb = c0.take([8, 128], F32)
        modT = c0.take([16, 2], F32)
        bgate = c0.take([D], F32)
        lamraw = c0.take([256], F32)
        lamtmp = c0.take([256], F32)
        lbraw = c0.take([2, 2, 512], F32)
        g128 = c0.take([2, 128], F32)

        dma(cm, cmat.rearrange("p a b -> p (a b)"), "c_cm", w=["cm"])
        dma(selt, sel, "c_sel", w=["selt"])
        dma(cvt, cvec.rearrange("w (c p) -> p w c", p=128), "c_cv", w=["cvt"], slow=True)
        dma(bada_t, b_ada.rearrange("(c p) -> p c", p=128), "c_ba", w=["bada"], slow=True)
        dma(bgate, b_ada[2 * D:3 * D].partition_broadcast(128), "c_bg", w=["bgate"])
        dma(lamraw, dlam.partition_broadcast(128), "c_lam", w=["lamraw"])
        dma(lbraw, lbown.rearrange("a b c -> (a b c)").partition_broadcast(128).rearrange("p (a b c) -> p a b c", a=2, b=2),
            "c_lbo", w=["lbraw"])
        dma(g128[:, 0, :], subg.partition_broadcast(128), "c_g1", w=["g128a"])
        dma(g128[:, 1, :], hng.partition_broadcast(128), "c_g2", w=["g128b"])

        cp("dve", identb, cm[:, 0:128], r=["cm"], w=["identb"])
        cp("dve", rpermb, cm[:, 128:256], r=["cm"], w=["rpermb"])
        for hh in range(4):
            cp("dve", maskf[:, hh * 128:(hh + 1) * 128], cm[:, 512:640], r=["cm"], w=["maskf"])
            cp("dve", maskb[:, hh * 128:(hh + 1) * 128], cm[:, 640:768], r=["cm"], w=["maskb"])
            ts("dve", gainA[:, hh * 128:(hh + 1) * 128], g128[:, 0, :], 1.0 - LAMBDA_INIT, None, ALU.mult, None,
               r=["g128a"], w=["gainA"])
            cp("dve", gainH[:, hh * 128:(hh + 1) * 128], g128[:, 1, :], r=["g128b"], w=["gainH"])
        memset("dve", ones_f, 1.0, w=["ones"])
        tt("dve", lamtmp[:, 0:64], lamraw[:, 0:64], lamraw[:, 64:128], ALU.mult, r=["lamraw"], w=["lamtmp"])
        tt("dve", lamtmp[:, 64:128], lamraw[:, 128:192], lamraw[:, 192:256], ALU.mult, r=["lamraw"], w=["lamtmp"])
        pr.op("dve", lambda g: g.tensor_reduce(lam_t[:, 0:2], lamtmp[:, 0:128].rearrange("p (a b) -> p a b", a=2),
                                               AX.X, ALU.add), r=["lamtmp"], w=["lam"])
        act(lam_t[:, 0:2], lam_t[:, 0:2], AF.Exp, r=["lam"], w=["lam"])
        tt("dve", lam_t[:, 2:3], lam_t[:, 0:1], lam_t[:, 1:2], ALU.subtract, r=["lam"], w=["lam"])
        ts("dve", lam_t[:, 3:4], lam_t[:, 2:3], LAMBDA_INIT, -1.0, ALU.add, ALU.mult, r=["lam"], w=["lam"])
        for d_ in range(2):
            tt("dve", lbt_own[:, d_, :], lbraw[:, d_, 1, :], lbraw[:, d_, 0, :], ALU.subtract, r=["lbraw"], w=["lbo"])
        act(lbt_own, lbt_own, AF.Exp, r=["lbo"], w=["lbo"])
        recip1p(lbt_own, "lbo")
        ts("dve", oml_own, lbt_own, -1.0, 1.0, ALU.mult, ALU.add, r=["lbo"], w=["omo"])
        act(tmpa, cvt, AF.Exp, r=["cvt"], w=["tmpa"], scale=-1.0)
        recip1p(tmpa, "tmpa")
        tt("dve", sc, cvt, tmpa, ALU.mult, r=["cvt", "tmpa"], w=["sc"])
        for k in range(8):
            cp("dve", scb[:, k, :], sc[:, 0, k:k + 1].to_broadcast([128, 128]), r=["sc"], w=["scb"])
        wsts = [wst, wst2]
        for piece in range(6):
            wb = wsts[piece % 2]
            key = "wst%d" % (piece % 2)
            dma(wb, w_ada[:, piece * 512:(piece + 1) * 512].rearrange("(c p) n -> p c n", p=128), "d_" + key, w=[key])
            if piece < 4:
                for jj in range(4):
                    cc = piece * 4 + jj
                    for k in range(8):
                        mm(bank(0)[:, cc * 2:cc * 2 + 2], wb[:, k, jj * 128:(jj + 1) * 128], sc[:, :, k],
                           k == 0, k == 7, r=[key, "sc"], w=["b0"])
            else:
                hb = piece - 4
                for k in range(8):
                    mm(bank(1 + hb), scb[:, k, :], wb[:, k, :], k == 0, k == 7, r=[key, "scb"], w=["b%d" % (1 + hb)])
        cp("dve", modT, bank(0)[:, 0:32].rearrange("p (a b) -> p a b", a=16), r=["b0"], w=["modT"])
        for wch in range(2):
            tt("dve", sh_t[:, wch, :], modT[:, 0:8, wch], bada_t[:, 0:8], ALU.add, r=["modT", "bada"], w=["sh"])
            tt("dve", sc1_t[:, wch, :], modT[:, 8:16, wch], bada_t[:, 8:16], ALU.add, r=["modT", "bada"], w=["sc1"])
        ts("dve", sc1_t, sc1_t, 1.0, None, ALU.add, None, r=["sc1"], w=["sc1"])
        for hb in range(2):
            tt("dve", gate_t[:, hb * 512:(hb + 1) * 512], bank(1 + hb), bgate[:, hb * 512:(hb + 1) * 512], ALU.add,
               r=["b%d" % (1 + hb), "bgate"], w=["gate"])
        pr.barrier()

        ca = Carver(PBASE + 16 * 1024 + 4 * 8 * 1024)
        xt = [ca.take([D], F32) for _ in range(4)]
        hTo = [ca.take([8, 512], BF16) for _ in range(2)]
        for b_ in range(2):
            memset("pool", hTo[b_], 0.0, w=["hTo%d_%d" % (b_, k) for k in range(8)])
        cbw = Carver(PBASE)
        stgB = cbw.take([8, 512], F32)
        WB = [cbw.take([8, 512], BF16) for _ in range(4)]
        wci = [0]

        def load_wB(i_):
            nm = ("ak", "av", "aq", "ag")[i_]
            dma(stgB, w_in[:, WCOL[nm]:WCOL[nm] + 512].rearrange("(c p) n -> p c n", p=128), "d_stg0", w=["stg0"])
            cp(("dve", "act")[i_ % 2], WB[i_], stgB, r=["stg0"], w=["W" + nm])

        blkno = 0
        allrows = [st["row"] + j * 128 for st in sts for j in range(st["n"] // 128)]
        issued = [0]

        def xload_upto(n):
            while issued[0] < min(n, len(allrows)):
                i_ = issued[0]
                dma(xt[i_ % 4], xs[allrows[i_]:allrows[i_] + 128, :], "d_xt%d" % (i_ % 4), w=["xt%d" % (i_ % 4)])
                issued[0] += 1

        for si, st in enumerate(sts):
            ho = hTo[si % 2]
            hk = "hTo%d" % (si % 2)
            which = 1 if not st["lat"] else 0
            for j in range(st["n"] // 128):
                xb = xt[blkno % 4]
                xk = "xt%d" % (blkno % 4)
                xload_upto(blkno + 3)
                if blkno in (8, 20, 32, 44):
                    load_wB(wci[0])
                    wci[0] += 1
                for half in range(2):
                    bk = (blkno * 2 + half) % 4
                    for kk_ in range(4):
                        k = half * 4 + kk_
                        tr(bank(bk)[:, kk_ * 128:(kk_ + 1) * 128], xb[:, k * 128:(k + 1) * 128], identf,
                           r=[xk, "cm"], w=["b%d" % bk])
                    for kk_ in range(4):
                        k = half * 4 + kk_
                        eng = "act" if half == 0 else "dve"
                        src = bank(bk)[:, kk_ * 128:(kk_ + 1) * 128]
                        dst = ho[:, k, j * 128:(j + 1) * 128]
                        if eng == "act":
                            pr.op("act", lambda g, dst=dst, src=src, k=k, which=which: g.activation(
                                out=dst, in_=src, func=AF.Identity, bias=sh_t[:, which, k:k + 1],
                                scale=sc1_t[:, which, k:k + 1]), r=["b%d" % bk], w=[hk + "_%d" % k])
                        else:
                            ts("dve", dst, src, sc1_t[:, which, k:k + 1], sh_t[:, which, k:k + 1], ALU.mult, ALU.add,
                               r=["b%d" % bk], w=[hk + "_%d" % k])
                blkno += 1
            dma(hT_d[si], ho.rearrange("p c n -> p (c n)"), "d_" + hk + "o", r=[hk + "_%d" % k for k in range(8)])
        while wci[0] < 4:
            load_wB(wci[0])
            wci[0] += 1
        pr.barrier()

        def load_w(dst_bf, src_ap, ncols, tag, stg, stgk, ci=[0]):
            dma(stg[:, :, 0:ncols], src_ap.rearrange("(c p) n -> p c n", p=128), "d_" + stgk, w=[stgk])
            eng = ("dve", "act")[ci[0] % 2]
            ci[0] += 1
            cp(eng, dst_bf, stg[:, :, 0:ncols], r=[stgk], w=[tag])

        cb = Carver(PBASE)
        stg = [cb.take([8, 512], F32)] * 2
        Wk = cb.take([8, 512], BF16)
        Wv = cb.take([8, 512], BF16)
        Wq = cb.take([8, 512], BF16)
        Wg = cb.take([8, 512], BF16)
        hTi = [cb.take([8, 512], BF16) for _ in range(2)]
        ropet = [cb.take([2, 512], F32) for _ in range(2)]
        kb = [cb.take([512], BF16) for _ in range(4)]
        t1 = [cb.take([512], F32) for _ in range(2)]
        t2 = [cb.take([512], F32) for _ in range(2)]
        kTo = [cb.take([4, 512], BF16) for _ in range(2)]
        vo = [cb.take([4, 4, 130], BF16) for _ in range(2)]
        q0o = [cb.take([4, 512], BF16) for _ in range(2)]
        q1o = [cb.take([4, 512], BF16) for _ in range(2)]
        gu = [cb.take([512], F32) for _ in range(2)]
        gao = [cb.take([512], BF16) for _ in range(2)]

        for b_ in range(2):
            memset("pool", vo[b_], 0.0, w=["vo%d" % b_])
            memset("pool", vo[b_][:, :, :, 128:129], 1.0, w=["vo%d" % b_])
            memset("pool", q0o[b_], 0.0, w=["q0o%d" % b_])
            memset("pool", q1o[b_], 0.0, w=["q1o%d" % b_])

        pb = [0]

        def nbank():
            pb[0] = (pb[0] + 1) % 8
            return pb[0]

        cnt = 0
        ownblk = 0
        for si, st in enumerate(sts):
            if not st["kv"]:
                continue
            N = st["n"]
            hi_ = hTi[cnt % 2]
            hik = "hTi%d" % (cnt % 2)
            rk = None
            if st["lat"]:
                rt = ropet[cnt % 2]
                rk = "rope%d" % (cnt % 2)

            def b1_load(c_, si_):
                st_ = sts[si_]
                dma(hTi[c_ % 2].rearrange("p c n -> p (c n)"), hT_d[si_], "d_hTi%d" % (c_ % 2), w=["hTi%d" % (c_ % 2)])
                if st_["lat"]:
                    dma(ropet[c_ % 2].rearrange("p a n -> p (a n)"), rope[st_["li"]].rearrange("p a n -> p (a n)"),
                        "d_rope%d" % (c_ % 2), w=["rope%d" % (c_ % 2)])
            if cnt == 0:
                b1_load(0, si)
            nxt_ = [i_ for i_ in range(si + 1, NST) if sts[i_]["kv"]]
            if nxt_:
                b1_load(cnt + 1, nxt_[0])
            own = st["seg"] == 5
            ko = kTo[cnt % 2]
            kok = "kTo%d" % (cnt % 2)
            q0, q1 = q0o[cnt % 2], q1o[cnt % 2]
            qk_ = "qo%d" % (cnt % 2)

            def fm_proj(W, Wkey, h):
                bk = nbank()
                bkk = "b%d" % bk
                for k in range(8):
                    mm(bank(bk)[:, 0:N], W[:, k, h * 128:(h + 1) * 128], hi_[:, k, 0:N], k == 0, k == 7,
                       r=[Wkey, hik], w=[bkk])
                return bk, bkk

            def fm_evac(bk, bkk, outs, outkey, h):
                if not st["lat"]:
                    cp("act", outs[0][:, h, 0:N], bank(bk)[:, 0:N], r=[bkk], w=[outkey])
                    return
                cp("act", kb[h][:, 0:N], bank(bk)[:, 0:N], r=[bkk], w=["kb%d" % h])

            def fm_rope(outs, outkey, h):
                if not st["lat"]:
                    return
                sl = h % 2
                kbb, kbk = kb[h], "kb%d" % h
                bk2 = nbank()
                bk2k = "b%d" % bk2
                mm(bank(bk2)[:, 0:N], rpermb, kbb[:, 0:N], True, True, r=["rpermb", kbk], w=[bk2k])
                tt("dve", t1[sl][:, 0:N], kbb[:, 0:N], rt[:, 0, 0:N], ALU.mult, r=[kbk, rk], w=["t1%d" % sl])
                tt("dve", t2[sl][:, 0:N], bank(bk2)[:, 0:N], rt[:, 1, 0:N], ALU.mult, r=[bk2k, rk], w=["t2%d" % sl])
                if len(outs) == 1:
                    tt("pool", outs[0][:, h, 0:N], t1[sl][:, 0:N], t2[sl][:, 0:N], ALU.add,
                       r=["t1%d" % sl, "t2%d" % sl], w=[outkey])
                else:
                    tt("pool", outs[0][0:64, h, 0:N], t1[sl][0:64, 0:N], t2[sl][0:64, 0:N], ALU.add,
                       r=["t1%d" % sl, "t2%d" % sl], w=[outkey])
                    tt("pool", outs[1][64:128, h, 0:N], t1[sl][64:128, 0:N], t2[sl][64:128, 0:N], ALU.add,
                       r=["t1%d" % sl, "t2%d" % sl], w=[outkey])

            kbs = [fm_proj(Wk, "Wak", h) for h in range(4)]
            for h in range(4):
                fm_evac(kbs[h][0], kbs[h][1], [ko], kok, h)
            vb = vo[cnt % 2]
            vk = "vo%d" % (cnt % 2)
            nsub = N // 128
            for j in range(nsub):
                bk = nbank()
                bkk = "b%d" % bk
                for k in range(8):
                    mm(bank(bk), hi_[:, k, j * 128:(j + 1) * 128], Wv[:, k, :], k == 0, k == 7, r=["Wav", hik], w=[bkk])
                cp("act", vb[:, j, :, 0:128], bank(bk).rearrange("p (h e) -> p h e", h=4), r=[bkk], w=[vk])
            for h in range(4):
                fm_rope([ko], kok, h)
            dma(kT_d[:, :, st["key0"]:st["key0"] + N].rearrange("h p n -> p h n"), ko[:, :, 0:N], "d_" + kok, r=[kok])
            kt0 = st["key0"] // 128
            for h in range(4):
                dma(v_d[h, :, kt0:kt0 + nsub, :], vb[:, 0:nsub, h, :], "d_%s_%d" % (vk, h), r=[vk])
            if own:
                qbs = [fm_proj(Wq, "Waq", h) for h in range(4)]
                for h in range(4):
                    fm_evac(qbs[h][0], qbs[h][1], [q0, q1], qk_, h)
                for j in range(nsub):
                    bk = nbank()
                    bkk = "b%d" % bk
                    for k in range(8):
                        mm(bank(bk), hi_[:, k, j * 128:(j + 1) * 128], Wg[:, k, :], k == 0, k == 7,
                           r=["Wag", hik], w=[bkk])
                    sl = j % 2
                    act(gu[sl], bank(bk), AF.Exp, r=[bkk], w=["gu%d" % sl], scale=-1.0)
                    recip1p(gu[sl], "gu%d" % sl)
                    tt("dve", gu[sl], bank(bk), gu[sl], ALU.mult, r=[bkk, "gu%d" % sl], w=["gu%d" % sl])
                    tt("pool", gao[sl], gu[sl], gainA, ALU.mult, r=["gu%d" % sl, "gainA"], w=["gao%d" % sl])
                    dma(ga_d[ownblk + j], gao[sl], "d_gao%d" % sl, r=["gao%d" % sl])
                for h in range(4):
                    fm_rope([q0, q1], qk_, h)
                tok0 = ownblk * 128
                dma(q_d[:, 0, :, tok0:tok0 + N].rearrange("h p n -> p h n"), q0, "d_q0%d" % (cnt % 2), r=[qk_])
                dma(q_d[:, 1, :, tok0:tok0 + N].rearrange("h p n -> p h n"), q1, "d_q1%d" % (cnt % 2), r=[qk_])
                ownblk += nsub
            cnt += 1
        pr.barrier()

        ch = Carver(PBASE)
        stg = [ch.take([8, 512], F32)] * 2
        Whi = ch.take([8, 512], BF16)
        Wz1 = ch.take([8, 512], BF16)
        hTi = [ch.take([8, 512], BF16) for _ in range(2)]
        Lfb = ch.take([128], BF16)
        Lbb = ch.take([128], BF16)
        ones_b = ch.take([4], BF16)
        NS = 3
        U2 = [ch.take([2, 512], F32) for _ in range(2)]
        G2 = [ch.take([2, 512], F32) for _ in range(2)]
        GH2 = [ch.take([2, 512], BF16) for _ in range(2)]
        GL2 = [ch.take([2, 512], BF16) for _ in range(2)]
        KT2 = [ch.take([2, 512], BF16) for _ in range(2)]
        QT2 = [ch.take([2, 512], BF16) for _ in range(2)]
        D2 = [ch.take([2, 4], F32) for _ in range(NS)]
        def slot_at(base):
            c5 = Carver(base)
            return [c5.take([2, 512], F32), c5.take([2, 512], F32), c5.take([2, 512], BF16), c5.take([2, 512], BF16),
                    c5.take([2, 512], BF16), c5.take([2, 512], BF16)], c5.off
        slot2_b3, e5 = slot_at(PBASE)
        assert e5 <= PBASE + 8 * 512 * 4

        def set_slot2(bufs):
            for lst, ap_ in zip((U2, G2, GH2, GL2, KT2, QT2), bufs):
                if len(lst) == 2:
                    lst.append(ap_)
                else:
                    lst[2] = ap_
        hib2 = [ch.take([2, 512], BF16) for _ in range(3)]
        ss = ch.take([4], F32)
        c2 = Carver(ch.off + 3 * 8 * 512 * 2)
        lbr = c2.take([2, 512], F32)
        lbt_s = c2.take([512], F32)
        oml_s = c2.take([512], F32)
        lbt2 = c2.take([2, 512], F32)
        oml2 = c2.take([2, 512], F32)
        c3 = Carver(ch.off)
        slot2_b2, e6 = slot_at(ch.off)
        Wz2 = c3.take([8, 512], BF16)
        Whq = c3.take([8, 512], BF16)
        assert e6 <= c3.off
        Whg = c3.take([8, 512], BF16)
        set_slot2(slot2_b2)
        qf = [c3.take([512], F32) for _ in range(3)]
        ghu = c3.take([512], F32)
        gho = [c3.take([512], BF16) for _ in range(2)]
        XT = [c3.take([512], BF16) for _ in range(4)]
        scm = [c3.take([512], BF16) for _ in range(2)]
        Sbf = c3.take([512], BF16)
        Rt = c3.take([512], F32)
        opt = [c3.take([512], F32) for _ in range(2)]
        incb = [c3.take([512], F32) for _ in range(2)]
        HEND = max(c2.off, c3.off)

        cp("dve", Lfb, cm[:, 256:384], r=[], w=["Lfb"])
        cp("dve", Lbb, cm[:, 384:512], r=[], w=["Lbb"])
        memset("dve", ones_b, 1.0, w=["ones_b"])

        def npair():
            nxt = ((pb[0] + 2) // 2 * 2) % 8
            pb[0] = nxt + 1
            return nxt // 2

        def pair_ap(p):
            return psum[:, 2 * p:2 * p + 2, :]

        def pkeys(p):
            return ["b%d" % (2 * p), "b%d" % (2 * p + 1)]

        zc = [0]

        def zchain2_a(p):
            s = zc[0] % NS
            zc[0] += 1
            act(U2[s], pair_ap(p), AF.Exp, r=pkeys(p), w=["U%d" % s], scale=-1.0)
            return s

        def zchain2_b(s, Lmats, lbt, oml, lbkeys, want_q, qsrc=None, qkey=None, fixed=None):
            k_ = lambda n: "%s%d" % (n, s)
            recip1p(U2[s], k_("U"))
            tt("dve", U2[s], U2[s], oml, ALU.mult, r=[k_("U")] + lbkeys, w=[k_("U")])
            tt("dve", U2[s], U2[s], lbt, ALU.add, r=[k_("U")] + lbkeys, w=[k_("U")])
            act(G2[s], U2[s], AF.Ln, r=[k_("U")], w=[k_("G")])
            ts("pool", U2[s], U2[s], -1.0, 1.0, ALU.mult, ALU.add, r=[k_("U")], w=[k_("U")])
            cp("act", GH2[s], G2[s], r=[k_("G")], w=[k_("GH")])
            tt("dve", GL2[s], G2[s], GH2[s], ALU.subtract, r=[k_("G"), k_("GH")], w=[k_("GL")])
            pr.mark()
            p = npair() if fixed is None else fixed[0]
            for l in range(2):
                bkk = "b%d" % (2 * p + l)
                mm(bank(2 * p + l), Lmats[l][0], GH2[s][:, l, :], True, False, r=[Lmats[l][1], k_("GH")], w=[bkk])
                mm(bank(2 * p + l), Lmats[l][0], GL2[s][:, l, :], False, True, r=[Lmats[l][1], k_("GL")], w=[bkk])
            bd = nbank() if fixed is None else fixed[1]
            bdk = "b%d" % bd
            for l in range(2):
                for h in range(4):
                    hs = slice(h * 128, (h + 1) * 128)
                    c = l * 4 + h
                    mm(bank(bd)[:, c:c + 1], GH2[s][:, l, hs], ones_b[:, 0:1], True, False, r=[k_("GH"), "ones_b"], w=[bdk])
                    mm(bank(bd)[:, c:c + 1], GL2[s][:, l, hs], ones_b[:, 0:1], False, True, r=[k_("GL"), "ones_b"], w=[bdk])
            act(G2[s], pair_ap(p), AF.Exp, r=pkeys(p) + [k_("GL")], w=[k_("G")], scale=-1.0)
            tt("dve", KT2[s], U2[s], G2[s], ALU.mult, r=[k_("U"), k_("G")], w=[k_("KT")])
            if want_q:
                act(G2[s], pair_ap(p), AF.Exp, r=pkeys(p), w=[k_("G")])
                for l in range(2):
                    tt("dve", QT2[s][:, l, :], qsrc, G2[s][:, l, :], ALU.mult, r=[qkey, k_("G")], w=[k_("QT")])
            act(D2[s].rearrange("p a b -> p (a b)"), bank(bd)[:, 0:8], AF.Exp, r=[bdk], w=[k_("D")])
            pr.mark()

        def tokmajor_to(bk, W, Wkey, hi_, hik, j):
            bkk = "b%d" % bk
            for k in range(8):
                mm(bank(bk), hi_[:, k, j * 128:(j + 1) * 128], W[:, k, :], k == 0, k == 7, r=[Wkey, hik], w=[bkk])
            return bk, bkk

        def tokmajor(W, Wkey, hi_, hik, j):
            return tokmajor_to(nbank(), W, Wkey, hi_, hik, j)

        def inc_mm_to(bk, ktile, ktkey, hb_, hbk):
            bkk = "b%d" % bk
            for h in range(4):
                mm(bank(bk)[:, h * 128:(h + 1) * 128], ktile[:, h * 128:(h + 1) * 128], hb_[:, h * 128:(h + 1) * 128],
                   True, True, r=[ktkey, hbk], w=[bkk])
            return bk, bkk

        def rstep(R, Rkey, Dprev, Dkey, incsrc, inckey):
            for h in range(4):
                hs = slice(h * 128, (h + 1) * 128)
                stt(R[:, hs], R[:, hs], Dprev[:, h:h + 1], incsrc[:, hs], ALU.mult, ALU.add,
                    r=[Rkey, Dkey, inckey], w=[Rkey])

        def state_update(S, Skey, incsrc, inckey, Dtile, Dkey):
            tt("dve", Rt, S, incsrc, ALU.add, r=[Skey, inckey], w=["Rt"])
            for h in range(4):
                ts("dve", S[:, h * 128:(h + 1) * 128], Rt[:, h * 128:(h + 1) * 128],
                   Dtile[:, h:h + 1], None, ALU.mult, None, r=["Rt", Dkey], w=[Skey])

        load_w(Whi, w_in[:, WCOL["hi"]:WCOL["hi"] + 512], 512, "Whi", stg[0], "stg0")
        for S_ in (S_cF, S_cB, S_cur, SF_fin, SB_fin):
            memset("pool", S_, 0.0, w=["Sx"])
        pr.barrier()
        segs = {}
        for si, st in enumerate(sts):
            segs.setdefault(st["seg"], []).append(si)
        hcnt = [0]
        pend = [None]
        hlist = [si for seg_ in range(6) for si in segs[seg_]]
        hc = [0]

        def h_load(c_, si_):
            dma(hTi[c_ % 2].rearrange("p c n -> p (c n)"), hT_d[si_], "d_hTi%d" % (c_ % 2), w=["hTi%d" % (c_ % 2)])

        hist = []

        def split(lst):
            parts, cur = [], []
            for it_ in lst:
                if it_ is None:
                    parts.append(cur)
                    cur = []
                else:
                    cur.append(it_)
            parts.append(cur)
            return parts

        def interleave(*lists):
            n_ = max([len(l_) for l_ in lists] + [0])
            for i_ in range(n_):
                for l_ in lists:
                    if i_ < len(l_):
                        pr.replay(l_[i_])

        def pipe_emit():
            ls = []
            for k_ in range(3):
                if len(hist) > k_ and len(hist[-1 - k_]) > k_:
                    ls.append(hist[-1 - k_][k_])
            interleave(*ls)

        def flush():
            pipe_emit()
            for _ in range(2):
                hist.append([[], [], []])
                pipe_emit()
            del hist[:]

        Wzs = [(Wz1, "Wz1"), (Whg, "WzB")]
        load_w(Wz1, wz[0], 512, "Wz1", stg[0], "stg0")
        for seg in range(5):
            flush()
            Wz_, Wzk = Wzs[seg % 2]
            if seg + 1 < 5:
                load_w(Wzs[(seg + 1) % 2][0], wz[seg + 1], 512, Wzs[(seg + 1) % 2][1], stg[0], "stg0")
            dma(lbr.rearrange("p a n -> p (a n)"), lbseg[seg].rearrange("a n -> (a n)").partition_broadcast(128),
                "d_lbr", w=["lbr"])
            tt("dve", lbt_s, lbr[:, 1, :], lbr[:, 0, :], ALU.subtract, r=["lbr"], w=["lbt_s"])
            act(lbt_s, lbt_s, AF.Exp, r=["lbt_s"], w=["lbt_s"])
            recip1p(lbt_s, "lbt_s")
            ts("dve", oml_s, lbt_s, -1.0, 1.0, ALU.mult, ALU.add, r=["lbt_s"], w=["oml_s"])
            for l in range(2):
                cp("dve", lbt2[:, l, :], lbt_s, r=["lbt_s"], w=["lbt2"])
                cp("dve", oml2[:, l, :], oml_s, r=["oml_s"], w=["oml2"])
            if seg == 0:
                S, Sk = S_cF, "S_cF"
            elif seg == 1:
                S, Sk = S_cB, "S_cB"
            else:
                S, Sk = S_cur, "S_cur"
                b = seg - 2
                Xsrc, Xk = (S_cF, "S_cF") if b == 0 else (S_cur, "S_cur")
                stt(SF_fin, Xsrc, selt[:, b:b + 1], SF_fin, ALU.mult, ALU.add, r=[Xk, "SF_fin", "selt"], w=["SF_fin"])
                if b == 0:
                    ts("dve", S_cur, S_cF, selt[:, 8:9], None, ALU.mult, None, r=["S_cF", "selt"], w=["S_cur"])
                else:
                    ts("dve", S_cur, S_cur, selt[:, 4 + b:5 + b], None, ALU.mult, None, r=["S_cur", "selt"], w=["S_cur"])
                stt(S_cur, S_cB, selt[:, 12 + b:13 + b], S_cur, ALU.mult, ALU.add, r=["S_cB", "S_cur", "selt"], w=["S_cur"])
            dstate = [(ones_f, "ones")]
            for si in segs[seg]:
                st = sts[si]
                hi_ = hTi[hc[0] % 2]
                hik = "hTi%d" % (hc[0] % 2)
                if hc[0] == 0:
                    h_load(0, hlist[0])
                if hc[0] + 1 < len(hlist):
                    h_load(hc[0] + 1, hlist[hc[0] + 1])
                hc[0] += 1
                for jp in range(st["n"] // 256):
                    for l in range(2):
                        tokmajor_to(l, Whi, "Whi", hi_, hik, 2 * jp + l)
                    for l in range(2):
                        tokmajor_to(2 + l, Wz_, Wzk, hi_, hik, 2 * jp + l)
                    hsl = hcnt[0] % 3
                    hcnt[0] += 1
                    hb_ = hib2[hsl]
                    hbk = "hib%d" % hsl

                    def PPOST(hb_=hb_, hbk=hbk):
                        cp("act", hb_, pair_ap(0), r=pkeys(0), w=[hbk])
                        return zchain2_a(1)

                    def mkQ(s, hb_=hb_, hbk=hbk, S=S, Sk=Sk):
                        def Q():
                            zchain2_b(s, [(Lfb, "Lfb"), (Lfb, "Lfb")], lbt2, oml2, ["lbt2", "oml2"], False, fixed=(2, 6))
                            for l in range(2):
                                inc_mm_to(7, KT2[s][:, l, :], "KT%d" % s, hb_[:, l, :], hbk)
                                rstep(S, Sk, dstate[0][0], dstate[0][1], bank(7), "b7")
                                dstate[0] = (D2[s][:, l, :], "D%d" % s)
                        return Q
                    pipe_emit()
                    s_ = PPOST()
                    parts_ = split(pr.capture(mkQ(s_)))
                    hist.append(parts_ if DEEP3 else [parts_[0], parts_[1] + parts_[2]])
            flush()
            for h in range(4):
                hs = slice(h * 128, (h + 1) * 128)
                ts("dve", S[:, hs], S[:, hs], dstate[0][0][:, h:h + 1], None, ALU.mult, None, r=[Sk, dstate[0][1]], w=[Sk])
        stt(SF_fin, S_cur, selt[:, 3:4], SF_fin, ALU.mult, ALU.add, r=["S_cur", "SF_fin", "selt"], w=["SF_fin"])
        ts("dve", SB_fin, S_cur, selt[:, 7:8], None, ALU.mult, None, r=["S_cur", "selt"], w=["SB_fin"])
        stt(SB_fin, S_cB, selt[:, 15:16], SB_fin, ALU.mult, ALU.add, r=["S_cB", "SB_fin", "selt"], w=["SB_fin"])
        pr.barrier()

        load_w(Wz1, w_in[:, WCOL["hff"]:WCOL["hff"] + 512], 512, "Wz1", stg[0], "stg0")
        load_w(Wz2, w_in[:, WCOL["hfb"]:WCOL["hfb"] + 512], 512, "Wz2", stg[0], "stg0")
        load_w(Whq, w_in[:, WCOL["hq"]:WCOL["hq"] + 512], 512, "Whq", stg[0], "stg0")
        load_w(Whg, w_in[:, WCOL["hg"]:WCOL["hg"] + 512], 512, "Whg", stg[0], "stg0")
        pr.barrier()
        set_slot2(slot2_b3)
        ob = 0
        dstate = [(ones_f, "ones")]
        for si in segs[5]:
            st = sts[si]
            hi_ = hTi[hc[0] % 2]
            hik = "hTi%d" % (hc[0] % 2)
            if hc[0] + 1 < len(hlist):
                h_load(hc[0] + 1, hlist[hc[0] + 1])
            hc[0] += 1
            for j in range(4):
                sl = ob % 2
                s3 = ob % 3
                tokmajor_to(0, Whi, "Whi", hi_, hik, j)
                tokmajor_to(2, Wz1, "Wz1", hi_, hik, j)
                tokmajor_to(3, Wz2, "Wz2", hi_, hik, j)
                tokmajor_to(1, Whq, "Whq", hi_, hik, j)
                tokmajor_to(4, Whg, "Whg", hi_, hik, j)
                pipe_emit()
                hb_ = hib2[s3][:, 0, :]
                hbk = "hib%d" % s3
                cp("act", hb_, bank(0), r=["b0"], w=[hbk])
                s = zchain2_a(1)
                qk = "qf%d" % s3
                act(qf[s3], bank(1), AF.Exp, r=["b1"], w=[qk], scale=-1.0)
                recip1p(qf[s3], qk)
                tt("dve", qf[s3], bank(1), qf[s3], ALU.mult, r=["b1", qk], w=[qk])
                act(ghu, bank(4), AF.Exp, r=["b4"], w=["ghu"], scale=-1.0)
                recip1p(ghu, "ghu")
                tt("dve", ghu, bank(4), ghu, ALU.mult, r=["b4", "ghu"], w=["ghu"])
                tt("pool", gho[sl], ghu, gainH, ALU.mult, r=["ghu", "gainH"], w=["gho%d" % sl])
                dma(gh_d[ob], gho[sl], "d_gho%d" % sl, r=["gho%d" % sl])

                def mkQ(ob=ob, sl=sl, s3=s3, hb_=hb_, hbk=hbk, s=s, qk=qk):
                    def Q():
                        zchain2_b(s, [(Lfb, "Lfb"), (Lbb, "Lbb")], lbt_own, oml_own, ["lbo", "omo"], True, qf[s3], qk,
                                  fixed=(3, 5))
                        srcs = [(QT2[s][:, 0, :], "QT%d" % s), (KT2[s][:, 0, :], "KT%d" % s),
                                (QT2[s][:, 1, :], "QT%d" % s), (KT2[s][:, 1, :], "KT%d" % s)]
                        for xi, (src, srck) in enumerate(srcs):
                            bk = 5 + xi // 2
                            bkk = "b%d" % bk
                            c0_ = (xi % 2) * 512
                            for h in range(4):
                                tr(bank_bf(bk)[:, c0_ + h * 128:c0_ + (h + 1) * 128], src[:, h * 128:(h + 1) * 128], identb,
                                   r=[srck, "identb"], w=[bkk])
                            cp("act" if bk == 5 else "dve", XT[xi], bank_bf(bk)[:, c0_:c0_ + 512], r=[bkk], w=["XT%d" % xi])
                        for d_, (qi, ki, msk, mk, bk) in enumerate(((0, 1, maskf, "maskf", 7), (2, 3, maskb, "maskb", 5))):
                            bkk = "b%d" % bk
                            for h in range(4):
                                mm(bank(bk)[:, h * 128:(h + 1) * 128], XT[ki][:, h * 128:(h + 1) * 128],
                                   XT[qi][:, h * 128:(h + 1) * 128], True, True, r=["XT%d" % ki, "XT%d" % qi], w=[bkk])
                            tt("dve", scm[d_], bank(bk), msk, ALU.mult, r=[bkk, mk], w=["scm%d" % d_])
                        for h in range(4):
                            hs = slice(h * 128, (h + 1) * 128)
                            ts("dve", Sbf[:, hs], SF_fin[:, hs], dstate[0][0][:, h:h + 1], None, ALU.mult, None,
                               r=["SF_fin", dstate[0][1]], w=["Sbf"])
                        for h in range(4):
                            hs = slice(h * 128, (h + 1) * 128)
                            mm(bank(6)[:, hs], scm[0][:, hs], hb_[:, hs], True, False, r=["scm0", hbk], w=["b6"])
                            mm(bank(6)[:, hs], scm[1][:, hs], hb_[:, hs], False, False, r=["scm1", hbk], w=["b6"])
                            mm(bank(6)[:, hs], XT[0][:, hs], Sbf[:, hs], False, True, r=["XT0", "Sbf"], w=["b6"])
                        cp("act", opt[sl], bank(6), r=["b6"], w=["opt%d" % sl])
                        dma(op_d[ob], opt[sl], "d_opt%d" % sl, r=["opt%d" % sl])
                        dma(qbT_d[ob], XT[2], "d_qbT", r=["XT2"])
                        inc_mm_to(6, KT2[s][:, 0, :], "KT%d" % s, hb_, hbk)
                        inc_mm_to(7, KT2[s][:, 1, :], "KT%d" % s, hb_, hbk)
                        rstep(SF_fin, "SF_fin", dstate[0][0], dstate[0][1], bank(6), "b6")
                        dstate[0] = (D2[s][:, 0, :], "D%d" % s)
                        cp("act", incb[sl], bank(7), r=["b7"], w=["incb%d" % sl])
                        dma(incb_d[ob], incb[sl], "d_incb%d" % sl, r=["incb%d" % sl])
                        cp("pool", Db_all[:, ob, :], D2[s][:, 1, :], r=["D%d" % s], w=["Db_all"])
                    return Q
                parts_ = split(pr.capture(mkQ()))
                hist.append([parts_[0], parts_[1] + parts_[2]])
                ob += 1
        flush()
        pr.barrier()

        cc_ = Carver(PBASE)
        KTt = [cc_.take([NK], BF16) for _ in range(2)]
        Vt = [cc_.take([NKT, 130], BF16) for _ in range(2)]
        Qt = [cc_.take([2, NQ], BF16) for _ in range(2)]
        PT = [cc_.take([2, 512], BF16) for _ in range(3)]
        rr = cc_.take([8], F32)
        accs = cc_.take([8, 130], F32)
        oa4 = cc_.take([4, 128], F32)
        osq4 = cc_.take([4, 128], F32)
        ssa4 = cc_.take([4], F32)
        mhalf = cc_.take([4], F32)
        gal = cc_.take([4, 128], BF16)
        memset("pool", mhalf, -0.5, w=["mhalf"])
        qbl = [cc_.take([512], BF16) for _ in range(2)]
        o2 = [cc_.take([512], F32) for _ in range(2)]
        sq4 = cc_.take([512], F32)
        ghl = [cc_.take([512], BF16) for _ in range(2)]
        opt4 = [cc_.take([512], F32) for _ in range(2)]
        incb4 = [cc_.take([512], F32) for _ in range(2)]
        Sbf4 = cc_.take([512], BF16)
        Rt4 = cc_.take([512], F32)
        ss4 = cc_.take([4], F32)

        def b4_block(ob):
            sl = ob % 2
            dma(qbl[sl], qbT_d[ob], "d_qbl%d" % sl, w=["qbl%d" % sl])
            dma(opt4[sl], op_d[ob], "d_opl%d" % sl, w=["opt4%d" % sl])
            dma(incb4[sl], incb_d[ob], "d_incl%d" % sl, w=["incb4%d" % sl])
            dma(ghl[sl], gh_d[ob], "d_ghl%d" % sl, w=["ghl%d" % sl])
            cp("pool", Sbf4, SB_fin, r=["SB_fin"], w=["Sbf4"])
            for h_ in range(4):
                hs = slice(h_ * 128, (h_ + 1) * 128)
                mm(bank(7)[:, hs], qbl[sl][:, hs], Sbf4[:, hs], True, True, r=["qbl%d" % sl, "Sbf4"], w=["b7"])
            tt("dve", o2[sl], bank(7), opt4[sl], ALU.add, r=["b7", "opt4%d" % sl], w=["o2%d" % sl])
            tt("dve", Rt4, SB_fin, incb4[sl], ALU.add, r=["SB_fin", "incb4%d" % sl], w=["Rt4"])
            for h_ in range(4):
                hs = slice(h_ * 128, (h_ + 1) * 128)
                ts("dve", SB_fin[:, hs], Rt4[:, hs], Db_all[:, ob, h_:h_ + 1], None, ALU.mult, None,
                   r=["Rt4", "Db_all"], w=["SB_fin"])
            tt("pool", sq4, o2[sl], o2[sl], ALU.mult, r=["o2%d" % sl], w=["sq4"])
            pr.op("dve", lambda g: g.tensor_reduce(ss4, sq4.rearrange("p (a b) -> p a b", a=4), AX.X, ALU.add),
                  r=["sq4"], w=["ss4"])
            ts("dve", ss4, ss4, 1.0 / 128.0, EPS, ALU.mult, ALU.add, r=["ss4"], w=["ss4"])
            tt("pool", ss4, ss4, mhalf, ALU.pow, r=["ss4", "mhalf"], w=["ss4"])
            tt("dve", o2[sl].rearrange("p (a b) -> p a b", a=4), o2[sl].rearrange("p (a b) -> p a b", a=4),
               ss4.unsqueeze(2).to_broadcast([128, 4, 128]), ALU.mult, r=["o2%d" % sl, "ss4"], w=["o2%d" % sl])
            tt("pool", Y[:, ob, 512:1024], o2[sl], ghl[sl], ALU.mult, r=["o2%d" % sl, "ghl%d" % sl], w=["Yh"])

        b4_next = [NBO - 1]
        b4_every = max(1, (4 * (NQ // 512) * NKT) // (NBO + 2))
        NQC = NQ // 512
        it = 0
        def acc(m, qs):
            i = m * 4 + qs
            return psum[:, 4 + i // 3, (i % 3) * 130:(i % 3) * 130 + 130]

        def c_load(h_):
            hb_ = h_ % 2
            dma(KTt[hb_], kT_d[h_], "d_KT%d" % hb_, w=["KTt%d" % hb_])
            dma(Vt[hb_].rearrange("p a b -> p (a b)"), v_d[h_].rearrange("p a b -> p (a b)"), "d_V%d" % hb_, w=["Vt%d" % hb_])
            dma(Qt[hb_], q_d[h_].rearrange("m p n -> p m n"), "d_Q%d" % hb_, w=["Qt%d" % hb_])

        for h in range(4):
            hb = h % 2
            if PREF_C:
                if h == 0:
                    c_load(0)
                if h + 1 < 4:
                    c_load(h + 1)
            else:
                c_load(h)
            for qc in range(NQC):
                def qk_exp(kt, it_):
                    sb_ = it_ % 2
                    for m in range(2):
                        ps_ = slice(m * 64, (m + 1) * 64)
                        pr.op("pe", lambda g, o_=bank(sb_ * 2 + m), l_=KTt[hb][ps_, kt * 128:(kt + 1) * 128],
                              r_=Qt[hb][ps_, m, qc * 512:(qc + 1) * 512], tp=(m * 64, 0):
                              g.matmul(o_, l_, r_, start=True, stop=True, tile_position=tp),
                              r=["KTt%d" % hb, "Qt%d" % hb], w=["S%d" % sb_])
                    pt = PT[it_ % 3]
                    ptk = "PT%d" % (it_ % 3)
                    act(pt, psum[:, sb_ * 2:sb_ * 2 + 2, :], AF.Exp, r=["S%d" % sb_], w=[ptk], scale=0.125)

                def pv(kt, it_):
                    pt = PT[it_ % 3]
                    ptk = "PT%d" % (it_ % 3)
                    for m in range(2):
                        for qs in range(4):
                            stf = (kt == 0) and ((m * 4 + qs) in (0, 3, 6))
                            pr.op("pe", lambda g, a=acc(m, qs), l=pt[:, m, qs * 128:(qs + 1) * 128], v=Vt[hb][:, kt, :],
                                  stf=stf, sp_=(kt == NKT - 1): g.matmul(a, l, v, start=stf, stop=sp_, skip_group_check=True),
                                  r=[ptk, "Vt%d" % hb], w=["acc"])

                qk_exp(0, it)
                qk_exp(1, it + 1)
                for kt in range(NKT):
                    if kt + 2 < NKT:
                        qk_exp(kt + 2, it + 2)
                    pv(kt, it)
                    it += 1
                    if it % b4_every == 0 and b4_next[0] >= 0:
                        b4_block(b4_next[0])
                        b4_next[0] -= 1
                for bk_, n_ in ((4, 3), (5, 3), (6, 2)):
                    cp("dve", accs[:, (bk_ - 4) * 3:(bk_ - 4) * 3 + n_, :],
                       psum[:, bk_, 0:n_ * 130].rearrange("p (a b) -> p a b", a=n_), r=["acc"], w=["accs"])
                dma(gal, ga_d[qc * 4:(qc + 1) * 4, :, h * 128:(h + 1) * 128].rearrange("q p e -> p q e"), "d_gal", w=["gal"])
                pr.op("dve", lambda g: g.reciprocal(rr, accs[:, :, 128:129].rearrange("p a b -> p (a b)")), r=["accs"], w=["rr"])
                ts("dve", rr[:, 4:8], rr[:, 4:8], lam_t[:, 3:4], None, ALU.mult, None, r=["rr", "lam"], w=["rr"])
                tt("dve", accs[:, :, 0:128], accs[:, :, 0:128], rr.unsqueeze(2).to_broadcast([128, 8, 128]), ALU.mult,
                   r=["accs", "rr"], w=["accs"])
                tt("dve", oa4, accs[:, 0:4, 0:128], accs[:, 4:8, 0:128], ALU.add, r=["accs"], w=["oa4"])
                tt("pool", osq4, oa4, oa4, ALU.mult, r=["oa4"], w=["osq4"])
                pr.op("dve", lambda g: g.tensor_reduce(ssa4, osq4, AX.X, ALU.add), r=["osq4"], w=["ssa4"])
                ts("dve", ssa4, ssa4, 1.0 / 128.0, EPS, ALU.mult, ALU.add, r=["ssa4"], w=["ssa4"])
                tt("pool", ssa4, ssa4, mhalf, ALU.pow, r=["ssa4", "mhalf"], w=["ssa4"])
                tt("dve", oa4, oa4, ssa4.unsqueeze(2).to_broadcast([128, 4, 128]), ALU.mult, r=["oa4", "ssa4"], w=["oa4"])
                tt("dve", Y[:, qc * 4:(qc + 1) * 4, h * 128:(h + 1) * 128], oa4, gal, ALU.mult, r=["oa4", "gal"], w=["Ya"])
        while b4_next[0] >= 0:
            b4_block(b4_next[0])
            b4_next[0] -= 1
        pr.barrier()

        cd = Carver(PBASE)
        stg = [cd.take([8, 512], F32) for _ in range(2)]
        Wo = cd.take([8, D], BF16)
        YT = [cd.take([8, 128], BF16) for _ in range(2)]
        xo = [cd.take([D], F32) for _ in range(2)]
        zt = [cd.take([D], F32) for _ in range(2)]
        stats2 = [cd.take([2, 6], F32) for _ in range(2)]
        mv2 = [cd.take([2], F32) for _ in range(2)]
        rstd2 = [cd.take([1], F32) for _ in range(2)]
        nbias2 = [cd.take([1], F32) for _ in range(2)]
        mhalfd = cd.take([1], F32)
        memset("pool", mhalfd, -0.5, w=["mhalfd"])
        lng_t = cd.take([D], F32)
        lnb_t = cd.take([D], F32)
        dma(lng_t, lng.partition_broadcast(128), "c_lng", w=["lng"])
        dma(lnb_t, lnb.partition_broadcast(128), "c_lnb", w=["lnb"])
        for hb in range(2):
            load_w(Wo[:, :, hb * 512:(hb + 1) * 512], w_out[:, hb * 512:(hb + 1) * 512], 512, "Wo", stg[hb], "stg%d" % hb)
        own_row0 = 2 * CTX + 3 * NQ
        def d_load(t_):
            dma(xo[t_ % 2], xs[own_row0 + t_ * 128:own_row0 + (t_ + 1) * 128, :], "d_xo%d" % (t_ % 2), w=["xo%d" % (t_ % 2)])

        def d_pe(qt_):
            sl = qt_ % 2
            d_load(qt_)
            bk = 3 * sl
            bkk = "b%d" % bk
            for k in range(8):
                tr(bank_bf(bk)[:, k * 128:(k + 1) * 128], Y[:, qt_, k * 128:(k + 1) * 128], identb, r=["Y", "identb"], w=[bkk])
            cp("act", YT[sl].rearrange("p a b -> p (a b)"), bank_bf(bk), r=[bkk], w=["YT%d" % sl])
            for hb in range(2):
                bko = 3 * sl + 1 + hb
                bkok = "b%d" % bko
                for k in range(8):
                    mm(bank(bko), YT[sl][:, k, :], Wo[:, k, hb * 512:(hb + 1) * 512], k == 0, k == 7,
                       r=["YT%d" % sl, "Wo"], w=[bkok])

        def d_chain(qt_):
            sl = qt_ % 2
            for hb in range(2):
                bko = 3 * sl + 1 + hb
                hs = slice(hb * 512, (hb + 1) * 512)
                tt("dve", zt[sl][:, hs], bank(bko), gate_t[:, hs], ALU.mult, r=["b%d" % bko, "gate"], w=["zt%d" % sl])
            stt(zt[sl], xo[sl], ALPHA, zt[sl], ALU.mult, ALU.add, r=["xo%d" % sl, "zt%d" % sl], w=["zt%d" % sl])
            stats, mv, rstd = stats2[sl], mv2[sl], rstd2[sl]
            for hb in range(2):
                pr.op("dve", lambda g, hb=hb, sl=sl, stats=stats: g.bn_stats(stats[:, hb, :], zt[sl][:, hb * 512:(hb + 1) * 512]),
                      r=["zt%d" % sl], w=["stats%d" % sl])
            pr.op("dve", lambda g, stats=stats, mv=mv: g.bn_aggr(mv, stats.rearrange("p a b -> p (a b)")),
                  r=["stats%d" % sl], w=["mv%d" % sl])
            ts("dve", rstd, mv[:, 1:2], EPS, None, ALU.add, None, r=["mv%d" % sl], w=["rstd%d" % sl])
            tt("pool", rstd, rstd, mhalfd, ALU.pow, r=["rstd%d" % sl, "mhalfd"], w=["rstd%d" % sl])
            nb_ = nbias2[sl]
            stt(nb_, mv[:, 0:1], -1.0, rstd, ALU.mult, ALU.mult, r=["mv%d" % sl, "rstd%d" % sl], w=["nb%d" % sl])
            pr.op("act", lambda g, sl=sl, rstd=rstd, nb_=nb_: g.activation(out=zt[sl], in_=zt[sl], func=AF.Identity,
                                                                        bias=nb_[:, 0:1], scale=rstd[:, 0:1]),
                  r=["zt%d" % sl, "nb%d" % sl, "rstd%d" % sl], w=["zt%d" % sl])
            tt("dve", zt[sl], zt[sl], lng_t, ALU.mult, r=["zt%d" % sl, "lng"], w=["zt%d" % sl])
            tt("pool", xo[sl], zt[sl], lnb_t, ALU.add, r=["zt%d" % sl, "lnb"], w=["xo%d" % sl])
            dma(y[qt_ * 128:(qt_ + 1) * 128, :], xo[sl], "d_yo%d" % sl, r=["xo%d" % sl])

        for qt_ in range(NBO):
            d_pe(qt_)
            if qt_ > 0:
                d_chain(qt_ - 1)
        d_chain(NBO - 1)
        pr.barrier()

        dma_keys = sorted(pr.dma_cnt.keys())
        dma_sems = {k: stack.enter_context(nc.semaphore("dq_%d" % i)) for i, k in enumerate(dma_keys)}
        block = stack.enter_context(nc.Block())
        pr.emit(nc, block, None, sems, dma_sems)
    return nc


def _rope_tables(seq):
    t = np.arange(seq)
    pos = np.stack([(t // 64).astype(np.float32), (t % 64).astype(np.float32)], -1)
    inv = (np.float32(10000.0) ** (-np.arange(16, dtype=np.float32) / np.float32(16))).astype(np.float32)
    ang = (pos[:, :, None] * inv).astype(np.float32)
    ang = np.stack([ang, ang], axis=2).reshape(seq, 64)
    return np.cos(ang).astype(np.float32), np.sin(ang).astype(np.float32)


def _const_mats():
    s = np.arange(128)
    ident = np.eye(128, dtype=np.float32)
    rperm = np.zeros((128, 128), np.float32)
    for dst in range(128):
        half = (dst % 32) // 16
        if half == 0:
            rperm[dst + 16, dst] = -1.0
        else:
            rperm[dst - 16, dst] = 1.0
    Lf = (s[:, None] <= s[None, :]).astype(np.float32)
    Lb = (s[:, None] >= s[None, :]).astype(np.float32)
    return np.stack([ident, rperm, Lf, Lb, Lf, Lb], axis=1).astype(np.float32)


_NC_CACHE = {}


def make_in_maps(x, c, ctx, c_ctx, w_ada, b_ada, w_in, w_out, diff_lambda, diff_subln_gain,
                 hgrn_lower_bound, hgrn_norm_gain, ln_gain, ln_bias):
    B, SEQ, _ = x.shape
    NQ = SEQ // 4
    f = lambda a: np.ascontiguousarray(np.asarray(a, dtype=np.float32))
    x, c, ctx, c_ctx = f(x), f(c), f(ctx), f(c_ctx)
    w_in0 = f(w_in)[0]
    cos, sin = _rope_tables(SEQ)
    cmat = _const_mats()
    lbr = f(hgrn_lower_bound)
    wzf = w_in0[:, WCOL["hff"]:WCOL["hff"] + 512]
    wzb = w_in0[:, WCOL["hfb"]:WCOL["hfb"] + 512]
    maps = []
    for core in range(8):
        b, j = core // 4, core % 4
        others = [q for q in range(4) if q != j]
        left = [q for q in others if q < j]
        right = sorted([q for q in others if q > j], reverse=True)
        slots = left + right
        idx = []
        slot_is_f = []
        for q in slots:
            t = np.arange(q * NQ, (q + 1) * NQ)
            if q > j:
                t = t[::-1]
            idx.append(t)
            slot_is_f.append(q < j)
        idx.append(np.arange(j * NQ, (j + 1) * NQ))
        lat_idx = np.concatenate(idx)
        xs = np.concatenate([ctx[b], ctx[b][::-1], x[b][lat_idx]], axis=0)
        cs = cos[lat_idx]
        sn = sin[lat_idx]
        cs2 = np.concatenate([cs, cs], axis=1).T
        sn2 = np.concatenate([sn, sn], axis=1).T
        nst = 4 * NQ // 512
        ropet = np.stack([cs2.reshape(128, nst, 512), sn2.reshape(128, nst, 512)], axis=2)
        ropet = np.ascontiguousarray(ropet.transpose(1, 0, 2, 3))
        segdir = [True, False] + slot_is_f
        wz = np.stack([wzf if d else wzb for d in segdir], axis=0)
        lbseg = np.stack([lbr[0] if d else lbr[1] for d in segdir], axis=0)
        sel = np.zeros((16,), np.float32)
        for bnd in range(4):
            sel[bnd] = 1.0 if bnd == j else 0.0
            sel[4 + bnd] = 0.0 if bnd == j else 1.0
            sel[12 + bnd] = 1.0 if bnd == j else 0.0
        sel[8] = 1.0 if j > 0 else 0.0
        maps.append(dict(
            xs=np.ascontiguousarray(xs), cvec=np.stack([c[b], c_ctx], 0), w_ada=f(w_ada)[0], b_ada=f(b_ada)[0],
            w_in=w_in0, wz=np.ascontiguousarray(wz), lbseg=np.ascontiguousarray(lbseg), lbown=lbr,
            rope=ropet, sel=np.ascontiguousarray(np.tile(sel[None], (128, 1))), cmat=cmat, w_out=f(w_out)[0],
            dlam=f(diff_lambda)[0].reshape(256), subg=f(diff_subln_gain)[0], hng=f(hgrn_norm_gain)[0],
            lng=f(ln_gain)[0], lnb=f(ln_bias)[0]))
    return maps, SEQ, NQ


def kernel(x, c, ctx, c_ctx, w_ada, b_ada, w_in, w_out, diff_lambda, diff_subln_gain,
           hgrn_lower_bound, hgrn_norm_gain, ln_gain, ln_bias):
    maps, SEQ, NQ = make_in_maps(x, c, ctx, c_ctx, w_ada, b_ada, w_in, w_out, diff_lambda, diff_subln_gain,
                                 hgrn_lower_bound, hgrn_norm_gain, ln_gain, ln_bias)
    if SEQ not in _NC_CACHE:
        _NC_CACHE[SEQ] = build_program(SEQ)
    nc = _NC_CACHE[SEQ]
    res = run_bass_kernel_spmd(nc, maps, core_ids=list(range(8)))
    B = np.asarray(x).shape[0]
    out = np.zeros((B, SEQ, D), np.float32)
    for core in range(8):
        b, j = core // 4, core % 4
        out[b, j * NQ:(j + 1) * NQ] = np.asarray(res.results[core]["y"], dtype=np.float32)
    return out
```

```python
import math
import numpy as np
import concourse.bass as bass
import concourse.mybir as mybir
from concourse.bass_utils import run_bass_kernel_spmd

F32 = mybir.dt.float32
BF16 = mybir.dt.bfloat16
U8 = mybir.dt.uint8
AF = mybir.ActivationFunctionType
ALU = mybir.AluOpType
AX = mybir.AxisListType

D = 1024
CTX = 256
NCH = 8
EPS = 1e-5
LAMBDA_INIT = 0.8 - 0.6 * math.exp(0.0)
ALPHA = 2.0 ** 0.25
PREF_C = True
PREF_D = False
WCOL = dict(aq=0, ak=512, av=1024, ag=1536, hq=2048, hi=2560, hff=3072, hfb=3584, hg=4096)


class _Op:
    __slots__ = ("eng", "fn", "idx", "inc", "semval", "dma", "dma_val", "waits")


class Prog:
    ENGS = ("pe", "act", "dve", "pool", "sp")

    def __init__(self):
        self.q = {e: [] for e in self.ENGS}
        self.lw = {}
        self.rd = {}
        self.waited = {e: {} for e in self.ENGS}
        self.dma_cnt = {}
        self.cap = None

    def capture(self, f):
        self.cap = []
        f()
        lst, self.cap = self.cap, None
        return lst

    def replay(self, item):
        if item is not None:
            self.op(item[0], item[1], r=item[2], w=item[3], dma=item[4])

    def mark(self):
        if self.cap is not None:
            self.cap.append(None)

    def op(self, eng, fn, r=(), w=(), dma=None, extra=()):
        if self.cap is not None:
            self.cap.append((eng, fn, tuple(r), tuple(w), dma))
            return None
        o = _Op()
        o.eng, o.fn, o.idx, o.inc, o.dma, o.semval, o.dma_val = eng, fn, len(self.q[eng]), False, dma, 0, 0
        deps = list(extra)
        for k in r:
            p = self.lw.get(k)
            if p is not None:
                deps.append(p)
        for k in w:
            p = self.lw.get(k)
            if p is not None:
                deps.append(p)
            deps.extend(self.rd.get(k, ()))
        best = {}
        wd = self.waited[eng]
        for p in deps:
            if p.dma is not None:
                key = ("dma", p.dma)
                if wd.get(key, 0) < p.dma_val and best.get(key, 0) < p.dma_val:
                    best[key] = p.dma_val
            else:
                if p.eng == eng and eng in ("pe", "sp"):
                    continue
                key = p.eng
                if wd.get(key, -1) < p.idx and (key not in best or best[key].idx < p.idx):
                    best[key] = p
        waits = []
        for key, v in best.items():
            if isinstance(key, tuple):
                wd[key] = v
                waits.append(("dma", key[1], v))
            else:
                wd[key] = v.idx
                v.inc = True
                waits.append(("eng", v))
        o.waits = waits
        if dma is not None:
            self.dma_cnt[dma] = self.dma_cnt.get(dma, 0) + 16
            o.dma_val = self.dma_cnt[dma]
        self.q[eng].append(o)
        for k in w:
            self.lw[k] = o
            self.rd[k] = []
        for k in r:
            self.rd.setdefault(k, []).append(o)
        return o

    def barrier(self):
        lasts = []
        for e in self.ENGS:
            if self.q[e]:
                lasts.append(self.q[e][-1])
        dmas = {}
        for e in self.ENGS:
            for o in self.q[e]:
                if o.dma is not None:
                    dmas[o.dma] = o
        for e in self.ENGS:
            self.op(e, lambda g: g.nop(), extra=lasts + list(dmas.values()))
        self.lw.clear()
        self.rd.clear()

    def emit(self, nc, block, engs, sems, dma_sems):
        for e in self.ENGS:
            c = 0
            for o in self.q[e]:
                if o.dma is None and o.inc:
                    c += 1
                    o.semval = c

        def run(e):
            def body(g):
                for o in self.q[e]:
                    for w in o.waits:
                        if w[0] == "dma":
                            g.wait_ge(dma_sems[w[1]], w[2])
                        else:
                            g.wait_ge(sems[w[1].eng], w[1].semval)
                    ins = o.fn(g)
                    if o.dma is not None:
                        ins.then_inc(dma_sems[o.dma], 16)
                    elif o.inc:
                        ins.then_inc(sems[e], 1)
            return body

        block.tensor(run("pe"))
        block.scalar(run("act"))
        block.vector(run("dve"))
        block.gpsimd(run("pool"))
        block.sync(run("sp"))


def build_program(SEQ):
    NQ = SEQ // 4
    NBO = NQ // 128
    NSTO = NQ // 512
    TALL = 2 * CTX + 4 * NQ
    NK = CTX + 4 * NQ
    NKT = NK // 128
    sts = []
    sts.append(dict(row=0, n=CTX, seg=0, kv=True, lat=False, key0=0))
    sts.append(dict(row=CTX, n=CTX, seg=1, kv=False, lat=False, key0=None))
    row, key, li = 2 * CTX, CTX, 0
    for seg in (2, 3, 4, 5):
        for _ in range(NSTO):
            sts.append(dict(row=row, n=512, seg=seg, kv=True, lat=True, key0=key, li=li))
            row += 512
            key += 512
            li += 1
    NST = len(sts)
    NLST = li

    nc = bass.Bass("TRN2", target_bir_lowering=False)

    def din(name, shape, dt=F32):
        return nc.dram_tensor(name, list(shape), dt, kind="ExternalInput").ap()

    xs = din("xs", [TALL, D])
    cvec = din("cvec", [2, D])
    w_ada = din("w_ada", [D, 3 * D])
    b_ada = din("b_ada", [3 * D])
    w_in = din("w_in", [D, 4608])
    wz = din("wz", [5, D, 512])
    lbseg = din("lbseg", [5, 2, 512])
    lbown = din("lbown", [2, 2, 512])
    rope = din("rope", [NLST, 128, 2, 512])
    sel = din("sel", [128, 16])
    cmat = din("cmat", [128, 6, 128])
    w_out = din("w_out", [D, D])
    dlam = din("dlam", [256])
    subg = din("subg", [128])
    hng = din("hng", [128])
    lng = din("lng", [D])
    lnb = din("lnb", [D])
    y = nc.dram_tensor("y", [NQ, D], F32, kind="ExternalOutput").ap()

    def dscr(name, shape, dt):
        return nc.dram_tensor(name, list(shape), dt).ap()

    hT_d = dscr("hT_d", [NST, 128, NCH * 512], BF16)
    kT_d = dscr("kT_d", [4, 128, NK], BF16)
    v_d = dscr("v_d", [4, 128, NKT, 130], BF16)
    q_d = dscr("q_d", [4, 2, 128, NQ], BF16)
    ga_d = dscr("ga_d", [NBO, 128, 512], BF16)
    gh_d = dscr("gh_d", [NBO, 128, 512], BF16)
    qbT_d = dscr("qbT_d", [NBO, 128, 512], BF16)
    incb_d = dscr("incb_d", [NBO, 128, 512], F32)
    op_d = dscr("op_d", [NBO, 128, 512], F32)

    pr = Prog()
    ARENA = 207 * 1024

    import contextlib
    stack = contextlib.ExitStack()
    with stack:
        arena = stack.enter_context(nc.sbuf_tensor("arena", [128, ARENA], U8))
        psum = stack.enter_context(nc.psum_tensor("psum", [128, 8, 512], F32))
        sems = {e: stack.enter_context(nc.semaphore("s_" + e)) for e in Prog.ENGS}

        class Carver:
            def __init__(self, base=0):
                self.off = base

            def take(self, shape, dt):
                nb = int(np.prod(shape)) * (4 if dt == F32 else 2)
                nb = (nb + 63) // 64 * 64
                a = arena[:, self.off:self.off + nb // 1].bitcast(dt)
                n = int(np.prod(shape))
                a = a[:, 0:n]
                self.off += nb
                assert self.off <= ARENA, f"arena overflow {self.off}"
                if len(shape) == 2:
                    a = a.rearrange("p (a b) -> p a b", a=shape[0])
                elif len(shape) == 3:
                    a = a.rearrange("p (a b c) -> p a b c", a=shape[0], b=shape[1])
                return a

        def bank(i):
            return psum[:, i, :]

        def bank_bf(i):
            return psum[:, i, :].bitcast(BF16)

        cv = Carver(0)
        cm = cv.take([6 * 128], F32)
        cm = cm
        identb = cv.take([128], BF16)
        rpermb = cv.take([128], BF16)
        maskf = cv.take([512], BF16)
        maskb = cv.take([512], BF16)
        ones_f = cv.take([4], F32)
        selt = cv.take([16], F32)
        sc = cv.take([2, 8], F32)
        sh_t = cv.take([2, 8], F32)
        sc1_t = cv.take([2, 8], F32)
        gate_t = cv.take([D], F32)
        gainA = cv.take([512], F32)
        gainH = cv.take([512], F32)
        lam_t = cv.take([4], F32)
        lbt_own = cv.take([2, 512], F32)
        oml_own = cv.take([2, 512], F32)
        S_cF = cv.take([512], F32)
        S_cB = cv.take([512], F32)
        S_cur = cv.take([512], F32)
        SF_fin = cv.take([512], F32)
        SB_fin = cv.take([512], F32)
        Db_all = cv.take([NBO, 4], F32)
        Y = cv.take([NBO, D], BF16)
        PBASE = cv.off

        identf = cm[:, 0:128]
        Lf = cm[:, 256:384]
        Lb = cm[:, 384:512]

        def dma(out, in_, key, r=(), w=(), slow=False):
            if slow:
                return pr.op("sp", lambda g: g.dma_start(out=out, in_=in_, allow_slow_non_contiguous=True),
                             r=r, w=w, dma=key)
            return pr.op("sp", lambda g: g.dma_start(out=out, in_=in_), r=r, w=w, dma=key)

        def act(out, in_, func, r, w, scale=1.0, bias=0.0):
            return pr.op("act", lambda g: g.activation(out=out, in_=in_, func=func, bias=bias, scale=scale), r=r, w=w)

        def mm(out, lhsT, rhs, start, stop, r, w):
            return pr.op("pe", lambda g: g.matmul(out, lhsT, rhs, start=start, stop=stop), r=r, w=w)

        def tr(out, in_, ident, r, w):
            return pr.op("pe", lambda g: g.transpose(out, in_, ident), r=r, w=w)

        def ts(eng, out, in0, s1, s2, op0, op1, r, w):
            if s2 is None:
                return pr.op(eng, lambda g: g.tensor_single_scalar(out, in0, s1, op0), r=r, w=w)
            return pr.op(eng, lambda g: g.tensor_scalar(out, in0, s1, s2, op0, op1), r=r, w=w)

        def tt(eng, out, in0, in1, op, r, w):
            return pr.op(eng, lambda g: g.tensor_tensor(out, in0, in1, op), r=r, w=w)

        def stt(out, in0, scalar, in1, op0, op1, r, w):
            return pr.op("dve", lambda g: g.scalar_tensor_tensor(out, in0, scalar, in1, op0, op1), r=r, w=w)

        def cp(eng, out, in_, r, w):
            if eng == "act":
                return pr.op("act", lambda g: g.copy(out, in_), r=r, w=w)
            return pr.op(eng, lambda g: g.tensor_copy(out, in_), r=r, w=w)

        def recip1p(buf, key):
            act(buf, buf, AF.Ln, r=[key], w=[key], bias=1.0)
            act(buf, buf, AF.Exp, r=[key], w=[key], scale=-1.0)

        def rsqrt_small(buf, key):
            act(buf, buf, AF.Ln, r=[key], w=[key])
            act(buf, buf, AF.Exp, r=[key], w=[key], scale=-0.5)

        def memset(eng, ap, val, w):
            return pr.op(eng, lambda g: g.memset(ap, val), w=w)

        c0 = Carver(PBASE)
        wst = c0.take([8, 512], F32)
        wst2 = c0.take([8, 512], F32)
        cvt = c0.take([2, 8], F32)
        bada_t = c0.take([24], F32)
        tmpa = c0.take([2, 8], F32)
        scb = c0.take([8, 128], F32)
        modT = c0.take([16, 2], F32)
        bgate = c0.take([D], F32)
        lamraw = c0.take([256], F32)
        lamtmp = c0.take([256], F32)
        lbraw = c0.take([2, 2, 512], F32)
        g128 = c0.take([2, 128], F32)

        dma(cm, cmat.rearrange("p a b -> p (a b)"), "c_cm", w=["cm"])
        dma(selt, sel, "c_sel", w=["selt"])
        dma(cvt, cvec.rearrange("w (c p) -> p w c", p=128), "c_cv", w=["cvt"], slow=True)
        dma(bada_t, b_ada.rearrange("(c p) -> p c", p=128), "c_ba", w=["bada"], slow=True)
        dma(bgate, b_ada[2 * D:3 * D].partition_broadcast(128), "c_bg", w=["bgate"])
        dma(lamraw, dlam.partition_broadcast(128), "c_lam", w=["lamraw"])
        dma(lbraw, lbown.rearrange("a b c -> (a b c)").partition_broadcast(128).rearrange("p (a b c) -> p a b c", a=2, b=2),
            "c_lbo", w=["lbraw"])
        dma(g128[:, 0, :], subg.partition_broadcast(128), "c_g1", w=["g128a"])
        dma(g128[:, 1, :], hng.partition_broadcast(128), "c_g2", w=["g128b"])

        cp("dve", identb, cm[:, 0:128], r=["cm"], w=["identb"])
        cp("dve", rpermb, cm[:, 128:256], r=["cm"], w=["rpermb"])
        for hh in range(4):
            cp("dve", maskf[:, hh * 128:(hh + 1) * 128], cm[:, 512:640], r=["cm"], w=["maskf"])
            cp("dve", maskb[:, hh * 128:(hh + 1) * 128], cm[:, 640:768], r=["cm"], w=["maskb"])
            ts("dve", gainA[:, hh * 128:(hh + 1) * 128], g128[:, 0, :], 1.0 - LAMBDA_INIT, None, ALU.mult, None,
               r=["g128a"], w=["gainA"])
            cp("dve", gainH[:, hh * 128:(hh + 1) * 128], g128[:, 1, :], r=["g128b"], w=["gainH"])
        memset("dve", ones_f, 1.0, w=["ones"])
        tt("dve", lamtmp[:, 0:64], lamraw[:, 0:64], lamraw[:, 64:128], ALU.mult, r=["lamraw"], w=["lamtmp"])
        tt("dve", lamtmp[:, 64:128], lamraw[:, 128:192], lamraw[:, 192:256], ALU.mult, r=["lamraw"], w=["lamtmp"])
        pr.op("dve", lambda g: g.tensor_reduce(lam_t[:, 0:2], lamtmp[:, 0:128].rearrange("p (a b) -> p a b", a=2),
                                               AX.X, ALU.add), r=["lamtmp"], w=["lam"])
        act(lam_t[:, 0:2], lam_t[:, 0:2], AF.Exp, r=["lam"], w=["lam"])
        tt("dve", lam_t[:, 2:3], lam_t[:, 0:1], lam_t[:, 1:2], ALU.subtract, r=["lam"], w=["lam"])
        ts("dve", lam_t[:, 3:4], lam_t[:, 2:3], LAMBDA_INIT, -1.0, ALU.add, ALU.mult, r=["lam"], w=["lam"])
        for d_ in range(2):
            tt("dve", lbt_own[:, d_, :], lbraw[:, d_, 1, :], lbraw[:, d_, 0, :], ALU.subtract, r=["lbraw"], w=["lbo"])
        act(lbt_own, lbt_own, AF.Exp, r=["lbo"], w=["lbo"])
        recip1p(lbt_own, "lbo")
        ts("dve", oml_own, lbt_own, -1.0, 1.0, ALU.mult, ALU.add, r=["lbo"], w=["omo"])
        act(tmpa, cvt, AF.Exp, r=["cvt"], w=["tmpa"], scale=-1.0)
        recip1p(tmpa, "tmpa")
        tt("dve", sc, cvt, tmpa, ALU.mult, r=["cvt", "tmpa"], w=["sc"])
        for k in range(8):
            cp("dve", scb[:, k, :], sc[:, 0, k:k + 1].to_broadcast([128, 128]), r=["sc"], w=["scb"])
        wsts = [wst, wst2]
        for piece in range(6):
            wb = wsts[piece % 2]
            key = "wst%d" % (piece % 2)
            dma(wb, w_ada[:, piece * 512:(piece + 1) * 512].rearrange("(c p) n -> p c n", p=128), "d_" + key, w=[key])
            if piece < 4:
                for jj in range(4):
                    cc = piece * 4 + jj
                    for k in range(8):
                        mm(bank(0)[:, cc * 2:cc * 2 + 2], wb[:, k, jj * 128:(jj + 1) * 128], sc[:, :, k],
                           k == 0, k == 7, r=[key, "sc"], w=["b0"])
            else:
                hb = piece - 4
                for k in range(8):
                    mm(bank(1 + hb), scb[:, k, :], wb[:, k, :], k == 0, k == 7, r=[key, "scb"], w=["b%d" % (1 + hb)])
        cp("dve", modT, bank(0)[:, 0:32].rearrange("p (a b) -> p a b", a=16), r=["b0"], w=["modT"])
        for wch in range(2):
            tt("dve", sh_t[:, wch, :], modT[:, 0:8, wch], bada_t[:, 0:8], ALU.add, r=["modT", "bada"], w=["sh"])
            tt("dve", sc1_t[:, wch, :], modT[:, 8:16, wch], bada_t[:, 8:16], ALU.add, r=["modT", "bada"], w=["sc1"])
        ts("dve", sc1_t, sc1_t, 1.0, None, ALU.add, None, r=["sc1"], w=["sc1"])
        for hb in range(2):
            tt("dve", gate_t[:, hb * 512:(hb + 1) * 512], bank(1 + hb), bgate[:, hb * 512:(hb + 1) * 512], ALU.add,
               r=["b%d" % (1 + hb), "bgate"], w=["gate"])
        pr.barrier()

        ca = Carver(PBASE + 16 * 1024 + 4 * 8 * 1024)
        xt = [ca.take([D], F32) for _ in range(4)]
        hTo = [ca.take([8, 512], BF16) for _ in range(2)]
        for b_ in range(2):
            memset("pool", hTo[b_], 0.0, w=["hTo%d_%d" % (b_, k) for k in range(8)])
        cbw = Carver(PBASE)
        stgB = cbw.take([8, 512], F32)
        WB = [cbw.take([8, 512], BF16) for _ in range(4)]
        wci = [0]

        def load_wB(i_):
            nm = ("ak", "av", "aq", "ag")[i_]
            dma(stgB, w_in[:, WCOL[nm]:WCOL[nm] + 512].rearrange("(c p) n -> p c n", p=128), "d_stg0", w=["stg0"])
            cp(("dve", "act")[i_ % 2], WB[i_], stgB, r=["stg0"], w=["W" + nm])

        blkno = 0
        allrows = [st["row"] + j * 128 for st in sts for j in range(st["n"] // 128)]
        issued = [0]

        def xload_upto(n):
            while issued[0] < min(n, len(allrows)):
                i_ = issued[0]
                dma(xt[i_ % 4], xs[allrows[i_]:allrows[i_] + 128, :], "d_xt%d" % (i_ % 4), w=["xt%d" % (i_ % 4)])
                issued[0] += 1

        for si, st in enumerate(sts):
            ho = hTo[si % 2]
            hk = "hTo%d" % (si % 2)
            which = 1 if not st["lat"] else 0
            for j in range(st["n"] // 128):
                xb = xt[blkno % 4]
                xk = "xt%d" % (blkno % 4)
                xload_upto(blkno + 3)
                if blkno in (8, 20, 32, 44):
                    load_wB(wci[0])
                    wci[0] += 1
                for half in range(2):
                    bk = (blkno * 2 + half) % 4
                    for kk_ in range(4):
                        k = half * 4 + kk_
                        tr(bank(bk)[:, kk_ * 128:(kk_ + 1) * 128], xb[:, k * 128:(k + 1) * 128], identf,
                           r=[xk, "cm"], w=["b%d" % bk])
                    for kk_ in range(4):
                        k = half * 4 + kk_
                        eng = "act" if half == 0 else "dve"
                        src = bank(bk)[:, kk_ * 128:(kk_ + 1) * 128]
                        dst = ho[:, k, j * 128:(j + 1) * 128]
                        if eng == "act":
                            pr.op("act", lambda g, dst=dst, src=src, k=k, which=which: g.activation(
                                out=dst, in_=src, func=AF.Identity, bias=sh_t[:, which, k:k + 1],
                                scale=sc1_t[:, which, k:k + 1]), r=["b%d" % bk], w=[hk + "_%d" % k])
                        else:
                            ts("dve", dst, src, sc1_t[:, which, k:k + 1], sh_t[:, which, k:k + 1], ALU.mult, ALU.add,
                               r=["b%d" % bk], w=[hk + "_%d" % k])
                blkno += 1
            dma(hT_d[si], ho.rearrange("p c n -> p (c n)"), "d_" + hk + "o", r=[hk + "_%d" % k for k in range(8)])
        while wci[0] < 4:
            load_wB(wci[0])
            wci[0] += 1
        pr.barrier()

        def load_w(dst_bf, src_ap, ncols, tag, stg, stgk, ci=[0]):
            dma(stg[:, :, 0:ncols], src_ap.rearrange("(c p) n -> p c n", p=128), "d_" + stgk, w=[stgk])
            eng = ("dve", "act")[ci[0] % 2]
            ci[0] += 1
            cp(eng, dst_bf, stg[:, :, 0:ncols], r=[stgk], w=[tag])

        cb = Carver(PBASE)
        stg = [cb.take([8, 512], F32)] * 2
        Wk = cb.take([8, 512], BF16)
        Wv = cb.take([8, 512], BF16)
        Wq = cb.take([8, 512], BF16)
        Wg = cb.take([8, 512], BF16)
        hTi = [cb.take([8, 512], BF16) for _ in range(2)]
        ropet = [cb.take([2, 512], F32) for _ in range(2)]
        kb = [cb.take([512], BF16) for _ in range(4)]
        t1 = [cb.take([512], F32) for _ in range(2)]
        t2 = [cb.take([512], F32) for _ in range(2)]
        kTo = [cb.take([4, 512], BF16) for _ in range(2)]
        vo = [cb.take([4, 4, 130], BF16) for _ in range(2)]
        q0o = [cb.take([4, 512], BF16) for _ in range(2)]
        q1o = [cb.take([4, 512], BF16) for _ in range(2)]
        gu = [cb.take([512], F32) for _ in range(2)]
        gao = [cb.take([512], BF16) for _ in range(2)]

        for b_ in range(2):
            memset("pool", vo[b_], 0.0, w=["vo%d" % b_])
            memset("pool", vo[b_][:, :, :, 128:129], 1.0, w=["vo%d" % b_])
            memset("pool", q0o[b_], 0.0, w=["q0o%d" % b_])
            memset("pool", q1o[b_], 0.0, w=["q1o%d" % b_])

        pb = [0]

        def nbank():
            pb[0] = (pb[0] + 1) % 8
            return pb[0]

        cnt = 0
        ownblk = 0
        for si, st in enumerate(sts):
            if not st["kv"]:
                continue
            N = st["n"]
            hi_ = hTi[cnt % 2]
            hik = "hTi%d" % (cnt % 2)
            rk = None
            if st["lat"]:
                rt = ropet[cnt % 2]
                rk = "rope%d" % (cnt % 2)

            def b1_load(c_, si_):
                st_ = sts[si_]
                dma(hTi[c_ % 2].rearrange("p c n -> p (c n)"), hT_d[si_], "d_hTi%d" % (c_ % 2), w=["hTi%d" % (c_ % 2)])
                if st_["lat"]:
                    dma(ropet[c_ % 2].rearrange("p a n -> p (a n)"), rope[st_["li"]].rearrange("p a n -> p (a n)"),
                        "d_rope%d" % (c_ % 2), w=["rope%d" % (c_ % 2)])
            if cnt == 0:
                b1_load(0, si)
            nxt_ = [i_ for i_ in range(si + 1, NST) if sts[i_]["kv"]]
            if nxt_:
                b1_load(cnt + 1, nxt_[0])
            own = st["seg"] == 5
            ko = kTo[cnt % 2]
            kok = "kTo%d" % (cnt % 2)
            q0, q1 = q0o[cnt % 2], q1o[cnt % 2]
            qk_ = "qo%d" % (cnt % 2)

            def fm_proj(W, Wkey, h):
                bk = nbank()
                bkk = "b%d" % bk
                for k in range(8):
                    mm(bank(bk)[:, 0:N], W[:, k, h * 128:(h + 1) * 128], hi_[:, k, 0:N], k == 0, k == 7,
                       r=[Wkey, hik], w=[bkk])
                return bk, bkk

            def fm_evac(bk, bkk, outs, outkey, h):
                if not st["lat"]:
                    cp("act", outs[0][:, h, 0:N], bank(bk)[:, 0:N], r=[bkk], w=[outkey])
                    return
                cp("act", kb[h][:, 0:N], bank(bk)[:, 0:N], r=[bkk], w=["kb%d" % h])

            def fm_rope(outs, outkey, h):
                if not st["lat"]:
                    return
                sl = h % 2
                kbb, kbk = kb[h], "kb%d" % h
                bk2 = nbank()
                bk2k = "b%d" % bk2
                mm(bank(bk2)[:, 0:N], rpermb, kbb[:, 0:N], True, True, r=["rpermb", kbk], w=[bk2k])
                tt("dve", t1[sl][:, 0:N], kbb[:, 0:N], rt[:, 0, 0:N], ALU.mult, r=[kbk, rk], w=["t1%d" % sl])
                tt("dve", t2[sl][:, 0:N], bank(bk2)[:, 0:N], rt[:, 1, 0:N], ALU.mult, r=[bk2k, rk], w=["t2%d" % sl])
                if len(outs) == 1:
                    tt("pool", outs[0][:, h, 0:N], t1[sl][:, 0:N], t2[sl][:, 0:N], ALU.add,
                       r=["t1%d" % sl, "t2%d" % sl], w=[outkey])
                else:
                    tt("pool", outs[0][0:64, h, 0:N], t1[sl][0:64, 0:N], t2[sl][0:64, 0:N], ALU.add,
                       r=["t1%d" % sl, "t2%d" % sl], w=[outkey])
                    tt("pool", outs[1][64:128, h, 0:N], t1[sl][64:128, 0:N], t2[sl][64:128, 0:N], ALU.add,
                       r=["t1%d" % sl, "t2%d" % sl], w=[outkey])

            kbs = [fm_proj(Wk, "Wak", h) for h in range(4)]
            for h in range(4):
                fm_evac(kbs[h][0], kbs[h][1], [ko], kok, h)
            vb = vo[cnt % 2]
            vk = "vo%d" % (cnt % 2)
            nsub = N // 128
            for j in range(nsub):
                bk = nbank()
                bkk = "b%d" % bk
                for k in range(8):
                    mm(bank(bk), hi_[:, k, j * 128:(j + 1) * 128], Wv[:, k, :], k == 0, k == 7, r=["Wav", hik], w=[bkk])
                cp("act", vb[:, j, :, 0:128], bank(bk).rearrange("p (h e) -> p h e", h=4), r=[bkk], w=[vk])
            for h in range(4):
                fm_rope([ko], kok, h)
            dma(kT_d[:, :, st["key0"]:st["key0"] + N].rearrange("h p n -> p h n"), ko[:, :, 0:N], "d_" + kok, r=[kok])
            kt0 = st["key0"] // 128
            for h in range(4):
                dma(v_d[h, :, kt0:kt0 + nsub, :], vb[:, 0:nsub, h, :], "d_%s_%d" % (vk, h), r=[vk])
            if own:
                qbs = [fm_proj(Wq, "Waq", h) for h in range(4)]
                for h in range(4):
                    fm_evac(qbs[h][0], qbs[h][1], [q0, q1], qk_, h)
                for j in range(nsub):
                    bk = nbank()
                    bkk = "b%d" % bk
                    for k in range(8):
                        mm(bank(bk), hi_[:, k, j * 128:(j + 1) * 128], Wg[:, k, :], k == 0, k == 7,
                           r=["Wag", hik], w=[bkk])
                    sl = j % 2
                    act(gu[sl], bank(bk), AF.Exp, r=[bkk], w=["gu%d" % sl], scale=-1.0)
                    recip1p(gu[sl], "gu%d" % sl)
                    tt("dve", gu[sl], bank(bk), gu[sl], ALU.mult, r=[bkk, "gu%d" % sl], w=["gu%d" % sl])
                    tt("pool", gao[sl], gu[sl], gainA, ALU.mult, r=["gu%d" % sl, "gainA"], w=["gao%d" % sl])
                    dma(ga_d[ownblk + j], gao[sl], "d_gao%d" % sl, r=["gao%d" % sl])
                for h in range(4):
                    fm_rope([q0, q1], qk_, h)
                tok0 = ownblk * 128
                dma(q_d[:, 0, :, tok0:tok0 + N].rearrange("h p n -> p h n"), q0, "d_q0%d" % (cnt % 2), r=[qk_])
                dma(q_d[:, 1, :, tok0:tok0 + N].rearrange("h p n -> p h n"), q1, "d_q1%d" % (cnt % 2), r=[qk_])
                ownblk += nsub
            cnt += 1
        pr.barrier()

        ch = Carver(PBASE)
        stg = [ch.take([8, 512], F32)] * 2
        Whi = ch.take([8, 512], BF16)
        Wz1 = ch.take([8, 512], BF16)
        hTi = [ch.take([8, 512], BF16) for _ in range(2)]
        Lfb = ch.take([128], BF16)
        Lbb = ch.take([128], BF16)
        ones_b = ch.take([4], BF16)
        NS = 3
        U2 = [ch.take([2, 512], F32) for _ in range(2)]
        G2 = [ch.take([2, 512], F32) for _ in range(2)]
        GH2 = [ch.take([2, 512], BF16) for _ in range(2)]
        GL2 = [ch.take([2, 512], BF16) for _ in range(2)]
        KT2 = [ch.take([2, 512], BF16) for _ in range(2)]
        QT2 = [ch.take([2, 512], BF16) for _ in range(2)]
        D2 = [ch.take([2, 4], F32) for _ in range(NS)]
        def slot_at(base):
            c5 = Carver(base)
            return [c5.take([2, 512], F32), c5.take([2, 512], F32), c5.take([2, 512], BF16), c5.take([2, 512], BF16),
                    c5.take([2, 512], BF16), c5.take([2, 512], BF16)], c5.off
        slot2_b3, e5 = slot_at(PBASE)
        assert e5 <= PBASE + 8 * 512 * 4

        def set_slot2(bufs):
            for lst, ap_ in zip((U2, G2, GH2, GL2, KT2, QT2), bufs):
                if len(lst) == 2:
                    lst.append(ap_)
                else:
                    lst[2] = ap_
        hib2 = [ch.take([2, 512], BF16) for _ in range(3)]
        ss = ch.take([4], F32)
        c2 = Carver(ch.off + 3 * 8 * 512 * 2)
        lbr = c2.take([2, 512], F32)
        lbt_s = c2.take([512], F32)
        oml_s = c2.take([512], F32)
        lbt2 = c2.take([2, 512], F32)
        oml2 = c2.take([2, 512], F32)
        c3 = Carver(ch.off)
        slot2_b2, e6 = slot_at(ch.off)
        Wz2 = c3.take([8, 512], BF16)
        Whq = c3.take([8, 512], BF16)
        assert e6 <= c3.off
        Whg = c3.take([8, 512], BF16)
        set_slot2(slot2_b2)
        qf = [c3.take([512], F32) for _ in range(3)]
        ghu = c3.take([512], F32)
        gho = [c3.take([512], BF16) for _ in range(2)]
        XT = [c3.take([512], BF16) for _ in range(4)]
        scm = [c3.take([512], BF16) for _ in range(2)]
        Sbf = c3.take([512], BF16)
        Rt = c3.take([512], F32)
        opt = [c3.take([512], F32) for _ in range(2)]
        incb = [c3.take([512], F32) for _ in range(2)]
        HEND = max(c2.off, c3.off)

        cp("dve", Lfb, cm[:, 256:384], r=[], w=["Lfb"])
        cp("dve", Lbb, cm[:, 384:512], r=[], w=["Lbb"])
        memset("dve", ones_b, 1.0, w=["ones_b"])

        def npair():
            nxt = ((pb[0] + 2) // 2 * 2) % 8
            pb[0] = nxt + 1
            return nxt // 2

        def pair_ap(p):
            return psum[:, 2 * p:2 * p + 2, :]

        def pkeys(p):
            return ["b%d" % (2 * p), "b%d" % (2 * p + 1)]

        zc = [0]

        def zchain2_a(p):
            s = zc[0] % NS
            zc[0] += 1
            act(U2[s], pair_ap(p), AF.Exp, r=pkeys(p), w=["U%d" % s], scale=-1.0)
            return s

        def zchain2_b(s, Lmats, lbt, oml, lbkeys, want_q, qsrc=None, qkey=None, fixed=None):
            k_ = lambda n: "%s%d" % (n, s)
            recip1p(U2[s], k_("U"))
            tt("dve", U2[s], U2[s], oml, ALU.mult, r=[k_("U")] + lbkeys, w=[k_("U")])
            tt("dve", U2[s], U2[s], lbt, ALU.add, r=[k_("U")] + lbkeys, w=[k_("U")])
            act(G2[s], U2[s], AF.Ln, r=[k_("U")], w=[k_("G")])
            ts("pool", U2[s], U2[s], -1.0, 1.0, ALU.mult, ALU.add, r=[k_("U")], w=[k_("U")])
            cp("act", GH2[s], G2[s], r=[k_("G")], w=[k_("GH")])
            tt("dve", GL2[s], G2[s], GH2[s], ALU.subtract, r=[k_("G"), k_("GH")], w=[k_("GL")])
            pr.mark()
            p = npair() if fixed is None else fixed[0]
            for l in range(2):
                bkk = "b%d" % (2 * p + l)
                mm(bank(2 * p + l), Lmats[l][0], GH2[s][:, l, :], True, False, r=[Lmats[l][1], k_("GH")], w=[bkk])
                mm(bank(2 * p + l), Lmats[l][0], GL2[s][:, l, :], False, True, r=[Lmats[l][1], k_("GL")], w=[bkk])
            bd = nbank() if fixed is None else fixed[1]
            bdk = "b%d" % bd
            for l in range(2):
                for h in range(4):
                    hs = slice(h * 128, (h + 1) * 128)
                    c = l * 4 + h
                    mm(bank(bd)[:, c:c + 1], GH2[s][:, l, hs], ones_b[:, 0:1], True, False, r=[k_("GH"), "ones_b"], w=[bdk])
                    mm(bank(bd)[:, c:c + 1], GL2[s][:, l, hs], ones_b[:, 0:1], False, True, r=[k_("GL"), "ones_b"], w=[bdk])
            act(G2[s], pair_ap(p), AF.Exp, r=pkeys(p) + [k_("GL")], w=[k_("G")], scale=-1.0)
            tt("dve", KT2[s], U2[s], G2[s], ALU.mult, r=[k_("U"), k_("G")], w=[k_("KT")])
            if want_q:
                act(G2[s], pair_ap(p), AF.Exp, r=pkeys(p), w=[k_("G")])
                for l in range(2):
                    tt("dve", QT2[s][:, l, :], qsrc, G2[s][:, l, :], ALU.mult, r=[qkey, k_("G")], w=[k_("QT")])
            act(D2[s].rearrange("p a b -> p (a b)"), bank(bd)[:, 0:8], AF.Exp, r=[bdk], w=[k_("D")])

        def tokmajor_to(bk, W, Wkey, hi_, hik, j):
            bkk = "b%d" % bk
            for k in range(8):
                mm(bank(bk), hi_[:, k, j * 128:(j + 1) * 128], W[:, k, :], k == 0, k == 7, r=[Wkey, hik], w=[bkk])
            return bk, bkk

        def tokmajor(W, Wkey, hi_, hik, j):
            return tokmajor_to(nbank(), W, Wkey, hi_, hik, j)

        def inc_mm_to(bk, ktile, ktkey, hb_, hbk):
            bkk = "b%d" % bk
            for h in range(4):
                mm(bank(bk)[:, h * 128:(h + 1) * 128], ktile[:, h * 128:(h + 1) * 128], hb_[:, h * 128:(h + 1) * 128],
                   True, True, r=[ktkey, hbk], w=[bkk])
            return bk, bkk

        def rstep(R, Rkey, Dprev, Dkey, incsrc, inckey):
            for h in range(4):
                hs = slice(h * 128, (h + 1) * 128)
                stt(R[:, hs], R[:, hs], Dprev[:, h:h + 1], incsrc[:, hs], ALU.mult, ALU.add,
                    r=[Rkey, Dkey, inckey], w=[Rkey])

        def state_update(S, Skey, incsrc, inckey, Dtile, Dkey):
            tt("dve", Rt, S, incsrc, ALU.add, r=[Skey, inckey], w=["Rt"])
            for h in range(4):
                ts("dve", S[:, h * 128:(h + 1) * 128], Rt[:, h * 128:(h + 1) * 128],
                   Dtile[:, h:h + 1], None, ALU.mult, None, r=["Rt", Dkey], w=[Skey])

        load_w(Whi, w_in[:, WCOL["hi"]:WCOL["hi"] + 512], 512, "Whi", stg[0], "stg0")
        for S_ in (S_cF, S_cB, S_cur, SF_fin, SB_fin):
            memset("pool", S_, 0.0, w=["Sx"])
        pr.barrier()
        segs = {}
        for si, st in enumerate(sts):
            segs.setdefault(st["seg"], []).append(si)
        hcnt = [0]
        pend = [None]
        hlist = [si for seg_ in range(6) for si in segs[seg_]]
        hc = [0]

        def h_load(c_, si_):
            dma(hTi[c_ % 2].rearrange("p c n -> p (c n)"), hT_d[si_], "d_hTi%d" % (c_ % 2), w=["hTi%d" % (c_ % 2)])

        hist = []

        def split(lst):
            i_ = lst.index(None)
            return lst[:i_], lst[i_ + 1:]

        def interleave(a_, b_):
            for i_ in range(max(len(a_), len(b_))):
                if i_ < len(a_):
                    pr.replay(a_[i_])
                if i_ < len(b_):
                    pr.replay(b_[i_])

        def pipe_step(new_q):
            a_ = hist[-1][0] if len(hist) >= 1 else []
            b_ = hist[-2][1] if len(hist) >= 2 else []
            interleave(a_, b_)
            if new_q is not None:
                hist.append(split(pr.capture(new_q)))

        def flush():
            if len(hist) >= 1:
                pipe_step(None)
                interleave([], hist[-1][1])
            del hist[:]

        Wzs = [(Wz1, "Wz1"), (Whg, "WzB")]
        load_w(Wz1, wz[0], 512, "Wz1", stg[0], "stg0")
        for seg in range(5):
            flush()
            Wz_, Wzk = Wzs[seg % 2]
            if seg + 1 < 5:
                load_w(Wzs[(seg + 1) % 2][0], wz[seg + 1], 512, Wzs[(seg + 1) % 2][1], stg[0], "stg0")
            dma(lbr.rearrange("p a n -> p (a n)"), lbseg[seg].rearrange("a n -> (a n)").partition_broadcast(128),
                "d_lbr", w=["lbr"])
            tt("dve", lbt_s, lbr[:, 1, :], lbr[:, 0, :], ALU.subtract, r=["lbr"], w=["lbt_s"])
            act(lbt_s, lbt_s, AF.Exp, r=["lbt_s"], w=["lbt_s"])
            recip1p(lbt_s, "lbt_s")
            ts("dve", oml_s, lbt_s, -1.0, 1.0, ALU.mult, ALU.add, r=["lbt_s"], w=["oml_s"])
            for l in range(2):
                cp("dve", lbt2[:, l, :], lbt_s, r=["lbt_s"], w=["lbt2"])
                cp("dve", oml2[:, l, :], oml_s, r=["oml_s"], w=["oml2"])
            if seg == 0:
                S, Sk = S_cF, "S_cF"
            elif seg == 1:
                S, Sk = S_cB, "S_cB"
            else:
                S, Sk = S_cur, "S_cur"
                b = seg - 2
                Xsrc, Xk = (S_cF, "S_cF") if b == 0 else (S_cur, "S_cur")
                stt(SF_fin, Xsrc, selt[:, b:b + 1], SF_fin, ALU.mult, ALU.add, r=[Xk, "SF_fin", "selt"], w=["SF_fin"])
                if b == 0:
                    ts("dve", S_cur, S_cF, selt[:, 8:9], None, ALU.mult, None, r=["S_cF", "selt"], w=["S_cur"])
                else:
                    ts("dve", S_cur, S_cur, selt[:, 4 + b:5 + b], None, ALU.mult, None, r=["S_cur", "selt"], w=["S_cur"])
                stt(S_cur, S_cB, selt[:, 12 + b:13 + b], S_cur, ALU.mult, ALU.add, r=["S_cB", "S_cur", "selt"], w=["S_cur"])
            dstate = [(ones_f, "ones")]
            for si in segs[seg]:
                st = sts[si]
                hi_ = hTi[hc[0] % 2]
                hik = "hTi%d" % (hc[0] % 2)
                if hc[0] == 0:
                    h_load(0, hlist[0])
                if hc[0] + 1 < len(hlist):
                    h_load(hc[0] + 1, hlist[hc[0] + 1])
                hc[0] += 1
                for jp in range(st["n"] // 256):
                    for l in range(2):
                        tokmajor_to(l, Whi, "Whi", hi_, hik, 2 * jp + l)
                    for l in range(2):
                        tokmajor_to(2 + l, Wz_, Wzk, hi_, hik, 2 * jp + l)
                    hsl = hcnt[0] % 3
                    hcnt[0] += 1
                    hb_ = hib2[hsl]
                    hbk = "hib%d" % hsl

                    def PPOST(hb_=hb_, hbk=hbk):
                        cp("act", hb_, pair_ap(0), r=pkeys(0), w=[hbk])
                        return zchain2_a(1)

                    def mkQ(s, hb_=hb_, hbk=hbk, S=S, Sk=Sk):
                        def Q():
                            zchain2_b(s, [(Lfb, "Lfb"), (Lfb, "Lfb")], lbt2, oml2, ["lbt2", "oml2"], False, fixed=(2, 6))
                            for l in range(2):
                                inc_mm_to(4 + l, KT2[s][:, l, :], "KT%d" % s, hb_[:, l, :], hbk)
                            for l in range(2):
                                rstep(S, Sk, dstate[0][0], dstate[0][1], bank(4 + l), "b%d" % (4 + l))
                                dstate[0] = (D2[s][:, l, :], "D%d" % s)
                        return Q
                    a_ = hist[-1][0] if len(hist) >= 1 else []
                    b_ = hist[-2][1] if len(hist) >= 2 else []
                    interleave(a_, b_)
                    s_ = PPOST()
                    hist.append(split(pr.capture(mkQ(s_))))
            flush()
            for h in range(4):
                hs = slice(h * 128, (h + 1) * 128)
                ts("dve", S[:, hs], S[:, hs], dstate[0][0][:, h:h + 1], None, ALU.mult, None, r=[Sk, dstate[0][1]], w=[Sk])
        stt(SF_fin, S_cur, selt[:, 3:4], SF_fin, ALU.mult, ALU.add, r=["S_cur", "SF_fin", "selt"], w=["SF_fin"])
        ts("dve", SB_fin, S_cur, selt[:, 7:8], None, ALU.mult, None, r=["S_cur", "selt"], w=["SB_fin"])
        stt(SB_fin, S_cB, selt[:, 15:16], SB_fin, ALU.mult, ALU.add, r=["S_cB", "SB_fin", "selt"], w=["SB_fin"])
        pr.barrier()

        load_w(Wz1, w_in[:, WCOL["hff"]:WCOL["hff"] + 512], 512, "Wz1", stg[0], "stg0")
        load_w(Wz2, w_in[:, WCOL["hfb"]:WCOL["hfb"] + 512], 512, "Wz2", stg[0], "stg0")
        load_w(Whq, w_in[:, WCOL["hq"]:WCOL["hq"] + 512], 512, "Whq", stg[0], "stg0")
        load_w(Whg, w_in[:, WCOL["hg"]:WCOL["hg"] + 512], 512, "Whg", stg[0], "stg0")
        pr.barrier()
        set_slot2(slot2_b3)
        ob = 0
        dstate = [(ones_f, "ones")]
        for si in segs[5]:
            st = sts[si]
            hi_ = hTi[hc[0] % 2]
            hik = "hTi%d" % (hc[0] % 2)
            if hc[0] + 1 < len(hlist):
                h_load(hc[0] + 1, hlist[hc[0] + 1])
            hc[0] += 1
            for j in range(4):
                sl = ob % 2
                s3 = ob % 3
                tokmajor_to(0, Whi, "Whi", hi_, hik, j)
                tokmajor_to(2, Wz1, "Wz1", hi_, hik, j)
                tokmajor_to(3, Wz2, "Wz2", hi_, hik, j)
                tokmajor_to(1, Whq, "Whq", hi_, hik, j)
                tokmajor_to(4, Whg, "Whg", hi_, hik, j)
                a_ = hist[-1][0] if len(hist) >= 1 else []
                b_ = hist[-2][1] if len(hist) >= 2 else []
                interleave(a_, b_)
                hb_ = hib2[s3][:, 0, :]
                hbk = "hib%d" % s3
                cp("act", hb_, bank(0), r=["b0"], w=[hbk])
                s = zchain2_a(1)
                qk = "qf%d" % s3
                act(qf[s3], bank(1), AF.Exp, r=["b1"], w=[qk], scale=-1.0)
                recip1p(qf[s3], qk)
                tt("dve", qf[s3], bank(1), qf[s3], ALU.mult, r=["b1", qk], w=[qk])
                act(ghu, bank(4), AF.Exp, r=["b4"], w=["ghu"], scale=-1.0)
                recip1p(ghu, "ghu")
                tt("dve", ghu, bank(4), ghu, ALU.mult, r=["b4", "ghu"], w=["ghu"])
                tt("pool", gho[sl], ghu, gainH, ALU.mult, r=["ghu", "gainH"], w=["gho%d" % sl])
                dma(gh_d[ob], gho[sl], "d_gho%d" % sl, r=["gho%d" % sl])

                def mkQ(ob=ob, sl=sl, s3=s3, hb_=hb_, hbk=hbk, s=s, qk=qk):
                    def Q():
                        zchain2_b(s, [(Lfb, "Lfb"), (Lbb, "Lbb")], lbt_own, oml_own, ["lbo", "omo"], True, qf[s3], qk,
                                  fixed=(3, 5))
                        srcs = [(QT2[s][:, 0, :], "QT%d" % s), (KT2[s][:, 0, :], "KT%d" % s),
                                (QT2[s][:, 1, :], "QT%d" % s), (KT2[s][:, 1, :], "KT%d" % s)]
                        for xi, (src, srck) in enumerate(srcs):
                            bk = 5 + xi // 2
                            bkk = "b%d" % bk
                            c0_ = (xi % 2) * 512
                            for h in range(4):
                                tr(bank_bf(bk)[:, c0_ + h * 128:c0_ + (h + 1) * 128], src[:, h * 128:(h + 1) * 128], identb,
                                   r=[srck, "identb"], w=[bkk])
                            cp("act" if bk == 5 else "dve", XT[xi], bank_bf(bk)[:, c0_:c0_ + 512], r=[bkk], w=["XT%d" % xi])
                        for d_, (qi, ki, msk, mk, bk) in enumerate(((0, 1, maskf, "maskf", 7), (2, 3, maskb, "maskb", 5))):
                            bkk = "b%d" % bk
                            for h in range(4):
                                mm(bank(bk)[:, h * 128:(h + 1) * 128], XT[ki][:, h * 128:(h + 1) * 128],
                                   XT[qi][:, h * 128:(h + 1) * 128], True, True, r=["XT%d" % ki, "XT%d" % qi], w=[bkk])
                            tt("dve", scm[d_], bank(bk), msk, ALU.mult, r=[bkk, mk], w=["scm%d" % d_])
                        for h in range(4):
                            hs = slice(h * 128, (h + 1) * 128)
                            ts("dve", Sbf[:, hs], SF_fin[:, hs], dstate[0][0][:, h:h + 1], None, ALU.mult, None,
                               r=["SF_fin", dstate[0][1]], w=["Sbf"])
                        for h in range(4):
                            hs = slice(h * 128, (h + 1) * 128)
                            mm(bank(6)[:, hs], scm[0][:, hs], hb_[:, hs], True, False, r=["scm0", hbk], w=["b6"])
                            mm(bank(6)[:, hs], scm[1][:, hs], hb_[:, hs], False, False, r=["scm1", hbk], w=["b6"])
                            mm(bank(6)[:, hs], XT[0][:, hs], Sbf[:, hs], False, True, r=["XT0", "Sbf"], w=["b6"])
                        cp("act", opt[sl], bank(6), r=["b6"], w=["opt%d" % sl])
                        dma(op_d[ob], opt[sl], "d_opt%d" % sl, r=["opt%d" % sl])
                        dma(qbT_d[ob], XT[2], "d_qbT", r=["XT2"])
                        inc_mm_to(6, KT2[s][:, 0, :], "KT%d" % s, hb_, hbk)
                        inc_mm_to(7, KT2[s][:, 1, :], "KT%d" % s, hb_, hbk)
                        rstep(SF_fin, "SF_fin", dstate[0][0], dstate[0][1], bank(6), "b6")
                        dstate[0] = (D2[s][:, 0, :], "D%d" % s)
                        cp("act", incb[sl], bank(7), r=["b7"], w=["incb%d" % sl])
                        dma(incb_d[ob], incb[sl], "d_incb%d" % sl, r=["incb%d" % sl])
                        cp("pool", Db_all[:, ob, :], D2[s][:, 1, :], r=["D%d" % s], w=["Db_all"])
                    return Q
                hist.append(split(pr.capture(mkQ())))
                ob += 1
        flush()
        pr.barrier()

        cc_ = Carver(PBASE)
        KTt = [cc_.take([NK], BF16) for _ in range(2)]
        Vt = [cc_.take([NKT, 130], BF16) for _ in range(2)]
        Qt = [cc_.take([2, NQ], BF16) for _ in range(2)]
        PT = [cc_.take([2, 512], BF16) for _ in range(3)]
        rr = cc_.take([8], F32)
        accs = cc_.take([8, 130], F32)
        oa4 = cc_.take([4, 128], F32)
        osq4 = cc_.take([4, 128], F32)
        ssa4 = cc_.take([4], F32)
        mhalf = cc_.take([4], F32)
        gal = cc_.take([4, 128], BF16)
        memset("pool", mhalf, -0.5, w=["mhalf"])
        qbl = [cc_.take([512], BF16) for _ in range(2)]
        o2 = [cc_.take([512], F32) for _ in range(2)]
        sq4 = cc_.take([512], F32)
        ghl = [cc_.take([512], BF16) for _ in range(2)]
        opt4 = [cc_.take([512], F32) for _ in range(2)]
        incb4 = [cc_.take([512], F32) for _ in range(2)]
        Sbf4 = cc_.take([512], BF16)
        Rt4 = cc_.take([512], F32)
        ss4 = cc_.take([4], F32)

        def b4_block(ob):
            sl = ob % 2
            dma(qbl[sl], qbT_d[ob], "d_qbl%d" % sl, w=["qbl%d" % sl])
            dma(opt4[sl], op_d[ob], "d_opl%d" % sl, w=["opt4%d" % sl])
            dma(incb4[sl], incb_d[ob], "d_incl%d" % sl, w=["incb4%d" % sl])
            dma(ghl[sl], gh_d[ob], "d_ghl%d" % sl, w=["ghl%d" % sl])
            cp("pool", Sbf4, SB_fin, r=["SB_fin"], w=["Sbf4"])
            for h_ in range(4):
                hs = slice(h_ * 128, (h_ + 1) * 128)
                mm(bank(7)[:, hs], qbl[sl][:, hs], Sbf4[:, hs], True, True, r=["qbl%d" % sl, "Sbf4"], w=["b7"])
            tt("dve", o2[sl], bank(7), opt4[sl], ALU.add, r=["b7", "opt4%d" % sl], w=["o2%d" % sl])
            tt("dve", Rt4, SB_fin, incb4[sl], ALU.add, r=["SB_fin", "incb4%d" % sl], w=["Rt4"])
            for h_ in range(4):
                hs = slice(h_ * 128, (h_ + 1) * 128)
                ts("dve", SB_fin[:, hs], Rt4[:, hs], Db_all[:, ob, h_:h_ + 1], None, ALU.mult, None,
                   r=["Rt4", "Db_all"], w=["SB_fin"])
            tt("pool", sq4, o2[sl], o2[sl], ALU.mult, r=["o2%d" % sl], w=["sq4"])
            pr.op("dve", lambda g: g.tensor_reduce(ss4, sq4.rearrange("p (a b) -> p a b", a=4), AX.X, ALU.add),
                  r=["sq4"], w=["ss4"])
            ts("dve", ss4, ss4, 1.0 / 128.0, EPS, ALU.mult, ALU.add, r=["ss4"], w=["ss4"])
            tt("pool", ss4, ss4, mhalf, ALU.pow, r=["ss4", "mhalf"], w=["ss4"])
            tt("dve", o2[sl].rearrange("p (a b) -> p a b", a=4), o2[sl].rearrange("p (a b) -> p a b", a=4),
               ss4.unsqueeze(2).to_broadcast([128, 4, 128]), ALU.mult, r=["o2%d" % sl, "ss4"], w=["o2%d" % sl])
            tt("pool", Y[:, ob, 512:1024], o2[sl], ghl[sl], ALU.mult, r=["o2%d" % sl, "ghl%d" % sl], w=["Yh"])

        b4_next = [NBO - 1]
        b4_every = max(1, (4 * (NQ // 512) * NKT) // (NBO + 2))
        NQC = NQ // 512
        it = 0
        def acc(m, qs):
            i = m * 4 + qs
            return psum[:, 4 + i // 3, (i % 3) * 130:(i % 3) * 130 + 130]

        def c_load(h_):
            hb_ = h_ % 2
            dma(KTt[hb_], kT_d[h_], "d_KT%d" % hb_, w=["KTt%d" % hb_])
            dma(Vt[hb_].rearrange("p a b -> p (a b)"), v_d[h_].rearrange("p a b -> p (a b)"), "d_V%d" % hb_, w=["Vt%d" % hb_])
            dma(Qt[hb_], q_d[h_].rearrange("m p n -> p m n"), "d_Q%d" % hb_, w=["Qt%d" % hb_])

        for h in range(4):
            hb = h % 2
            if PREF_C:
                if h == 0:
                    c_load(0)
                if h + 1 < 4:
                    c_load(h + 1)
            else:
                c_load(h)
            for qc in range(NQC):
                def qk_exp(kt, it_):
                    sb_ = it_ % 2
                    for m in range(2):
                        ps_ = slice(m * 64, (m + 1) * 64)
                        pr.op("pe", lambda g, o_=bank(sb_ * 2 + m), l_=KTt[hb][ps_, kt * 128:(kt + 1) * 128],
                              r_=Qt[hb][ps_, m, qc * 512:(qc + 1) * 512], tp=(m * 64, 0):
                              g.matmul(o_, l_, r_, start=True, stop=True, tile_position=tp),
                              r=["KTt%d" % hb, "Qt%d" % hb], w=["S%d" % sb_])
                    pt = PT[it_ % 3]
                    ptk = "PT%d" % (it_ % 3)
                    act(pt, psum[:, sb_ * 2:sb_ * 2 + 2, :], AF.Exp, r=["S%d" % sb_], w=[ptk], scale=0.125)

                def pv(kt, it_):
                    pt = PT[it_ % 3]
                    ptk = "PT%d" % (it_ % 3)
                    for m in range(2):
                        for qs in range(4):
                            stf = (kt == 0) and ((m * 4 + qs) in (0, 3, 6))
                            pr.op("pe", lambda g, a=acc(m, qs), l=pt[:, m, qs * 128:(qs + 1) * 128], v=Vt[hb][:, kt, :],
                                  stf=stf, sp_=(kt == NKT - 1): g.matmul(a, l, v, start=stf, stop=sp_, skip_group_check=True),
                                  r=[ptk, "Vt%d" % hb], w=["acc"])

                qk_exp(0, it)
                qk_exp(1, it + 1)
                for kt in range(NKT):
                    if kt + 2 < NKT:
                        qk_exp(kt + 2, it + 2)
                    pv(kt, it)
                    it += 1
                    if it % b4_every == 0 and b4_next[0] >= 0:
                        b4_block(b4_next[0])
                        b4_next[0] -= 1
                for bk_, n_ in ((4, 3), (5, 3), (6, 2)):
                    cp("dve", accs[:, (bk_ - 4) * 3:(bk_ - 4) * 3 + n_, :],
                       psum[:, bk_, 0:n_ * 130].rearrange("p (a b) -> p a b", a=n_), r=["acc"], w=["accs"])
                dma(gal, ga_d[qc * 4:(qc + 1) * 4, :, h * 128:(h + 1) * 128].rearrange("q p e -> p q e"), "d_gal", w=["gal"])
                pr.op("dve", lambda g: g.reciprocal(rr, accs[:, :, 128:129].rearrange("p a b -> p (a b)")), r=["accs"], w=["rr"])
                ts("dve", rr[:, 4:8], rr[:, 4:8], lam_t[:, 3:4], None, ALU.mult, None, r=["rr", "lam"], w=["rr"])
                tt("dve", accs[:, :, 0:128], accs[:, :, 0:128], rr.unsqueeze(2).to_broadcast([128, 8, 128]), ALU.mult,
                   r=["accs", "rr"], w=["accs"])
                tt("dve", oa4, accs[:, 0:4, 0:128], accs[:, 4:8, 0:128], ALU.add, r=["accs"], w=["oa4"])
                tt("pool", osq4, oa4, oa4, ALU.mult, r=["oa4"], w=["osq4"])
                pr.op("dve", lambda g: g.tensor_reduce(ssa4, osq4, AX.X, ALU.add), r=["osq4"], w=["ssa4"])
                ts("dve", ssa4, ssa4, 1.0 / 128.0, EPS, ALU.mult, ALU.add, r=["ssa4"], w=["ssa4"])
                tt("pool", ssa4, ssa4, mhalf, ALU.pow, r=["ssa4", "mhalf"], w=["ssa4"])
                tt("dve", oa4, oa4, ssa4.unsqueeze(2).to_broadcast([128, 4, 128]), ALU.mult, r=["oa4", "ssa4"], w=["oa4"])
                tt("dve", Y[:, qc * 4:(qc + 1) * 4, h * 128:(h + 1) * 128], oa4, gal, ALU.mult, r=["oa4", "gal"], w=["Ya"])
        while b4_next[0] >= 0:
            b4_block(b4_next[0])
            b4_next[0] -= 1
        pr.barrier()

        cd = Carver(PBASE)
        stg = [cd.take([8, 512], F32) for _ in range(2)]
        Wo = cd.take([8, D], BF16)
        YT = [cd.take([8, 128], BF16) for _ in range(2)]
        xo = [cd.take([D], F32) for _ in range(2)]
        zt = [cd.take([D], F32) for _ in range(2)]
        stats2 = [cd.take([2, 6], F32) for _ in range(2)]
        mv2 = [cd.take([2], F32) for _ in range(2)]
        rstd2 = [cd.take([1], F32) for _ in range(2)]
        nbias2 = [cd.take([1], F32) for _ in range(2)]
        mhalfd = cd.take([1], F32)
        memset("pool", mhalfd, -0.5, w=["mhalfd"])
        lng_t = cd.take([D], F32)
        lnb_t = cd.take([D], F32)
        dma(lng_t, lng.partition_broadcast(128), "c_lng", w=["lng"])
        dma(lnb_t, lnb.partition_broadcast(128), "c_lnb", w=["lnb"])
        for hb in range(2):
            load_w(Wo[:, :, hb * 512:(hb + 1) * 512], w_out[:, hb * 512:(hb + 1) * 512], 512, "Wo", stg[hb], "stg%d" % hb)
        own_row0 = 2 * CTX + 3 * NQ
        def d_load(t_):
            dma(xo[t_ % 2], xs[own_row0 + t_ * 128:own_row0 + (t_ + 1) * 128, :], "d_xo%d" % (t_ % 2), w=["xo%d" % (t_ % 2)])

        def d_pe(qt_):
            sl = qt_ % 2
            d_load(qt_)
            bk = 3 * sl
            bkk = "b%d" % bk
            for k in range(8):
                tr(bank_bf(bk)[:, k * 128:(k + 1) * 128], Y[:, qt_, k * 128:(k + 1) * 128], identb, r=["Y", "identb"], w=[bkk])
            cp("act", YT[sl].rearrange("p a b -> p (a b)"), bank_bf(bk), r=[bkk], w=["YT%d" % sl])
            for hb in range(2):
                bko = 3 * sl + 1 + hb
                bkok = "b%d" % bko
                for k in range(8):
                    mm(bank(bko), YT[sl][:, k, :], Wo[:, k, hb * 512:(hb + 1) * 512], k == 0, k == 7,
                       r=["YT%d" % sl, "Wo"], w=[bkok])

        def d_chain(qt_):
            sl = qt_ % 2
            for hb in range(2):
                bko = 3 * sl + 1 + hb
                hs = slice(hb * 512, (hb + 1) * 512)
                tt("dve", zt[sl][:, hs], bank(bko), gate_t[:, hs], ALU.mult, r=["b%d" % bko, "gate"], w=["zt%d" % sl])
            stt(zt[sl], xo[sl], ALPHA, zt[sl], ALU.mult, ALU.add, r=["xo%d" % sl, "zt%d" % sl], w=["zt%d" % sl])
            stats, mv, rstd = stats2[sl], mv2[sl], rstd2[sl]
            for hb in range(2):
                pr.op("dve", lambda g, hb=hb, sl=sl, stats=stats: g.bn_stats(stats[:, hb, :], zt[sl][:, hb * 512:(hb + 1) * 512]),
                      r=["zt%d" % sl], w=["stats%d" % sl])
            pr.op("dve", lambda g, stats=stats, mv=mv: g.bn_aggr(mv, stats.rearrange("p a b -> p (a b)")),
                  r=["stats%d" % sl], w=["mv%d" % sl])
            ts("dve", rstd, mv[:, 1:2], EPS, None, ALU.add, None, r=["mv%d" % sl], w=["rstd%d" % sl])
            tt("pool", rstd, rstd, mhalfd, ALU.pow, r=["rstd%d" % sl, "mhalfd"], w=["rstd%d" % sl])
            nb_ = nbias2[sl]
            stt(nb_, mv[:, 0:1], -1.0, rstd, ALU.mult, ALU.mult, r=["mv%d" % sl, "rstd%d" % sl], w=["nb%d" % sl])
            pr.op("act", lambda g, sl=sl, rstd=rstd, nb_=nb_: g.activation(out=zt[sl], in_=zt[sl], func=AF.Identity,
                                                                        bias=nb_[:, 0:1], scale=rstd[:, 0:1]),
                  r=["zt%d" % sl, "nb%d" % sl, "rstd%d" % sl], w=["zt%d" % sl])
            tt("dve", zt[sl], zt[sl], lng_t, ALU.mult, r=["zt%d" % sl, "lng"], w=["zt%d" % sl])
            tt("pool", xo[sl], zt[sl], lnb_t, ALU.add, r=["zt%d" % sl, "lnb"], w=["xo%d" % sl])
            dma(y[qt_ * 128:(qt_ + 1) * 128, :], xo[sl], "d_yo%d" % sl, r=["xo%d" % sl])

        for qt_ in range(NBO):
            d_pe(qt_)
            if qt_ > 0:
                d_chain(qt_ - 1)
        d_chain(NBO - 1)
        pr.barrier()

        dma_keys = sorted(pr.dma_cnt.keys())
        dma_sems = {k: stack.enter_context(nc.semaphore("dq_%d" % i)) for i, k in enumerate(dma_keys)}
        block = stack.enter_context(nc.Block())
        pr.emit(nc, block, None, sems, dma_sems)
    return nc


def _rope_tables(seq):
    t = np.arange(seq)
    pos = np.stack([(t // 64).astype(np.float32), (t % 64).astype(np.float32)], -1)
    inv = (np.float32(10000.0) ** (-np.arange(16, dtype=np.float32) / np.float32(16))).astype(np.float32)
    ang = (pos[:, :, None] * inv).astype(np.float32)
    ang = np.stack([ang, ang], axis=2).reshape(seq, 64)
    return np.cos(ang).astype(np.float32), np.sin(ang).astype(np.float32)


def _const_mats():
    s = np.arange(128)
    ident = np.eye(128, dtype=np.float32)
    rperm = np.zeros((128, 128), np.float32)
    for dst in range(128):
        half = (dst % 32) // 16
        if half == 0:
            rperm[dst + 16, dst] = -1.0
        else:
            rperm[dst - 16, dst] = 1.0
    Lf = (s[:, None] <= s[None, :]).astype(np.float32)
    Lb = (s[:, None] >= s[None, :]).astype(np.float32)
    return np.stack([ident, rperm, Lf, Lb, Lf, Lb], axis=1).astype(np.float32)


_NC_CACHE = {}


def make_in_maps(x, c, ctx, c_ctx, w_ada, b_ada, w_in, w_out, diff_lambda, diff_subln_gain,
                 hgrn_lower_bound, hgrn_norm_gain, ln_gain, ln_bias):
    B, SEQ, _ = x.shape
    NQ = SEQ // 4
    f = lambda a: np.ascontiguousarray(np.asarray(a, dtype=np.float32))
    x, c, ctx, c_ctx = f(x), f(c), f(ctx), f(c_ctx)
    w_in0 = f(w_in)[0]
    cos, sin = _rope_tables(SEQ)
    cmat = _const_mats()
    lbr = f(hgrn_lower_bound)
    wzf = w_in0[:, WCOL["hff"]:WCOL["hff"] + 512]
    wzb = w_in0[:, WCOL["hfb"]:WCOL["hfb"] + 512]
    maps = []
    for core in range(8):
        b, j = core // 4, core % 4
        others = [q for q in range(4) if q != j]
        left = [q for q in others if q < j]
        right = sorted([q for q in others if q > j], reverse=True)
        slots = left + right
        idx = []
        slot_is_f = []
        for q in slots:
            t = np.arange(q * NQ, (q + 1) * NQ)
            if q > j:
                t = t[::-1]
            idx.append(t)
            slot_is_f.append(q < j)
        idx.append(np.arange(j * NQ, (j + 1) * NQ))
        lat_idx = np.concatenate(idx)
        xs = np.concatenate([ctx[b], ctx[b][::-1], x[b][lat_idx]], axis=0)
        cs = cos[lat_idx]
        sn = sin[lat_idx]
        cs2 = np.concatenate([cs, cs], axis=1).T
        sn2 = np.concatenate([sn, sn], axis=1).T
        nst = 4 * NQ // 512
        ropet = np.stack([cs2.reshape(128, nst, 512), sn2.reshape(128, nst, 512)], axis=2)
        ropet = np.ascontiguousarray(ropet.transpose(1, 0, 2, 3))
        segdir = [True, False] + slot_is_f
        wz = np.stack([wzf if d else wzb for d in segdir], axis=0)
        lbseg = np.stack([lbr[0] if d else lbr[1] for d in segdir], axis=0)
        sel = np.zeros((16,), np.float32)
        for bnd in range(4):
            sel[bnd] = 1.0 if bnd == j else 0.0
            sel[4 + bnd] = 0.0 if bnd == j else 1.0
            sel[12 + bnd] = 1.0 if bnd == j else 0.0
        sel[8] = 1.0 if j > 0 else 0.0
        maps.append(dict(
            xs=np.ascontiguousarray(xs), cvec=np.stack([c[b], c_ctx], 0), w_ada=f(w_ada)[0], b_ada=f(b_ada)[0],
            w_in=w_in0, wz=np.ascontiguousarray(wz), lbseg=np.ascontiguousarray(lbseg), lbown=lbr,
            rope=ropet, sel=np.ascontiguousarray(np.tile(sel[None], (128, 1))), cmat=cmat, w_out=f(w_out)[0],
            dlam=f(diff_lambda)[0].reshape(256), subg=f(diff_subln_gain)[0], hng=f(hgrn_norm_gain)[0],
            lng=f(ln_gain)[0], lnb=f(ln_bias)[0]))
    return maps, SEQ, NQ


def kernel(x, c, ctx, c_ctx, w_ada, b_ada, w_in, w_out, diff_lambda, diff_subln_gain,
           hgrn_lower_bound, hgrn_norm_gain, ln_gain, ln_bias):
    maps, SEQ, NQ = make_in_maps(x, c, ctx, c_ctx, w_ada, b_ada, w_in, w_out, diff_lambda, diff_subln_gain,
                                 hgrn_lower_bound, hgrn_norm_gain, ln_gain, ln_bias)
    if SEQ not in _NC_CACHE:
        _NC_CACHE[SEQ] = build_program(SEQ)
    nc = _NC_CACHE[SEQ]
    res = run_bass_kernel_spmd(nc, maps, core_ids=list(range(8)))
    B = np.asarray(x).shape[0]
    out = np.zeros((B, SEQ, D), np.float32)
    for core in range(8):
        b, j = core // 4, core % 4
        out[b, j * NQ:(j + 1) * NQ] = np.asarray(res.results[core]["y"], dtype=np.float32)
    return out
```

```python
import math
import numpy as np
import concourse.bass as bass
import concourse.mybir as mybir
from concourse.bass_utils import run_bass_kernel_spmd

F32 = mybir.dt.float32
BF16 = mybir.dt.bfloat16
U8 = mybir.dt.uint8
AF = mybir.ActivationFunctionType
ALU = mybir.AluOpType
AX = mybir.AxisListType

D = 1024
CTX = 256
NCH = 8
EPS = 1e-5
LAMBDA_INIT = 0.8 - 0.6 * math.exp(0.0)
ALPHA = 2.0 ** 0.25
PREF_C = False
PREF_D = False
WCOL = dict(aq=0, ak=512, av=1024, ag=1536, hq=2048, hi=2560, hff=3072, hfb=3584, hg=4096)


class _Op:
    __slots__ = ("eng", "fn", "idx", "inc", "semval", "dma", "dma_val", "waits")


class Prog:
    ENGS = ("pe", "act", "dve", "pool", "sp")

    def __init__(self):
        self.q = {e: [] for e in self.ENGS}
        self.lw = {}
        self.rd = {}
        self.waited = {e: {} for e in self.ENGS}
        self.dma_cnt = {}
        self.cap = None

    def capture(self, f):
        self.cap = []
        f()
        lst, self.cap = self.cap, None
        return lst

    def replay(self, item):
        if item is not None:
            self.op(item[0], item[1], r=item[2], w=item[3], dma=item[4])

    def mark(self):
        if self.cap is not None:
            self.cap.append(None)

    def op(self, eng, fn, r=(), w=(), dma=None, extra=()):
        if self.cap is not None:
            self.cap.append((eng, fn, tuple(r), tuple(w), dma))
            return None
        o = _Op()
        o.eng, o.fn, o.idx, o.inc, o.dma, o.semval, o.dma_val = eng, fn, len(self.q[eng]), False, dma, 0, 0
        deps = list(extra)
        for k in r:
            p = self.lw.get(k)
            if p is not None:
                deps.append(p)
        for k in w:
            p = self.lw.get(k)
            if p is not None:
                deps.append(p)
            deps.extend(self.rd.get(k, ()))
        best = {}
        wd = self.waited[eng]
        for p in deps:
            if p.dma is not None:
                key = ("dma", p.dma)
                if wd.get(key, 0) < p.dma_val and best.get(key, 0) < p.dma_val:
                    best[key] = p.dma_val
            else:
                if p.eng == eng and eng in ("pe", "sp"):
                    continue
                key = p.eng
                if wd.get(key, -1) < p.idx and (key not in best or best[key].idx < p.idx):
                    best[key] = p
        waits = []
        for key, v in best.items():
            if isinstance(key, tuple):
                wd[key] = v
                waits.append(("dma", key[1], v))
            else:
                wd[key] = v.idx
                v.inc = True
                waits.append(("eng", v))
        o.waits = waits
        if dma is not None:
            self.dma_cnt[dma] = self.dma_cnt.get(dma, 0) + 16
            o.dma_val = self.dma_cnt[dma]
        self.q[eng].append(o)
        for k in w:
            self.lw[k] = o
            self.rd[k] = []
        for k in r:
            self.rd.setdefault(k, []).append(o)
        return o

    def barrier(self):
        lasts = []
        for e in self.ENGS:
            if self.q[e]:
                lasts.append(self.q[e][-1])
        dmas = {}
        for e in self.ENGS:
            for o in self.q[e]:
                if o.dma is not None:
                    dmas[o.dma] = o
        for e in self.ENGS:
            self.op(e, lambda g: g.nop(), extra=lasts + list(dmas.values()))
        self.lw.clear()
        self.rd.clear()

    def emit(self, nc, block, engs, sems, dma_sems):
        for e in self.ENGS:
            c = 0
            for o in self.q[e]:
                if o.dma is None and o.inc:
                    c += 1
                    o.semval = c

        def run(e):
            def body(g):
                for o in self.q[e]:
                    for w in o.waits:
                        if w[0] == "dma":
                            g.wait_ge(dma_sems[w[1]], w[2])
                        else:
                            g.wait_ge(sems[w[1].eng], w[1].semval)
                    ins = o.fn(g)
                    if o.dma is not None:
                        ins.then_inc(dma_sems[o.dma], 16)
                    elif o.inc:
                        ins.then_inc(sems[e], 1)
            return body

        block.tensor(run("pe"))
        block.scalar(run("act"))
        block.vector(run("dve"))
        block.gpsimd(run("pool"))
        block.sync(run("sp"))


def build_program(SEQ):
    NQ = SEQ // 4
    NBO = NQ // 128
    NSTO = NQ // 512
    TALL = 2 * CTX + 4 * NQ
    NK = CTX + 4 * NQ
    NKT = NK // 128
    sts = []
    sts.append(dict(row=0, n=CTX, seg=0, kv=True, lat=False, key0=0))
    sts.append(dict(row=CTX, n=CTX, seg=1, kv=False, lat=False, key0=None))
    row, key, li = 2 * CTX, CTX, 0
    for seg in (2, 3, 4, 5):
        for _ in range(NSTO):
            sts.append(dict(row=row, n=512, seg=seg, kv=True, lat=True, key0=key, li=li))
            row += 512
            key += 512
            li += 1
    NST = len(sts)
    NLST = li

    nc = bass.Bass("TRN2", target_bir_lowering=False)

    def din(name, shape, dt=F32):
        return nc.dram_tensor(name, list(shape), dt, kind="ExternalInput").ap()

    xs = din("xs", [TALL, D])
    cvec = din("cvec", [2, D])
    w_ada = din("w_ada", [D, 3 * D])
    b_ada = din("b_ada", [3 * D])
    w_in = din("w_in", [D, 4608])
    wz = din("wz", [5, D, 512])
    lbseg = din("lbseg", [5, 2, 512])
    lbown = din("lbown", [2, 2, 512])
    rope = din("rope", [NLST, 128, 2, 512])
    sel = din("sel", [128, 16])
    cmat = din("cmat", [128, 6, 128])
    w_out = din("w_out", [D, D])
    dlam = din("dlam", [256])
    subg = din("subg", [128])
    hng = din("hng", [128])
    lng = din("lng", [D])
    lnb = din("lnb", [D])
    y = nc.dram_tensor("y", [NQ, D], F32, kind="ExternalOutput").ap()

    def dscr(name, shape, dt):
        return nc.dram_tensor(name, list(shape), dt).ap()

    hT_d = dscr("hT_d", [NST, 128, NCH * 512], BF16)
    kT_d = dscr("kT_d", [4, 128, NK], BF16)
    v_d = dscr("v_d", [4, 128, NKT, 130], BF16)
    q_d = dscr("q_d", [4, 2, 128, NQ], BF16)
    ga_d = dscr("ga_d", [NBO, 128, 512], BF16)
    gh_d = dscr("gh_d", [NBO, 128, 512], BF16)
    qbT_d = dscr("qbT_d", [NBO, 128, 512], BF16)
    incb_d = dscr("incb_d", [NBO, 128, 512], F32)
    op_d = dscr("op_d", [NBO, 128, 512], F32)

    pr = Prog()
    ARENA = 207 * 1024

    import contextlib
    stack = contextlib.ExitStack()
    with stack:
        arena = stack.enter_context(nc.sbuf_tensor("arena", [128, ARENA], U8))
        psum = stack.enter_context(nc.psum_tensor("psum", [128, 8, 512], F32))
        sems = {e: stack.enter_context(nc.semaphore("s_" + e)) for e in Prog.ENGS}

        class Carver:
            def __init__(self, base=0):
                self.off = base

            def take(self, shape, dt):
                nb = int(np.prod(shape)) * (4 if dt == F32 else 2)
                nb = (nb + 63) // 64 * 64
                a = arena[:, self.off:self.off + nb // 1].bitcast(dt)
                n = int(np.prod(shape))
                a = a[:, 0:n]
                self.off += nb
                assert self.off <= ARENA, f"arena overflow {self.off}"
                if len(shape) == 2:
                    a = a.rearrange("p (a b) -> p a b", a=shape[0])
                elif len(shape) == 3:
                    a = a.rearrange("p (a b c) -> p a b c", a=shape[0], b=shape[1])
                return a

        def bank(i):
            return psum[:, i, :]

        def bank_bf(i):
            return psum[:, i, :].bitcast(BF16)

        cv = Carver(0)
        cm = cv.take([6 * 128], F32)
        cm = cm
        identb = cv.take([128], BF16)
        rpermb = cv.take([128], BF16)
        maskf = cv.take([512], BF16)
        maskb = cv.take([512], BF16)
        ones_f = cv.take([4], F32)
        selt = cv.take([16], F32)
        sc = cv.take([2, 8], F32)
        sh_t = cv.take([2, 8], F32)
        sc1_t = cv.take([2, 8], F32)
        gate_t = cv.take([D], F32)
        gainA = cv.take([512], F32)
        gainH = cv.take([512], F32)
        lam_t = cv.take([4], F32)
        lbt_own = cv.take([2, 512], F32)
        oml_own = cv.take([2, 512], F32)
        S_cF = cv.take([512], F32)
        S_cB = cv.take([512], F32)
        S_cur = cv.take([512], F32)
        SF_fin = cv.take([512], F32)
        SB_fin = cv.take([512], F32)
        Db_all = cv.take([NBO, 4], F32)
        Y = cv.take([NBO, D], BF16)
        PBASE = cv.off

        identf = cm[:, 0:128]
        Lf = cm[:, 256:384]
        Lb = cm[:, 384:512]

        def dma(out, in_, key, r=(), w=(), slow=False):
            if slow:
                return pr.op("sp", lambda g: g.dma_start(out=out, in_=in_, allow_slow_non_contiguous=True),
                             r=r, w=w, dma=key)
            return pr.op("sp", lambda g: g.dma_start(out=out, in_=in_), r=r, w=w, dma=key)

        def act(out, in_, func, r, w, scale=1.0, bias=0.0):
            return pr.op("act", lambda g: g.activation(out=out, in_=in_, func=func, bias=bias, scale=scale), r=r, w=w)

        def mm(out, lhsT, rhs, start, stop, r, w):
            return pr.op("pe", lambda g: g.matmul(out, lhsT, rhs, start=start, stop=stop), r=r, w=w)

        def tr(out, in_, ident, r, w):
            return pr.op("pe", lambda g: g.transpose(out, in_, ident), r=r, w=w)

        def ts(eng, out, in0, s1, s2, op0, op1, r, w):
            if s2 is None:
                return pr.op(eng, lambda g: g.tensor_single_scalar(out, in0, s1, op0), r=r, w=w)
            return pr.op(eng, lambda g: g.tensor_scalar(out, in0, s1, s2, op0, op1), r=r, w=w)

        def tt(eng, out, in0, in1, op, r, w):
            return pr.op(eng, lambda g: g.tensor_tensor(out, in0, in1, op), r=r, w=w)

        def stt(out, in0, scalar, in1, op0, op1, r, w):
            return pr.op("dve", lambda g: g.scalar_tensor_tensor(out, in0, scalar, in1, op0, op1), r=r, w=w)

        def cp(eng, out, in_, r, w):
            if eng == "act":
                return pr.op("act", lambda g: g.copy(out, in_), r=r, w=w)
            return pr.op(eng, lambda g: g.tensor_copy(out, in_), r=r, w=w)

        def recip1p(buf, key):
            act(buf, buf, AF.Ln, r=[key], w=[key], bias=1.0)
            act(buf, buf, AF.Exp, r=[key], w=[key], scale=-1.0)

        def rsqrt_small(buf, key):
            act(buf, buf, AF.Ln, r=[key], w=[key])
            act(buf, buf, AF.Exp, r=[key], w=[key], scale=-0.5)

        def memset(eng, ap, val, w):
            return pr.op(eng, lambda g: g.memset(ap, val), w=w)

        c0 = Carver(PBASE)
        wst = c0.take([8, 512], F32)
        wst2 = c0.take([8, 512], F32)
        cvt = c0.take([2, 8], F32)
        bada_t = c0.take([24], F32)
        tmpa = c0.take([2, 8], F32)
        scb = c0.take([8, 128], F32)
        modT = c0.take([16, 2], F32)
        bgate = c0.take([D], F32)
        lamraw = c0.take([256], F32)
        lamtmp = c0.take([256], F32)
        lbraw = c0.take([2, 2, 512], F32)
        g128 = c0.take([2, 128], F32)

        dma(cm, cmat.rearrange("p a b -> p (a b)"), "c_cm", w=["cm"])
        dma(selt, sel, "c_sel", w=["selt"])
        dma(cvt, cvec.rearrange("w (c p) -> p w c", p=128), "c_cv", w=["cvt"], slow=True)
        dma(bada_t, b_ada.rearrange("(c p) -> p c", p=128), "c_ba", w=["bada"], slow=True)
        dma(bgate, b_ada[2 * D:3 * D].partition_broadcast(128), "c_bg", w=["bgate"])
        dma(lamraw, dlam.partition_broadcast(128), "c_lam", w=["lamraw"])
        dma(lbraw, lbown.rearrange("a b c -> (a b c)").partition_broadcast(128).rearrange("p (a b c) -> p a b c", a=2, b=2),
            "c_lbo", w=["lbraw"])
        dma(g128[:, 0, :], subg.partition_broadcast(128), "c_g1", w=["g128a"])
        dma(g128[:, 1, :], hng.partition_broadcast(128), "c_g2", w=["g128b"])

        cp("dve", identb, cm[:, 0:128], r=["cm"], w=["identb"])
        cp("dve", rpermb, cm[:, 128:256], r=["cm"], w=["rpermb"])
        for hh in range(4):
            cp("dve", maskf[:, hh * 128:(hh + 1) * 128], cm[:, 512:640], r=["cm"], w=["maskf"])
            cp("dve", maskb[:, hh * 128:(hh + 1) * 128], cm[:, 640:768], r=["cm"], w=["maskb"])
            ts("dve", gainA[:, hh * 128:(hh + 1) * 128], g128[:, 0, :], 1.0 - LAMBDA_INIT, None, ALU.mult, None,
               r=["g128a"], w=["gainA"])
            cp("dve", gainH[:, hh * 128:(hh + 1) * 128], g128[:, 1, :], r=["g128b"], w=["gainH"])
        memset("dve", ones_f, 1.0, w=["ones"])
        tt("dve", lamtmp[:, 0:64], lamraw[:, 0:64], lamraw[:, 64:128], ALU.mult, r=["lamraw"], w=["lamtmp"])
        tt("dve", lamtmp[:, 64:128], lamraw[:, 128:192], lamraw[:, 192:256], ALU.mult, r=["lamraw"], w=["lamtmp"])
        pr.op("dve", lambda g: g.tensor_reduce(lam_t[:, 0:2], lamtmp[:, 0:128].rearrange("p (a b) -> p a b", a=2),
                                               AX.X, ALU.add), r=["lamtmp"], w=["lam"])
        act(lam_t[:, 0:2], lam_t[:, 0:2], AF.Exp, r=["lam"], w=["lam"])
        tt("dve", lam_t[:, 2:3], lam_t[:, 0:1], lam_t[:, 1:2], ALU.subtract, r=["lam"], w=["lam"])
        ts("dve", lam_t[:, 3:4], lam_t[:, 2:3], LAMBDA_INIT, -1.0, ALU.add, ALU.mult, r=["lam"], w=["lam"])
        for d_ in range(2):
            tt("dve", lbt_own[:, d_, :], lbraw[:, d_, 1, :], lbraw[:, d_, 0, :], ALU.subtract, r=["lbraw"], w=["lbo"])
        act(lbt_own, lbt_own, AF.Exp, r=["lbo"], w=["lbo"])
        recip1p(lbt_own, "lbo")
        ts("dve", oml_own, lbt_own, -1.0, 1.0, ALU.mult, ALU.add, r=["lbo"], w=["omo"])
        act(tmpa, cvt, AF.Exp, r=["cvt"], w=["tmpa"], scale=-1.0)
        recip1p(tmpa, "tmpa")
        tt("dve", sc, cvt, tmpa, ALU.mult, r=["cvt", "tmpa"], w=["sc"])
        for k in range(8):
            cp("dve", scb[:, k, :], sc[:, 0, k:k + 1].to_broadcast([128, 128]), r=["sc"], w=["scb"])
        wsts = [wst, wst2]
        for piece in range(6):
            wb = wsts[piece % 2]
            key = "wst%d" % (piece % 2)
            dma(wb, w_ada[:, piece * 512:(piece + 1) * 512].rearrange("(c p) n -> p c n", p=128), "d_" + key, w=[key])
            if piece < 4:
                for jj in range(4):
                    cc = piece * 4 + jj
                    for k in range(8):
                        mm(bank(0)[:, cc * 2:cc * 2 + 2], wb[:, k, jj * 128:(jj + 1) * 128], sc[:, :, k],
                           k == 0, k == 7, r=[key, "sc"], w=["b0"])
            else:
                hb = piece - 4
                for k in range(8):
                    mm(bank(1 + hb), scb[:, k, :], wb[:, k, :], k == 0, k == 7, r=[key, "scb"], w=["b%d" % (1 + hb)])
        cp("dve", modT, bank(0)[:, 0:32].rearrange("p (a b) -> p a b", a=16), r=["b0"], w=["modT"])
        for wch in range(2):
            tt("dve", sh_t[:, wch, :], modT[:, 0:8, wch], bada_t[:, 0:8], ALU.add, r=["modT", "bada"], w=["sh"])
            tt("dve", sc1_t[:, wch, :], modT[:, 8:16, wch], bada_t[:, 8:16], ALU.add, r=["modT", "bada"], w=["sc1"])
        ts("dve", sc1_t, sc1_t, 1.0, None, ALU.add, None, r=["sc1"], w=["sc1"])
        for hb in range(2):
            tt("dve", gate_t[:, hb * 512:(hb + 1) * 512], bank(1 + hb), bgate[:, hb * 512:(hb + 1) * 512], ALU.add,
               r=["b%d" % (1 + hb), "bgate"], w=["gate"])
        pr.barrier()

        ca = Carver(PBASE + 16 * 1024 + 4 * 8 * 1024)
        NXT = 8
        xt = [ca.take([D], F32) for _ in range(NXT)]
        hTo = [ca.take([8, 512], BF16) for _ in range(2)]
        for b_ in range(2):
            memset("pool", hTo[b_], 0.0, w=["hTo%d_%d" % (b_, k) for k in range(8)])
        cbw = Carver(PBASE)
        stgB = cbw.take([8, 512], F32)
        WB = [cbw.take([8, 512], BF16) for _ in range(4)]
        wci = [0]

        def load_wB(i_):
            nm = ("ak", "av", "aq", "ag")[i_]
            dma(stgB, w_in[:, WCOL[nm]:WCOL[nm] + 512].rearrange("(c p) n -> p c n", p=128), "d_stg0", w=["stg0"])
            cp(("dve", "act")[i_ % 2], WB[i_], stgB, r=["stg0"], w=["W" + nm])

        blkno = 0
        allrows = [st["row"] + j * 128 for st in sts for j in range(st["n"] // 128)]
        issued = [0]

        def xload_upto(n):
            while issued[0] < min(n, len(allrows)):
                i_ = issued[0]
                dma(xt[i_ % NXT], xs[allrows[i_]:allrows[i_] + 128, :], "d_xt%d" % (i_ % NXT), w=["xt%d" % (i_ % NXT)])
                issued[0] += 1

        for si, st in enumerate(sts):
            ho = hTo[si % 2]
            hk = "hTo%d" % (si % 2)
            which = 1 if not st["lat"] else 0
            nb = st["n"] // 128
            xload_upto(blkno + nb + 4)
            if si in (3, 6, 9, 12) and wci[0] < 4:
                load_wB(wci[0])
                wci[0] += 1
            for k in range(8):
                bkk = "b%d" % k
                for j in range(nb):
                    xs_ = (blkno + j) % NXT
                    tr(bank(k)[:, j * 128:(j + 1) * 128], xt[xs_][:, k * 128:(k + 1) * 128], identf,
                       r=["xt%d" % xs_, "cm"], w=[bkk])
                src = bank(k)[:, 0:nb * 128]
                dst = ho[:, k, 0:nb * 128]
                if k % 2 == 0:
                    pr.op("act", lambda g, dst=dst, src=src, k=k, which=which: g.activation(
                        out=dst, in_=src, func=AF.Identity, bias=sh_t[:, which, k:k + 1],
                        scale=sc1_t[:, which, k:k + 1]), r=[bkk], w=[hk + "_%d" % k])
                else:
                    ts("dve", dst, src, sc1_t[:, which, k:k + 1], sh_t[:, which, k:k + 1], ALU.mult, ALU.add,
                       r=[bkk], w=[hk + "_%d" % k])
            blkno += nb
            dma(hT_d[si], ho.rearrange("p c n -> p (c n)"), "d_" + hk + "o", r=[hk + "_%d" % k for k in range(8)])
        while wci[0] < 4:
            load_wB(wci[0])
            wci[0] += 1
        pr.barrier()

        def load_w(dst_bf, src_ap, ncols, tag, stg, stgk, ci=[0]):
            dma(stg[:, :, 0:ncols], src_ap.rearrange("(c p) n -> p c n", p=128), "d_" + stgk, w=[stgk])
            eng = ("dve", "act")[ci[0] % 2]
            ci[0] += 1
            cp(eng, dst_bf, stg[:, :, 0:ncols], r=[stgk], w=[tag])

        cb = Carver(PBASE)
        stg = [cb.take([8, 512], F32)] * 2
        Wk = cb.take([8, 512], BF16)
        Wv = cb.take([8, 512], BF16)
        Wq = cb.take([8, 512], BF16)
        Wg = cb.take([8, 512], BF16)
        hTi = [cb.take([8, 512], BF16) for _ in range(2)]
        ropet = [cb.take([2, 512], F32) for _ in range(2)]
        kb = [cb.take([512], BF16) for _ in range(4)]
        t1 = [cb.take([512], F32) for _ in range(2)]
        t2 = [cb.take([512], F32) for _ in range(2)]
        kTo = [cb.take([4, 512], BF16) for _ in range(2)]
        vo = [cb.take([4, 4, 130], BF16) for _ in range(2)]
        q0o = [cb.take([4, 512], BF16) for _ in range(2)]
        q1o = [cb.take([4, 512], BF16) for _ in range(2)]
        gu = [cb.take([512], F32) for _ in range(2)]
        gao = [cb.take([512], BF16) for _ in range(2)]

        for b_ in range(2):
            memset("pool", vo[b_], 0.0, w=["vo%d" % b_])
            memset("pool", vo[b_][:, :, :, 128:129], 1.0, w=["vo%d" % b_])
            memset("pool", q0o[b_], 0.0, w=["q0o%d" % b_])
            memset("pool", q1o[b_], 0.0, w=["q1o%d" % b_])

        pb = [0]

        def nbank():
            pb[0] = (pb[0] + 1) % 8
            return pb[0]

        cnt = 0
        ownblk = 0
        for si, st in enumerate(sts):
            if not st["kv"]:
                continue
            N = st["n"]
            hi_ = hTi[cnt % 2]
            hik = "hTi%d" % (cnt % 2)
            rk = None
            if st["lat"]:
                rt = ropet[cnt % 2]
                rk = "rope%d" % (cnt % 2)

            def b1_load(c_, si_):
                st_ = sts[si_]
                dma(hTi[c_ % 2].rearrange("p c n -> p (c n)"), hT_d[si_], "d_hTi%d" % (c_ % 2), w=["hTi%d" % (c_ % 2)])
                if st_["lat"]:
                    dma(ropet[c_ % 2].rearrange("p a n -> p (a n)"), rope[st_["li"]].rearrange("p a n -> p (a n)"),
                        "d_rope%d" % (c_ % 2), w=["rope%d" % (c_ % 2)])
            if cnt == 0:
                b1_load(0, si)
            nxt_ = [i_ for i_ in range(si + 1, NST) if sts[i_]["kv"]]
            if nxt_:
                b1_load(cnt + 1, nxt_[0])
            own = st["seg"] == 5
            ko = kTo[cnt % 2]
            kok = "kTo%d" % (cnt % 2)
            q0, q1 = q0o[cnt % 2], q1o[cnt % 2]
            qk_ = "qo%d" % (cnt % 2)

            def fm_proj(W, Wkey, h):
                bk = nbank()
                bkk = "b%d" % bk
                for k in range(8):
                    mm(bank(bk)[:, 0:N], W[:, k, h * 128:(h + 1) * 128], hi_[:, k, 0:N], k == 0, k == 7,
                       r=[Wkey, hik], w=[bkk])
                return bk, bkk

            def fm_evac(bk, bkk, outs, outkey, h):
                if not st["lat"]:
                    cp("act", outs[0][:, h, 0:N], bank(bk)[:, 0:N], r=[bkk], w=[outkey])
                    return
                cp("act", kb[h][:, 0:N], bank(bk)[:, 0:N], r=[bkk], w=["kb%d" % h])

            def fm_rope(outs, outkey, h):
                if not st["lat"]:
                    return
                sl = h % 2
                kbb, kbk = kb[h], "kb%d" % h
                bk2 = nbank()
                bk2k = "b%d" % bk2
                mm(bank(bk2)[:, 0:N], rpermb, kbb[:, 0:N], True, True, r=["rpermb", kbk], w=[bk2k])
                tt("dve", t1[sl][:, 0:N], kbb[:, 0:N], rt[:, 0, 0:N], ALU.mult, r=[kbk, rk], w=["t1%d" % sl])
                tt("dve", t2[sl][:, 0:N], bank(bk2)[:, 0:N], rt[:, 1, 0:N], ALU.mult, r=[bk2k, rk], w=["t2%d" % sl])
                if len(outs) == 1:
                    tt("pool", outs[0][:, h, 0:N], t1[sl][:, 0:N], t2[sl][:, 0:N], ALU.add,
                       r=["t1%d" % sl, "t2%d" % sl], w=[outkey])
                else:
                    tt("pool", outs[0][0:64, h, 0:N], t1[sl][0:64, 0:N], t2[sl][0:64, 0:N], ALU.add,
                       r=["t1%d" % sl, "t2%d" % sl], w=[outkey])
                    tt("pool", outs[1][64:128, h, 0:N], t1[sl][64:128, 0:N], t2[sl][64:128, 0:N], ALU.add,
                       r=["t1%d" % sl, "t2%d" % sl], w=[outkey])

            kbs = [fm_proj(Wk, "Wak", h) for h in range(4)]
            for h in range(4):
                fm_evac(kbs[h][0], kbs[h][1], [ko], kok, h)
            vb = vo[cnt % 2]
            vk = "vo%d" % (cnt % 2)
            nsub = N // 128
            for j in range(nsub):
                bk = nbank()
                bkk = "b%d" % bk
                for k in range(8):
                    mm(bank(bk), hi_[:, k, j * 128:(j + 1) * 128], Wv[:, k, :], k == 0, k == 7, r=["Wav", hik], w=[bkk])
                cp("act", vb[:, j, :, 0:128], bank(bk).rearrange("p (h e) -> p h e", h=4), r=[bkk], w=[vk])
            for h in range(4):
                fm_rope([ko], kok, h)
            dma(kT_d[:, :, st["key0"]:st["key0"] + N].rearrange("h p n -> p h n"), ko[:, :, 0:N], "d_" + kok, r=[kok])
            kt0 = st["key0"] // 128
            for h in range(4):
                dma(v_d[h, :, kt0:kt0 + nsub, :], vb[:, 0:nsub, h, :], "d_%s_%d" % (vk, h), r=[vk])
            if own:
                qbs = [fm_proj(Wq, "Waq", h) for h in range(4)]
                for h in range(4):
                    fm_evac(qbs[h][0], qbs[h][1], [q0, q1], qk_, h)
                for j in range(nsub):
                    bk = nbank()
                    bkk = "b%d" % bk
                    for k in range(8):
                        mm(bank(bk), hi_[:, k, j * 128:(j + 1) * 128], Wg[:, k, :], k == 0, k == 7,
                           r=["Wag", hik], w=[bkk])
                    sl = j % 2
                    act(gu[sl], bank(bk), AF.Exp, r=[bkk], w=["gu%d" % sl], scale=-1.0)
                    recip1p(gu[sl], "gu%d" % sl)
                    tt("dve", gu[sl], bank(bk), gu[sl], ALU.mult, r=[bkk, "gu%d" % sl], w=["gu%d" % sl])
                    tt("pool", gao[sl], gu[sl], gainA, ALU.mult, r=["gu%d" % sl, "gainA"], w=["gao%d" % sl])
                    dma(ga_d[ownblk + j], gao[sl], "d_gao%d" % sl, r=["gao%d" % sl])
                for h in range(4):
                    fm_rope([q0, q1], qk_, h)
                tok0 = ownblk * 128
                dma(q_d[:, 0, :, tok0:tok0 + N].rearrange("h p n -> p h n"), q0, "d_q0%d" % (cnt % 2), r=[qk_])
                dma(q_d[:, 1, :, tok0:tok0 + N].rearrange("h p n -> p h n"), q1, "d_q1%d" % (cnt % 2), r=[qk_])
                ownblk += nsub
            cnt += 1
        pr.barrier()

        ch = Carver(PBASE)
        stg = [ch.take([8, 512], F32)] * 2
        Whi = ch.take([8, 512], BF16)
        Wz1 = ch.take([8, 512], BF16)
        hTi = [ch.take([8, 512], BF16) for _ in range(2)]
        Lfb = ch.take([128], BF16)
        Lbb = ch.take([128], BF16)
        ones_b = ch.take([4], BF16)
        NS = 3
        U2 = [ch.take([2, 512], F32) for _ in range(2)]
        G2 = [ch.take([2, 512], F32) for _ in range(2)]
        GH2 = [ch.take([2, 512], BF16) for _ in range(2)]
        GL2 = [ch.take([2, 512], BF16) for _ in range(2)]
        KT2 = [ch.take([2, 512], BF16) for _ in range(2)]
        QT2 = [ch.take([2, 512], BF16) for _ in range(2)]
        D2 = [ch.take([2, 4], F32) for _ in range(NS)]
        def slot_at(base):
            c5 = Carver(base)
            return [c5.take([2, 512], F32), c5.take([2, 512], F32), c5.take([2, 512], BF16), c5.take([2, 512], BF16),
                    c5.take([2, 512], BF16), c5.take([2, 512], BF16)], c5.off
        slot2_b3, e5 = slot_at(PBASE)
        assert e5 <= PBASE + 8 * 512 * 4

        def set_slot2(bufs):
            for lst, ap_ in zip((U2, G2, GH2, GL2, KT2, QT2), bufs):
                if len(lst) == 2:
                    lst.append(ap_)
                else:
                    lst[2] = ap_
        hib2 = [ch.take([2, 512], BF16) for _ in range(3)]
        ss = ch.take([4], F32)
        c2 = Carver(ch.off + 3 * 8 * 512 * 2)
        lbr = c2.take([2, 512], F32)
        lbt_s = c2.take([512], F32)
        oml_s = c2.take([512], F32)
        lbt2 = c2.take([2, 512], F32)
        oml2 = c2.take([2, 512], F32)
        c3 = Carver(ch.off)
        slot2_b2, e6 = slot_at(ch.off)
        Wz2 = c3.take([8, 512], BF16)
        Whq = c3.take([8, 512], BF16)
        assert e6 <= c3.off
        Whg = c3.take([8, 512], BF16)
        set_slot2(slot2_b2)
        qf = [c3.take([512], F32) for _ in range(3)]
        ghu = c3.take([512], F32)
        gho = [c3.take([512], BF16) for _ in range(2)]
        XT = [c3.take([512], BF16) for _ in range(4)]
        scm = [c3.take([512], BF16) for _ in range(2)]
        Sbf = c3.take([512], BF16)
        Rt = c3.take([512], F32)
        opt = [c3.take([512], F32) for _ in range(2)]
        incb = [c3.take([512], F32) for _ in range(2)]
        HEND = max(c2.off, c3.off)

        cp("dve", Lfb, cm[:, 256:384], r=[], w=["Lfb"])
        cp("dve", Lbb, cm[:, 384:512], r=[], w=["Lbb"])
        memset("dve", ones_b, 1.0, w=["ones_b"])

        def npair():
            nxt = ((pb[0] + 2) // 2 * 2) % 8
            pb[0] = nxt + 1
            return nxt // 2

        def pair_ap(p):
            return psum[:, 2 * p:2 * p + 2, :]

        def pkeys(p):
            return ["b%d" % (2 * p), "b%d" % (2 * p + 1)]

        zc = [0]

        def zchain2_a(p):
            s = zc[0] % NS
            zc[0] += 1
            act(U2[s], pair_ap(p), AF.Exp, r=pkeys(p), w=["U%d" % s], scale=-1.0)
            return s

        def zchain2_b(s, Lmats, lbt, oml, lbkeys, want_q, qsrc=None, qkey=None, fixed=None):
            k_ = lambda n: "%s%d" % (n, s)
            recip1p(U2[s], k_("U"))
            tt("dve", U2[s], U2[s], oml, ALU.mult, r=[k_("U")] + lbkeys, w=[k_("U")])
            tt("dve", U2[s], U2[s], lbt, ALU.add, r=[k_("U")] + lbkeys, w=[k_("U")])
            act(G2[s], U2[s], AF.Ln, r=[k_("U")], w=[k_("G")])
            ts("pool", U2[s], U2[s], -1.0, 1.0, ALU.mult, ALU.add, r=[k_("U")], w=[k_("U")])
            cp("act", GH2[s], G2[s], r=[k_("G")], w=[k_("GH")])
            tt("dve", GL2[s], G2[s], GH2[s], ALU.subtract, r=[k_("G"), k_("GH")], w=[k_("GL")])
            pr.mark()
            p = npair() if fixed is None else fixed[0]
            for l in range(2):
                bkk = "b%d" % (2 * p + l)
                mm(bank(2 * p + l), Lmats[l][0], GH2[s][:, l, :], True, False, r=[Lmats[l][1], k_("GH")], w=[bkk])
                mm(bank(2 * p + l), Lmats[l][0], GL2[s][:, l, :], False, True, r=[Lmats[l][1], k_("GL")], w=[bkk])
            bd = nbank() if fixed is None else fixed[1]
            bdk = "b%d" % bd
            for l in range(2):
                for h in range(4):
                    hs = slice(h * 128, (h + 1) * 128)
                    c = l * 4 + h
                    mm(bank(bd)[:, c:c + 1], GH2[s][:, l, hs], ones_b[:, 0:1], True, False, r=[k_("GH"), "ones_b"], w=[bdk])
                    mm(bank(bd)[:, c:c + 1], GL2[s][:, l, hs], ones_b[:, 0:1], False, True, r=[k_("GL"), "ones_b"], w=[bdk])
            act(G2[s], pair_ap(p), AF.Exp, r=pkeys(p) + [k_("GL")], w=[k_("G")], scale=-1.0)
            tt("dve", KT2[s], U2[s], G2[s], ALU.mult, r=[k_("U"), k_("G")], w=[k_("KT")])
            if want_q:
                act(G2[s], pair_ap(p), AF.Exp, r=pkeys(p), w=[k_("G")])
                for l in range(2):
                    tt("dve", QT2[s][:, l, :], qsrc, G2[s][:, l, :], ALU.mult, r=[qkey, k_("G")], w=[k_("QT")])
            act(D2[s].rearrange("p a b -> p (a b)"), bank(bd)[:, 0:8], AF.Exp, r=[bdk], w=[k_("D")])

        def tokmajor_to(bk, W, Wkey, hi_, hik, j):
            bkk = "b%d" % bk
            for k in range(8):
                mm(bank(bk), hi_[:, k, j * 128:(j + 1) * 128], W[:, k, :], k == 0, k == 7, r=[Wkey, hik], w=[bkk])
            return bk, bkk

        def tokmajor(W, Wkey, hi_, hik, j):
            return tokmajor_to(nbank(), W, Wkey, hi_, hik, j)

        def inc_mm_to(bk, ktile, ktkey, hb_, hbk):
            bkk = "b%d" % bk
            for h in range(4):
                mm(bank(bk)[:, h * 128:(h + 1) * 128], ktile[:, h * 128:(h + 1) * 128], hb_[:, h * 128:(h + 1) * 128],
                   True, True, r=[ktkey, hbk], w=[bkk])
            return bk, bkk

        def rstep(R, Rkey, Dprev, Dkey, incsrc, inckey):
            for h in range(4):
                hs = slice(h * 128, (h + 1) * 128)
                stt(R[:, hs], R[:, hs], Dprev[:, h:h + 1], incsrc[:, hs], ALU.mult, ALU.add,
                    r=[Rkey, Dkey, inckey], w=[Rkey])

        def state_update(S, Skey, incsrc, inckey, Dtile, Dkey):
            tt("dve", Rt, S, incsrc, ALU.add, r=[Skey, inckey], w=["Rt"])
            for h in range(4):
                ts("dve", S[:, h * 128:(h + 1) * 128], Rt[:, h * 128:(h + 1) * 128],
                   Dtile[:, h:h + 1], None, ALU.mult, None, r=["Rt", Dkey], w=[Skey])

        load_w(Whi, w_in[:, WCOL["hi"]:WCOL["hi"] + 512], 512, "Whi", stg[0], "stg0")
        for S_ in (S_cF, S_cB, S_cur, SF_fin, SB_fin):
            memset("pool", S_, 0.0, w=["Sx"])
        pr.barrier()
        segs = {}
        for si, st in enumerate(sts):
            segs.setdefault(st["seg"], []).append(si)
        hcnt = [0]
        pend = [None]
        hlist = [si for seg_ in range(6) for si in segs[seg_]]
        hc = [0]

        def h_load(c_, si_):
            dma(hTi[c_ % 2].rearrange("p c n -> p (c n)"), hT_d[si_], "d_hTi%d" % (c_ % 2), w=["hTi%d" % (c_ % 2)])

        hist = []

        def split(lst):
            i_ = lst.index(None)
            return lst[:i_], lst[i_ + 1:]

        def interleave(a_, b_):
            for i_ in range(max(len(a_), len(b_))):
                if i_ < len(a_):
                    pr.replay(a_[i_])
                if i_ < len(b_):
                    pr.replay(b_[i_])

        def pipe_step(new_q):
            a_ = hist[-1][0] if len(hist) >= 1 else []
            b_ = hist[-2][1] if len(hist) >= 2 else []
            interleave(a_, b_)
            if new_q is not None:
                hist.append(split(pr.capture(new_q)))

        def flush():
            if len(hist) >= 1:
                pipe_step(None)
                interleave([], hist[-1][1])
            del hist[:]

        Wzs = [(Wz1, "Wz1"), (Whg, "WzB")]
        load_w(Wz1, wz[0], 512, "Wz1", stg[0], "stg0")
        for seg in range(5):
            flush()
            Wz_, Wzk = Wzs[seg % 2]
            if seg + 1 < 5:
                load_w(Wzs[(seg + 1) % 2][0], wz[seg + 1], 512, Wzs[(seg + 1) % 2][1], stg[0], "stg0")
            dma(lbr.rearrange("p a n -> p (a n)"), lbseg[seg].rearrange("a n -> (a n)").partition_broadcast(128),
                "d_lbr", w=["lbr"])
            tt("dve", lbt_s, lbr[:, 1, :], lbr[:, 0, :], ALU.subtract, r=["lbr"], w=["lbt_s"])
            act(lbt_s, lbt_s, AF.Exp, r=["lbt_s"], w=["lbt_s"])
            recip1p(lbt_s, "lbt_s")
            ts("dve", oml_s, lbt_s, -1.0, 1.0, ALU.mult, ALU.add, r=["lbt_s"], w=["oml_s"])
            for l in range(2):
                cp("dve", lbt2[:, l, :], lbt_s, r=["lbt_s"], w=["lbt2"])
                cp("dve", oml2[:, l, :], oml_s, r=["oml_s"], w=["oml2"])
            if seg == 0:
                S, Sk = S_cF, "S_cF"
            elif seg == 1:
                S, Sk = S_cB, "S_cB"
            else:
                S, Sk = S_cur, "S_cur"
                b = seg - 2
                Xsrc, Xk = (S_cF, "S_cF") if b == 0 else (S_cur, "S_cur")
                stt(SF_fin, Xsrc, selt[:, b:b + 1], SF_fin, ALU.mult, ALU.add, r=[Xk, "SF_fin", "selt"], w=["SF_fin"])
                if b == 0:
                    ts("dve", S_cur, S_cF, selt[:, 8:9], None, ALU.mult, None, r=["S_cF", "selt"], w=["S_cur"])
                else:
                    ts("dve", S_cur, S_cur, selt[:, 4 + b:5 + b], None, ALU.mult, None, r=["S_cur", "selt"], w=["S_cur"])
                stt(S_cur, S_cB, selt[:, 12 + b:13 + b], S_cur, ALU.mult, ALU.add, r=["S_cB", "S_cur", "selt"], w=["S_cur"])
            dstate = [(ones_f, "ones")]
            for si in segs[seg]:
                st = sts[si]
                hi_ = hTi[hc[0] % 2]
                hik = "hTi%d" % (hc[0] % 2)
                if hc[0] == 0:
                    h_load(0, hlist[0])
                if hc[0] + 1 < len(hlist):
                    h_load(hc[0] + 1, hlist[hc[0] + 1])
                hc[0] += 1
                for jp in range(st["n"] // 256):
                    for l in range(2):
                        tokmajor_to(l, Whi, "Whi", hi_, hik, 2 * jp + l)
                    for l in range(2):
                        tokmajor_to(2 + l, Wz_, Wzk, hi_, hik, 2 * jp + l)
                    hsl = hcnt[0] % 3
                    hcnt[0] += 1
                    hb_ = hib2[hsl]
                    hbk = "hib%d" % hsl

                    def PPOST(hb_=hb_, hbk=hbk):
                        cp("act", hb_, pair_ap(0), r=pkeys(0), w=[hbk])
                        return zchain2_a(1)

                    def mkQ(s, hb_=hb_, hbk=hbk, S=S, Sk=Sk):
                        def Q():
                            zchain2_b(s, [(Lfb, "Lfb"), (Lfb, "Lfb")], lbt2, oml2, ["lbt2", "oml2"], False, fixed=(2, 6))
                            for l in range(2):
                                inc_mm_to(4 + l, KT2[s][:, l, :], "KT%d" % s, hb_[:, l, :], hbk)
                            for l in range(2):
                                rstep(S, Sk, dstate[0][0], dstate[0][1], bank(4 + l), "b%d" % (4 + l))
                                dstate[0] = (D2[s][:, l, :], "D%d" % s)
                        return Q
                    a_ = hist[-1][0] if len(hist) >= 1 else []
                    b_ = hist[-2][1] if len(hist) >= 2 else []
                    interleave(a_, b_)
                    s_ = PPOST()
                    hist.append(split(pr.capture(mkQ(s_))))
            flush()
            for h in range(4):
                hs = slice(h * 128, (h + 1) * 128)
                ts("dve", S[:, hs], S[:, hs], dstate[0][0][:, h:h + 1], None, ALU.mult, None, r=[Sk, dstate[0][1]], w=[Sk])
        stt(SF_fin, S_cur, selt[:, 3:4], SF_fin, ALU.mult, ALU.add, r=["S_cur", "SF_fin", "selt"], w=["SF_fin"])
        ts("dve", SB_fin, S_cur, selt[:, 7:8], None, ALU.mult, None, r=["S_cur", "selt"], w=["SB_fin"])
        stt(SB_fin, S_cB, selt[:, 15:16], SB_fin, ALU.mult, ALU.add, r=["S_cB", "SB_fin", "selt"], w=["SB_fin"])
        pr.barrier()

        load_w(Wz1, w_in[:, WCOL["hff"]:WCOL["hff"] + 512], 512, "Wz1", stg[0], "stg0")
        load_w(Wz2, w_in[:, WCOL["hfb"]:WCOL["hfb"] + 512], 512, "Wz2", stg[0], "stg0")
        load_w(Whq, w_in[:, WCOL["hq"]:WCOL["hq"] + 512], 512, "Whq", stg[0], "stg0")
        load_w(Whg, w_in[:, WCOL["hg"]:WCOL["hg"] + 512], 512, "Whg", stg[0], "stg0")
        pr.barrier()
        set_slot2(slot2_b3)
        ob = 0
        dstate = [(ones_f, "ones")]
        for si in segs[5]:
            st = sts[si]
            hi_ = hTi[hc[0] % 2]
            hik = "hTi%d" % (hc[0] % 2)
            if hc[0] + 1 < len(hlist):
                h_load(hc[0] + 1, hlist[hc[0] + 1])
            hc[0] += 1
            for j in range(4):
                sl = ob % 2
                s3 = ob % 3
                tokmajor_to(0, Whi, "Whi", hi_, hik, j)
                tokmajor_to(2, Wz1, "Wz1", hi_, hik, j)
                tokmajor_to(3, Wz2, "Wz2", hi_, hik, j)
                tokmajor_to(1, Whq, "Whq", hi_, hik, j)
                tokmajor_to(4, Whg, "Whg", hi_, hik, j)
                a_ = hist[-1][0] if len(hist) >= 1 else []
                b_ = hist[-2][1] if len(hist) >= 2 else []
                interleave(a_, b_)
                hb_ = hib2[s3][:, 0, :]
                hbk = "hib%d" % s3
                cp("act", hb_, bank(0), r=["b0"], w=[hbk])
                s = zchain2_a(1)
                qk = "qf%d" % s3
                act(qf[s3], bank(1), AF.Exp, r=["b1"], w=[qk], scale=-1.0)
                recip1p(qf[s3], qk)
                tt("dve", qf[s3], bank(1), qf[s3], ALU.mult, r=["b1", qk], w=[qk])
                act(ghu, bank(4), AF.Exp, r=["b4"], w=["ghu"], scale=-1.0)
                recip1p(ghu, "ghu")
                tt("dve", ghu, bank(4), ghu, ALU.mult, r=["b4", "ghu"], w=["ghu"])
                tt("pool", gho[sl], ghu, gainH, ALU.mult, r=["ghu", "gainH"], w=["gho%d" % sl])
                dma(gh_d[ob], gho[sl], "d_gho%d" % sl, r=["gho%d" % sl])

                def mkQ(ob=ob, sl=sl, s3=s3, hb_=hb_, hbk=hbk, s=s, qk=qk):
                    def Q():
                        zchain2_b(s, [(Lfb, "Lfb"), (Lbb, "Lbb")], lbt_own, oml_own, ["lbo", "omo"], True, qf[s3], qk,
                                  fixed=(3, 5))
                        srcs = [(QT2[s][:, 0, :], "QT%d" % s), (KT2[s][:, 0, :], "KT%d" % s),
                                (QT2[s][:, 1, :], "QT%d" % s), (KT2[s][:, 1, :], "KT%d" % s)]
                        for xi, (src, srck) in enumerate(srcs):
                            bk = 5 + xi // 2
                            bkk = "b%d" % bk
                            c0_ = (xi % 2) * 512
                            for h in range(4):
                                tr(bank_bf(bk)[:, c0_ + h * 128:c0_ + (h + 1) * 128], src[:, h * 128:(h + 1) * 128], identb,
                                   r=[srck, "identb"], w=[bkk])
                            cp("act" if bk == 5 else "dve", XT[xi], bank_bf(bk)[:, c0_:c0_ + 512], r=[bkk], w=["XT%d" % xi])
                        for d_, (qi, ki, msk, mk, bk) in enumerate(((0, 1, maskf, "maskf", 7), (2, 3, maskb, "maskb", 5))):
                            bkk = "b%d" % bk
                            for h in range(4):
                                mm(bank(bk)[:, h * 128:(h + 1) * 128], XT[ki][:, h * 128:(h + 1) * 128],
                                   XT[qi][:, h * 128:(h + 1) * 128], True, True, r=["XT%d" % ki, "XT%d" % qi], w=[bkk])
                            tt("dve", scm[d_], bank(bk), msk, ALU.mult, r=[bkk, mk], w=["scm%d" % d_])
                        for h in range(4):
                            hs = slice(h * 128, (h + 1) * 128)
                            ts("dve", Sbf[:, hs], SF_fin[:, hs], dstate[0][0][:, h:h + 1], None, ALU.mult, None,
                               r=["SF_fin", dstate[0][1]], w=["Sbf"])
                        for h in range(4):
                            hs = slice(h * 128, (h + 1) * 128)
                            mm(bank(6)[:, hs], scm[0][:, hs], hb_[:, hs], True, False, r=["scm0", hbk], w=["b6"])
                            mm(bank(6)[:, hs], scm[1][:, hs], hb_[:, hs], False, False, r=["scm1", hbk], w=["b6"])
                            mm(bank(6)[:, hs], XT[0][:, hs], Sbf[:, hs], False, True, r=["XT0", "Sbf"], w=["b6"])
                        cp("act", opt[sl], bank(6), r=["b6"], w=["opt%d" % sl])
                        dma(op_d[ob], opt[sl], "d_opt%d" % sl, r=["opt%d" % sl])
                        dma(qbT_d[ob], XT[2], "d_qbT", r=["XT2"])
                        inc_mm_to(6, KT2[s][:, 0, :], "KT%d" % s, hb_, hbk)
                        inc_mm_to(7, KT2[s][:, 1, :], "KT%d" % s, hb_, hbk)
                        rstep(SF_fin, "SF_fin", dstate[0][0], dstate[0][1], bank(6), "b6")
                        dstate[0] = (D2[s][:, 0, :], "D%d" % s)
                        cp("act", incb[sl], bank(7), r=["b7"], w=["incb%d" % sl])
                        dma(incb_d[ob], incb[sl], "d_incb%d" % sl, r=["incb%d" % sl])
                        cp("pool", Db_all[:, ob, :], D2[s][:, 1, :], r=["D%d" % s], w=["Db_all"])
                    return Q
                hist.append(split(pr.capture(mkQ())))
                ob += 1
        flush()
        pr.barrier()

        cc_ = Carver(PBASE)
        KTt = [cc_.take([NK], BF16) for _ in range(2)]
        Vt = [cc_.take([NKT, 130], BF16) for _ in range(2)]
        Qt = [cc_.take([2, NQ], BF16) for _ in range(2)]
        PT = [cc_.take([2, 512], BF16) for _ in range(3)]
        rr = cc_.take([8], F32)
        accs = cc_.take([8, 130], F32)
        oa4 = cc_.take([4, 128], F32)
        osq4 = cc_.take([4, 128], F32)
        ssa4 = cc_.take([4], F32)
        mhalf = cc_.take([4], F32)
        gal = cc_.take([4, 128], BF16)
        memset("pool", mhalf, -0.5, w=["mhalf"])
        qbl = [cc_.take([512], BF16) for _ in range(2)]
        o2 = [cc_.take([512], F32) for _ in range(2)]
        sq4 = cc_.take([512], F32)
        ghl = [cc_.take([512], BF16) for _ in range(2)]
        opt4 = [cc_.take([512], F32) for _ in range(2)]
        incb4 = [cc_.take([512], F32) for _ in range(2)]
        Sbf4 = cc_.take([512], BF16)
        Rt4 = cc_.take([512], F32)
        ss4 = cc_.take([4], F32)

        def b4_block(ob):
            sl = ob % 2
            dma(qbl[sl], qbT_d[ob], "d_qbl%d" % sl, w=["qbl%d" % sl])
            dma(opt4[sl], op_d[ob], "d_opl%d" % sl, w=["opt4%d" % sl])
            dma(incb4[sl], incb_d[ob], "d_incl%d" % sl, w=["incb4%d" % sl])
            dma(ghl[sl], gh_d[ob], "d_ghl%d" % sl, w=["ghl%d" % sl])
            cp("pool", Sbf4, SB_fin, r=["SB_fin"], w=["Sbf4"])
            for h_ in range(4):
                hs = slice(h_ * 128, (h_ + 1) * 128)
                mm(bank(7)[:, hs], qbl[sl][:, hs], Sbf4[:, hs], True, True, r=["qbl%d" % sl, "Sbf4"], w=["b7"])
            tt("dve", o2[sl], bank(7), opt4[sl], ALU.add, r=["b7", "opt4%d" % sl], w=["o2%d" % sl])
            tt("dve", Rt4, SB_fin, incb4[sl], ALU.add, r=["SB_fin", "incb4%d" % sl], w=["Rt4"])
            for h_ in range(4):
                hs = slice(h_ * 128, (h_ + 1) * 128)
                ts("dve", SB_fin[:, hs], Rt4[:, hs], Db_all[:, ob, h_:h_ + 1], None, ALU.mult, None,
                   r=["Rt4", "Db_all"], w=["SB_fin"])
            tt("pool", sq4, o2[sl], o2[sl], ALU.mult, r=["o2%d" % sl], w=["sq4"])
            pr.op("dve", lambda g: g.tensor_reduce(ss4, sq4.rearrange("p (a b) -> p a b", a=4), AX.X, ALU.add),
                  r=["sq4"], w=["ss4"])
            ts("dve", ss4, ss4, 1.0 / 128.0, EPS, ALU.mult, ALU.add, r=["ss4"], w=["ss4"])
            tt("pool", ss4, ss4, mhalf, ALU.pow, r=["ss4", "mhalf"], w=["ss4"])
            tt("dve", o2[sl].rearrange("p (a b) -> p a b", a=4), o2[sl].rearrange("p (a b) -> p a b", a=4),
               ss4.unsqueeze(2).to_broadcast([128, 4, 128]), ALU.mult, r=["o2%d" % sl, "ss4"], w=["o2%d" % sl])
            tt("pool", Y[:, ob, 512:1024], o2[sl], ghl[sl], ALU.mult, r=["o2%d" % sl, "ghl%d" % sl], w=["Yh"])

        b4_next = [NBO - 1]
        b4_every = max(1, (4 * (NQ // 512) * NKT) // (NBO + 2))
        NQC = NQ // 512
        it = 0
        def acc(m, qs):
            i = m * 4 + qs
            return psum[:, 4 + i // 3, (i % 3) * 130:(i % 3) * 130 + 130]

        def c_load(h_):
            hb_ = h_ % 2
            dma(KTt[hb_], kT_d[h_], "d_KT%d" % hb_, w=["KTt%d" % hb_])
            dma(Vt[hb_].rearrange("p a b -> p (a b)"), v_d[h_].rearrange("p a b -> p (a b)"), "d_V%d" % hb_, w=["Vt%d" % hb_])
            dma(Qt[hb_], q_d[h_].rearrange("m p n -> p m n"), "d_Q%d" % hb_, w=["Qt%d" % hb_])

        for h in range(4):
            hb = h % 2
            if PREF_C:
                if h == 0:
                    c_load(0)
                if h + 1 < 4:
                    c_load(h + 1)
            else:
                c_load(h)
            for qc in range(NQC):
                def qk_exp(kt, it_):
                    sb_ = it_ % 2
                    for m in range(2):
                        ps_ = slice(m * 64, (m + 1) * 64)
                        pr.op("pe", lambda g, o_=bank(sb_ * 2 + m), l_=KTt[hb][ps_, kt * 128:(kt + 1) * 128],
                              r_=Qt[hb][ps_, m, qc * 512:(qc + 1) * 512], tp=(m * 64, 0):
                              g.matmul(o_, l_, r_, start=True, stop=True, tile_position=tp),
                              r=["KTt%d" % hb, "Qt%d" % hb], w=["S%d" % sb_])
                    pt = PT[it_ % 3]
                    ptk = "PT%d" % (it_ % 3)
                    act(pt, psum[:, sb_ * 2:sb_ * 2 + 2, :], AF.Exp, r=["S%d" % sb_], w=[ptk], scale=0.125)

                def pv(kt, it_):
                    pt = PT[it_ % 3]
                    ptk = "PT%d" % (it_ % 3)
                    for m in range(2):
                        for qs in range(4):
                            stf = (kt == 0) and ((m * 4 + qs) in (0, 3, 6))
                            pr.op("pe", lambda g, a=acc(m, qs), l=pt[:, m, qs * 128:(qs + 1) * 128], v=Vt[hb][:, kt, :],
                                  stf=stf, sp_=(kt == NKT - 1): g.matmul(a, l, v, start=stf, stop=sp_, skip_group_check=True),
                                  r=[ptk, "Vt%d" % hb], w=["acc"])

                qk_exp(0, it)
                qk_exp(1, it + 1)
                for kt in range(NKT):
                    if kt + 2 < NKT:
                        qk_exp(kt + 2, it + 2)
                    pv(kt, it)
                    it += 1
                    if it % b4_every == 0 and b4_next[0] >= 0:
                        b4_block(b4_next[0])
                        b4_next[0] -= 1
                for bk_, n_ in ((4, 3), (5, 3), (6, 2)):
                    cp("dve", accs[:, (bk_ - 4) * 3:(bk_ - 4) * 3 + n_, :],
                       psum[:, bk_, 0:n_ * 130].rearrange("p (a b) -> p a b", a=n_), r=["acc"], w=["accs"])
                dma(gal, ga_d[qc * 4:(qc + 1) * 4, :, h * 128:(h + 1) * 128].rearrange("q p e -> p q e"), "d_gal", w=["gal"])
                pr.op("dve", lambda g: g.reciprocal(rr, accs[:, :, 128:129].rearrange("p a b -> p (a b)")), r=["accs"], w=["rr"])
                ts("dve", rr[:, 4:8], rr[:, 4:8], lam_t[:, 3:4], None, ALU.mult, None, r=["rr", "lam"], w=["rr"])
                tt("dve", accs[:, :, 0:128], accs[:, :, 0:128], rr.unsqueeze(2).to_broadcast([128, 8, 128]), ALU.mult,
                   r=["accs", "rr"], w=["accs"])
                tt("dve", oa4, accs[:, 0:4, 0:128], accs[:, 4:8, 0:128], ALU.add, r=["accs"], w=["oa4"])
                tt("pool", osq4, oa4, oa4, ALU.mult, r=["oa4"], w=["osq4"])
                pr.op("dve", lambda g: g.tensor_reduce(ssa4, osq4, AX.X, ALU.add), r=["osq4"], w=["ssa4"])
                ts("dve", ssa4, ssa4, 1.0 / 128.0, EPS, ALU.mult, ALU.add, r=["ssa4"], w=["ssa4"])
                tt("pool", ssa4, ssa4, mhalf, ALU.pow, r=["ssa4", "mhalf"], w=["ssa4"])
                tt("dve", oa4, oa4, ssa4.unsqueeze(2).to_broadcast([128, 4, 128]), ALU.mult, r=["oa4", "ssa4"], w=["oa4"])
                tt("dve", Y[:, qc * 4:(qc + 1) * 4, h * 128:(h + 1) * 128], oa4, gal, ALU.mult, r=["oa4", "gal"], w=["Ya"])
        while b4_next[0] >= 0:
            b4_block(b4_next[0])
            b4_next[0] -= 1
        pr.barrier()

        cd = Carver(PBASE)
        stg = [cd.take([8, 512], F32) for _ in range(2)]
        Wo = cd.take([8, D], BF16)
        YT = [cd.take([8, 128], BF16) for _ in range(2)]
        xo = [cd.take([D], F32) for _ in range(2)]
        zt = [cd.take([D], F32) for _ in range(2)]
        stats2 = [cd.take([2, 6], F32) for _ in range(2)]
        mv2 = [cd.take([2], F32) for _ in range(2)]
        rstd2 = [cd.take([1], F32) for _ in range(2)]
        nbias2 = [cd.take([1], F32) for _ in range(2)]
        mhalfd = cd.take([1], F32)
        memset("pool", mhalfd, -0.5, w=["mhalfd"])
        lng_t = cd.take([D], F32)
        lnb_t = cd.take([D], F32)
        dma(lng_t, lng.partition_broadcast(128), "c_lng", w=["lng"])
        dma(lnb_t, lnb.partition_broadcast(128), "c_lnb", w=["lnb"])
        for hb in range(2):
            load_w(Wo[:, :, hb * 512:(hb + 1) * 512], w_out[:, hb * 512:(hb + 1) * 512], 512, "Wo", stg[hb], "stg%d" % hb)
        own_row0 = 2 * CTX + 3 * NQ
        def d_load(t_):
            dma(xo[t_ % 2], xs[own_row0 + t_ * 128:own_row0 + (t_ + 1) * 128, :], "d_xo%d" % (t_ % 2), w=["xo%d" % (t_ % 2)])

        def d_pe(qt_):
            sl = qt_ % 2
            d_load(qt_)
            bk = 3 * sl
            bkk = "b%d" % bk
            for k in range(8):
                tr(bank_bf(bk)[:, k * 128:(k + 1) * 128], Y[:, qt_, k * 128:(k + 1) * 128], identb, r=["Y", "identb"], w=[bkk])
            cp("act", YT[sl].rearrange("p a b -> p (a b)"), bank_bf(bk), r=[bkk], w=["YT%d" % sl])
            for hb in range(2):
                bko = 3 * sl + 1 + hb
                bkok = "b%d" % bko
                for k in range(8):
                    mm(bank(bko), YT[sl][:, k, :], Wo[:, k, hb * 512:(hb + 1) * 512], k == 0, k == 7,
                       r=["YT%d" % sl, "Wo"], w=[bkok])

        def d_chain(qt_):
            sl = qt_ % 2
            for hb in range(2):
                bko = 3 * sl + 1 + hb
                hs = slice(hb * 512, (hb + 1) * 512)
                tt("dve", zt[sl][:, hs], bank(bko), gate_t[:, hs], ALU.mult, r=["b%d" % bko, "gate"], w=["zt%d" % sl])
            stt(zt[sl], xo[sl], ALPHA, zt[sl], ALU.mult, ALU.add, r=["xo%d" % sl, "zt%d" % sl], w=["zt%d" % sl])
            stats, mv, rstd = stats2[sl], mv2[sl], rstd2[sl]
            for hb in range(2):
                pr.op("dve", lambda g, hb=hb, sl=sl, stats=stats: g.bn_stats(stats[:, hb, :], zt[sl][:, hb * 512:(hb + 1) * 512]),
                      r=["zt%d" % sl], w=["stats%d" % sl])
            pr.op("dve", lambda g, stats=stats, mv=mv: g.bn_aggr(mv, stats.rearrange("p a b -> p (a b)")),
                  r=["stats%d" % sl], w=["mv%d" % sl])
            ts("dve", rstd, mv[:, 1:2], EPS, None, ALU.add, None, r=["mv%d" % sl], w=["rstd%d" % sl])
            tt("pool", rstd, rstd, mhalfd, ALU.pow, r=["rstd%d" % sl, "mhalfd"], w=["rstd%d" % sl])
            nb_ = nbias2[sl]
            stt(nb_, mv[:, 0:1], -1.0, rstd, ALU.mult, ALU.mult, r=["mv%d" % sl, "rstd%d" % sl], w=["nb%d" % sl])
            pr.op("act", lambda g, sl=sl, rstd=rstd, nb_=nb_: g.activation(out=zt[sl], in_=zt[sl], func=AF.Identity,
                                                                        bias=nb_[:, 0:1], scale=rstd[:, 0:1]),
                  r=["zt%d" % sl, "nb%d" % sl, "rstd%d" % sl], w=["zt%d" % sl])
            tt("dve", zt[sl], zt[sl], lng_t, ALU.mult, r=["zt%d" % sl, "lng"], w=["zt%d" % sl])
            tt("pool", xo[sl], zt[sl], lnb_t, ALU.add, r=["zt%d" % sl, "lnb"], w=["xo%d" % sl])
            dma(y[qt_ * 128:(qt_ + 1) * 128, :], xo[sl], "d_yo%d" % sl, r=["xo%d" % sl])

        for qt_ in range(NBO):
            d_pe(qt_)
            if qt_ > 0:
                d_chain(qt_ - 1)
        d_chain(NBO - 1)
        pr.barrier()

        dma_keys = sorted(pr.dma_cnt.keys())
        dma_sems = {k: stack.enter_context(nc.semaphore("dq_%d" % i)) for i, k in enumerate(dma_keys)}
        block = stack.enter_context(nc.Block())
        pr.emit(nc, block, None, sems, dma_sems)
    return nc


def _rope_tables(seq):
    t = np.arange(seq)
    pos = np.stack([(t // 64).astype(np.float32), (t % 64).astype(np.float32)], -1)
    inv = (np.float32(10000.0) ** (-np.arange(16, dtype=np.float32) / np.float32(16))).astype(np.float32)
    ang = (pos[:, :, None] * inv).astype(np.float32)
    ang = np.stack([ang, ang], axis=2).reshape(seq, 64)
    return np.cos(ang).astype(np.float32), np.sin(ang).astype(np.float32)


def _const_mats():
    s = np.arange(128)
    ident = np.eye(128, dtype=np.float32)
    rperm = np.zeros((128, 128), np.float32)
    for dst in range(128):
        half = (dst % 32) // 16
        if half == 0:
            rperm[dst + 16, dst] = -1.0
        else:
            rperm[dst - 16, dst] = 1.0
    Lf = (s[:, None] <= s[None, :]).astype(np.float32)
    Lb = (s[:, None] >= s[None, :]).astype(np.float32)
    return np.stack([ident, rperm, Lf, Lb, Lf, Lb], axis=1).astype(np.float32)


_NC_CACHE = {}


def make_in_maps(x, c, ctx, c_ctx, w_ada, b_ada, w_in, w_out, diff_lambda, diff_subln_gain,
                 hgrn_lower_bound, hgrn_norm_gain, ln_gain, ln_bias):
    B, SEQ, _ = x.shape
    NQ = SEQ // 4
    f = lambda a: np.ascontiguousarray(np.asarray(a, dtype=np.float32))
    x, c, ctx, c_ctx = f(x), f(c), f(ctx), f(c_ctx)
    w_in0 = f(w_in)[0]
    cos, sin = _rope_tables(SEQ)
    cmat = _const_mats()
    lbr = f(hgrn_lower_bound)
    wzf = w_in0[:, WCOL["hff"]:WCOL["hff"] + 512]
    wzb = w_in0[:, WCOL["hfb"]:WCOL["hfb"] + 512]
    maps = []
    for core in range(8):
        b, j = core // 4, core % 4
        others = [q for q in range(4) if q != j]
        left = [q for q in others if q < j]
        right = sorted([q for q in others if q > j], reverse=True)
        slots = left + right
        idx = []
        slot_is_f = []
        for q in slots:
            t = np.arange(q * NQ, (q + 1) * NQ)
            if q > j:
                t = t[::-1]
            idx.append(t)
            slot_is_f.append(q < j)
        idx.append(np.arange(j * NQ, (j + 1) * NQ))
        lat_idx = np.concatenate(idx)
        xs = np.concatenate([ctx[b], ctx[b][::-1], x[b][lat_idx]], axis=0)
        cs = cos[lat_idx]
        sn = sin[lat_idx]
        cs2 = np.concatenate([cs, cs], axis=1).T
        sn2 = np.concatenate([sn, sn], axis=1).T
        nst = 4 * NQ // 512
        ropet = np.stack([cs2.reshape(128, nst, 512), sn2.reshape(128, nst, 512)], axis=2)
        ropet = np.ascontiguousarray(ropet.transpose(1, 0, 2, 3))
        segdir = [True, False] + slot_is_f
        wz = np.stack([wzf if d else wzb for d in segdir], axis=0)
        lbseg = np.stack([lbr[0] if d else lbr[1] for d in segdir], axis=0)
        sel = np.zeros((16,), np.float32)
        for bnd in range(4):
            sel[bnd] = 1.0 if bnd == j else 0.0
            sel[4 + bnd] = 0.0 if bnd == j else 1.0
            sel[12 + bnd] = 1.0 if bnd == j else 0.0
        sel[8] = 1.0 if j > 0 else 0.0
        maps.append(dict(
            xs=np.ascontiguousarray(xs), cvec=np.stack([c[b], c_ctx], 0), w_ada=f(w_ada)[0], b_ada=f(b_ada)[0],
            w_in=w_in0, wz=np.ascontiguousarray(wz), lbseg=np.ascontiguousarray(lbseg), lbown=lbr,
            rope=ropet, sel=np.ascontiguousarray(np.tile(sel[None], (128, 1))), cmat=cmat, w_out=f(w_out)[0],
            dlam=f(diff_lambda)[0].reshape(256), subg=f(diff_subln_gain)[0], hng=f(hgrn_norm_gain)[0],
            lng=f(ln_gain)[0], lnb=f(ln_bias)[0]))
    return maps, SEQ, NQ


def kernel(x, c, ctx, c_ctx, w_ada, b_ada, w_in, w_out, diff_lambda, diff_subln_gain,
           hgrn_lower_bound, hgrn_norm_gain, ln_gain, ln_bias):
    maps, SEQ, NQ = make_in_maps(x, c, ctx, c_ctx, w_ada, b_ada, w_in, w_out, diff_lambda, diff_subln_gain,
                                 hgrn_lower_bound, hgrn_norm_gain, ln_gain, ln_bias)
    if SEQ not in _NC_CACHE:
        _NC_CACHE[SEQ] = build_program(SEQ)
    nc = _NC_CACHE[SEQ]
    res = run_bass_kernel_spmd(nc, maps, core_ids=list(range(8)))
    B = np.asarray(x).shape[0]
    out = np.zeros((B, SEQ, D), np.float32)
    for core in range(8):
        b, j = core // 4, core % 4
        out[b, j * NQ:(j + 1) * NQ] = np.asarray(res.results[core]["y"], dtype=np.float32)
    return out
```
